# Optimizing a Trainium2 kernel written in Bass

```python
import math
import jax, jax.numpy as jnp
from jax import lax
import numpy as np

D_MODEL = 1024
BATCH = 16
SEQ = 4096
DEPTH = 1
DEC_BATCH = 2
DEC_SEQ = 16384
PAST_LEN = 128

ATTN_WIDTH = D_MODEL // 2
CONV_WIDTH = D_MODEL - ATTN_WIDTH
N_HEADS = 4
HEAD_DIM = ATTN_WIDTH // N_HEADS // 2
V_DIM = 2 * HEAD_DIM
QK_WIDTH = N_HEADS * 2 * HEAD_DIM
IN_WIDTH = 2 * QK_WIDTH + ATTN_WIDTH + 2 * CONV_WIDTH
ROT_DIM = HEAD_DIM // 4
ROPE_THETA = 500000.0
CONV_KERNEL = 31
CONV_PAD = CONV_KERNEL // 2
N_GROUPS = 4
EXPERTS_PER_GROUP = 8
N_EXPERTS = N_GROUPS * EXPERTS_PER_GROUP
TOP_K = 2
EXPERT_FF = 512
Q_BLOCK = 128
MOE_BLOCK = 128
EPS = 1e-6

kernel_name = 'hymba_diffattn_conformer_hmoe_encoder'


def rms_norm(x, g):
    xf = x.astype(jnp.float32)
    y = xf * lax.rsqrt(jnp.mean(xf * xf, axis=-1, keepdims=True) + EPS)
    return (y * g.astype(jnp.float32)).astype(x.dtype)


def layer_norm(x, g, b):
    xf = x.astype(jnp.float32)
    mu = jnp.mean(xf, axis=-1, keepdims=True)
    var = jnp.mean(jnp.square(xf - mu), axis=-1, keepdims=True)
    y = (xf - mu) * lax.rsqrt(var + EPS)
    return (y * g.astype(jnp.float32) + b.astype(jnp.float32)).astype(x.dtype)


def modulate(h, shift, scale):
    return h * (1.0 + scale[:, None, :]) + shift[:, None, :]


def apply_rotary(x, pos):
    inv_freq = ROPE_THETA ** (-jnp.arange(0, ROT_DIM, 2, dtype=jnp.float32) / ROT_DIM)
    ang = pos[:, None] * inv_freq[None, :]
    cos = jnp.cos(ang)[:, None, None, :]
    sin = jnp.sin(ang)[:, None, None, :]
    xr = x[..., :ROT_DIM].astype(jnp.float32)
    x1 = xr[..., :ROT_DIM // 2]
    x2 = xr[..., ROT_DIM // 2:]
    rot = jnp.concatenate([x1 * cos - x2 * sin, x2 * cos + x1 * sin], axis=-1).astype(x.dtype)
    return jnp.concatenate([rot, x[..., ROT_DIM:]], axis=-1)


def diff_attention(q, k, v, lam):
    b, s = q.shape[0], q.shape[1]
    nq = s // Q_BLOCK
    qb = q.reshape(b, nq, Q_BLOCK, N_HEADS, 2, HEAD_DIM).transpose(1, 0, 2, 3, 4, 5)
    scale = HEAD_DIM ** -0.5

    def one_block(qi):
        sc = jnp.einsum('bqhcd,bkhcd->bhcqk', qi, k).astype(jnp.float32) * scale
        p = jax.nn.softmax(sc, axis=-1)
        a = p[:, :, 0] - lam * p[:, :, 1]
        return jnp.einsum('bhqk,bkhe->bqhe', a.astype(v.dtype), v)

    o = lax.map(one_block, qb)
    return o.transpose(1, 0, 2, 3, 4).reshape(b, s, N_HEADS, V_DIM)


def conformer_conv(u, w_dw, b_dw, g_ln, b_ln):
    a, gate = jnp.split(u, 2, axis=-1)
    z = a * jax.nn.sigmoid(gate)
    z = lax.conv_general_dilated(z, w_dw[:, None, :].astype(z.dtype), window_strides=(1,),
                                 padding=[(CONV_PAD, CONV_PAD)],
                                 dimension_numbers=('NWC', 'WIO', 'NWC'),
                                 feature_group_count=CONV_WIDTH) + b_dw
    return jax.nn.silu(layer_norm(z, g_ln, b_ln))


def hier_moe(h, w_rg, b_rg, w_re, b_re, w_gate_up, w_down):
    t, d = h.shape
    lg = (h @ w_rg).astype(jnp.float32) + b_rg
    pg = jax.nn.softmax(lg, axis=-1)
    g_sel = jnp.argmax(lg, axis=-1)
    p_group = jnp.take_along_axis(pg, g_sel[:, None], axis=1)[:, 0]
    le = ((h @ w_re).astype(jnp.float32) + b_re).reshape(t, N_GROUPS, EXPERTS_PER_GROUP)
    le_g = jnp.take_along_axis(le, g_sel[:, None, None], axis=1)[:, 0]
    pe = jax.nn.softmax(le_g, axis=-1)
    top_p, top_i = lax.top_k(pe, TOP_K)
    top_p = top_p / jnp.sum(top_p, axis=-1, keepdims=True)
    gates = p_group[:, None] * top_p
    eids = g_sel[:, None] * EXPERTS_PER_GROUP + top_i

    n_assign = t * TOP_K
    e_flat = eids.reshape(n_assign).astype(jnp.int32)
    t_flat = jnp.repeat(jnp.arange(t, dtype=jnp.int32), TOP_K)
    w_flat = gates.reshape(n_assign)
    order = jnp.argsort(e_flat)
    e_s, t_s, w_s = e_flat[order], t_flat[order], w_flat[order]
    counts = jax.ops.segment_sum(jnp.ones((n_assign,), jnp.int32), e_flat, num_segments=N_EXPERTS)
    starts = jnp.cumsum(counts) - counts
    padded = ((counts + MOE_BLOCK - 1) // MOE_BLOCK) * MOE_BLOCK
    pad_ends = jnp.cumsum(padded)
    pad_starts = pad_ends - padded
    rank = jnp.arange(n_assign, dtype=jnp.int32) - starts[e_s]
    dest = pad_starts[e_s] + rank
    n_rows = n_assign + N_EXPERTS * MOE_BLOCK
    n_blk = n_rows // MOE_BLOCK
    row_tok = jnp.full((n_rows,), t, jnp.int32).at[dest].set(t_s)
    row_w = jnp.zeros((n_rows,), jnp.float32).at[dest].set(w_s)
    blk_start = jnp.arange(n_blk, dtype=jnp.int32) * MOE_BLOCK
    blk_eid = jnp.minimum(jnp.searchsorted(pad_ends, blk_start, side='right'), N_EXPERTS - 1).astype(jnp.int32)
    h_pad = jnp.concatenate([h, jnp.zeros((1, d), h.dtype)], axis=0)
    xg = h_pad[row_tok].reshape(n_blk, MOE_BLOCK, d)

    def expert_block(args):
        xb, e = args
        gu = xb @ w_gate_up[e]
        g_, u_ = jnp.split(gu, 2, axis=-1)
        return (jax.nn.silu(g_) * u_) @ w_down[e]

    yb = lax.map(expert_block, (xg, blk_eid)).reshape(n_rows, d)
    y = jax.ops.segment_sum(yb * row_w[:, None].astype(yb.dtype), row_tok, num_segments=t + 1)
    return y[:t]


def encoder_layer(x, c, p, lam_init):
    b, s, d = x.shape
    mod = jax.nn.silu(c) @ p['w_ada'] + p['b_ada']
    sh1, sc1, g1, sh2, sc2, g2 = jnp.split(mod, 6, axis=-1)

    h = modulate(rms_norm(x, p['g_norm1']), sh1, sc1)
    proj = h @ p['w_in']
    q, k, v, u = jnp.split(proj, [QK_WIDTH, 2 * QK_WIDTH, 2 * QK_WIDTH + ATTN_WIDTH], axis=-1)
    pos = jnp.arange(s, dtype=jnp.float32)
    q = apply_rotary(rms_norm(q.reshape(b, s, N_HEADS, 2, HEAD_DIM), p['g_q']), pos)
    k = apply_rotary(rms_norm(k.reshape(b, s, N_HEADS, 2, HEAD_DIM), p['g_k']), pos)
    v = v.reshape(b, s, N_HEADS, V_DIM)
    lam = (jnp.exp(jnp.sum(p['lambda_q1'].astype(jnp.float32) * p['lambda_k1'].astype(jnp.float32)))
           - jnp.exp(jnp.sum(p['lambda_q2'].astype(jnp.float32) * p['lambda_k2'].astype(jnp.float32)))
           + lam_init)
    o = diff_attention(q, k, v, lam)
    o = (rms_norm(o, p['g_subln']) * (1.0 - lam_init)).reshape(b, s, ATTN_WIDTH)
    cv = conformer_conv(u, p['w_dw'], p['b_dw'], p['g_conv_ln'], p['b_conv_ln'])
    mix = jnp.concatenate([o, cv], axis=-1) @ p['w_out']
    x = x + g1[:, None, :] * mix

    h2 = modulate(rms_norm(x, p['g_norm2']), sh2, sc2)
    y = hier_moe(h2.reshape(b * s, d), p['w_router_group'], p['b_router_group'],
                 p['w_router_expert'], p['b_router_expert'], p['w_gate_up'], p['w_down'])
    return x + g2[:, None, :] * y.reshape(b, s, d)


def setup_inputs(seed: int = 0) -> dict:
    key = jax.random.key(seed)
    ks = jax.random.split(key, 32)
    L, D = DEPTH, D_MODEL

    def nrm(k, shape, scale):
        return jax.random.normal(k, shape, jnp.float32) * scale

    return {
        'x_prompt': nrm(ks[0], (BATCH, SEQ, D), 1.0),
        'x_sample': nrm(ks[1], (DEC_BATCH, DEC_SEQ, D), 1.0),
        'c_prompt': nrm(ks[2], (BATCH, D), 1.0),
        'c_sample': nrm(ks[3], (DEC_BATCH, D), 1.0),
        'w_ada': nrm(ks[4], (L, D, 6 * D), 0.1 * D ** -0.5),
        'b_ada': nrm(ks[5], (L, 6 * D), 0.01),
        'g_norm1': 1.0 + nrm(ks[6], (L, D), 0.02),
        'w_in': nrm(ks[7], (L, D, IN_WIDTH), D ** -0.5),
        'g_q': 1.0 + nrm(ks[8], (L, HEAD_DIM), 0.02),
        'g_k': 1.0 + nrm(ks[9], (L, HEAD_DIM), 0.02),
        'lambda_q1': nrm(ks[10], (L, HEAD_DIM), 0.1),
        'lambda_k1': nrm(ks[11], (L, HEAD_DIM), 0.1),
        'lambda_q2': nrm(ks[12], (L, HEAD_DIM), 0.1),
        'lambda_k2': nrm(ks[13], (L, HEAD_DIM), 0.1),
        'g_subln': 1.0 + nrm(ks[14], (L, V_DIM), 0.02),
        'w_dw': nrm(ks[15], (L, CONV_KERNEL, CONV_WIDTH), CONV_KERNEL ** -0.5),
        'b_dw': nrm(ks[16], (L, CONV_WIDTH), 0.01),
        'g_conv_ln': 1.0 + nrm(ks[17], (L, CONV_WIDTH), 0.02),
        'b_conv_ln': nrm(ks[18], (L, CONV_WIDTH), 0.01),
        'w_out': nrm(ks[19], (L, D, D), D ** -0.5),
        'g_norm2': 1.0 + nrm(ks[20], (L, D), 0.02),
        'w_router_group': nrm(ks[21], (L, D, N_GROUPS), D ** -0.5),
        'b_router_group': nrm(ks[22], (L, N_GROUPS), 0.01),
        'w_router_expert': nrm(ks[23], (L, D, N_EXPERTS), D ** -0.5),
        'b_router_expert': nrm(ks[24], (L, N_EXPERTS), 0.01),
        'w_gate_up': nrm(ks[25], (L, N_EXPERTS, D, 2 * EXPERT_FF), D ** -0.5),
        'w_down': nrm(ks[26], (L, N_EXPERTS, EXPERT_FF, D), EXPERT_FF ** -0.5),
    }


def reference(x_prompt, x_sample, c_prompt, c_sample, w_ada, b_ada, g_norm1, w_in, g_q, g_k,
              lambda_q1, lambda_k1, lambda_q2, lambda_k2, g_subln, w_dw, b_dw, g_conv_ln, b_conv_ln,
              w_out, g_norm2, w_router_group, b_router_group, w_router_expert, b_router_expert,
              w_gate_up, w_down):
    xp, xs = x_prompt, x_sample
    for l in range(DEPTH):
        lam_init = 0.8 - 0.6 * math.exp(-0.3 * l)
        p = dict(w_ada=w_ada[l], b_ada=b_ada[l], g_norm1=g_norm1[l], w_in=w_in[l], g_q=g_q[l], g_k=g_k[l],
                 lambda_q1=lambda_q1[l], lambda_k1=lambda_k1[l], lambda_q2=lambda_q2[l], lambda_k2=lambda_k2[l],
                 g_subln=g_subln[l], w_dw=w_dw[l], b_dw=b_dw[l], g_conv_ln=g_conv_ln[l], b_conv_ln=b_conv_ln[l],
                 w_out=w_out[l], g_norm2=g_norm2[l], w_router_group=w_router_group[l],
                 b_router_group=b_router_group[l], w_router_expert=w_router_expert[l],
                 b_router_expert=b_router_expert[l], w_gate_up=w_gate_up[l], w_down=w_down[l])
        xp = encoder_layer(xp, c_prompt, p, lam_init)
        xs = encoder_layer(xs, c_sample, p, lam_init)
    return (xp, xs)
```

```python
import numpy as np
from contextlib import ExitStack
import concourse.bass as bass
import concourse.mybir as mybir
from concourse.bass_utils import run_bass_kernel_spmd

F32 = mybir.dt.float32
BF16 = mybir.dt.bfloat16
I32 = mybir.dt.int32
AF = mybir.ActivationFunctionType
ALU = mybir.AluOpType
AX = mybir.AxisListType

EPOCH = 24000


class Buf:
    def __init__(self, t, name):
        self.t = t
        self.name = name
        self.last_writer = None
        self.readers = {}

    def __getitem__(self, k):
        return self.t[k]


class DSem:
    def __init__(self, sem):
        self.sem = sem
        self.count = 0
        self.last_op = None
        self.bg = False


class DPool:
    def __init__(self, sems, sw_sems):
        self.sems = sems
        self.sw_sems = sw_sems
        self.i = 0

    def next(self, eng):
        self.i += 1
        lst = self.sw_sems if eng == "pool" else self.sems
        return lst[self.i % len(lst)]


class Op:
    __slots__ = ("eng", "fn", "reads", "writes", "dsem", "dval", "deps", "signal", "sidx", "idx", "extra")

    def __init__(self, eng, fn, reads, writes, dsem):
        self.eng = eng
        self.fn = fn
        self.reads = reads
        self.writes = writes
        self.dsem = dsem
        self.dval = 0
        self.deps = []
        self.signal = False
        self.sidx = 0
        self.extra = None


ENGS = ("pe", "act", "dve", "pool", "sp")


class Prog:
    def __init__(self, nc, stack):
        self.nc = nc
        self.stack = stack
        self.ops = []
        self.bufs = []
        self.dsems = []
        self.last = {e: None for e in ENGS}
        self.nsem = 0

    def new_sem(self, name):
        self.nsem += 1
        return self.stack.enter_context(self.nc.semaphore(f"{name}_{self.nsem}"))

    def dsem(self, name="d"):
        d = DSem(self.new_sem(name))
        self.dsems.append(d)
        return d

    def dpool(self, name="dp", n=6):
        return DPool([self.dsem(name) for _ in range(n)], [self.dsem(name + "sw") for _ in range(n)])

    def buf(self, t, name):
        b = Buf(t, name)
        self.bufs.append(b)
        return b

    def sbuf(self, stack, name, shape, dt):
        self.nsem += 1
        name = f"{name}_{self.nsem}"
        t = stack.enter_context(self.nc.sbuf_tensor(name, list(shape), dt))
        return self.buf(t, name)

    def psum(self, stack, name, shape, dt):
        self.nsem += 1
        name = f"{name}_{self.nsem}"
        t = stack.enter_context(self.nc.psum_tensor(name, list(shape), dt))
        return self.buf(t, name)

    def dram(self, name, shape, dt, kind="Internal"):
        t = self.nc.dram_tensor(name, list(shape), dt, kind=kind)
        return self.buf(t.ap(), name)

    def op(self, eng, fn, reads=(), writes=(), dsem=None):
        if isinstance(dsem, DPool):
            dsem = dsem.next(eng)
        o = Op(eng, fn, list(reads), list(writes), dsem)
        o.idx = len(self.ops)
        deps = {}
        if dsem is not None:
            if dsem.last_op is not None:
                deps[dsem.last_op.idx] = (dsem.last_op, "sem")
            dsem.last_op = o
            dsem.count += 16
            o.dval = dsem.count
        for b in o.reads:
            w = b.last_writer
            if w is not None:
                deps[w.idx] = (w, "raw")
        for b in o.writes:
            w = b.last_writer
            if w is not None and w.idx not in deps:
                deps[w.idx] = (w, "waw")
            for r in b.readers.values():
                if r.idx not in deps:
                    deps[r.idx] = (r, "war")
        for (d, kind) in deps.values():
            if d is o:
                continue
            if d.dsem is None and d.eng == eng and dsem is None:
                if eng == "pe":
                    continue
            o.deps.append((d, d.dval if d.dsem is not None else 0))
            if d.dsem is None:
                d.signal = True
        for b in o.writes:
            b.last_writer = o
            b.readers = {}
        for b in o.reads:
            key = ("d", id(dsem)) if dsem is not None else eng
            b.readers[key] = o
        self.ops.append(o)
        if dsem is None:
            self.last[eng] = o
        return o

    def barrier(self, include_bg=False):
        lasts = [o for o in self.last.values() if o is not None]
        dlast = []
        for d in self.dsems:
            if d.count > 0 and (include_bg or not d.bg):
                dlast.append(d)
        for e in ENGS:
            o = Op(e, None, [], [], None)
            o.idx = len(self.ops)
            for l in lasts:
                if l.eng != e:
                    o.deps.append((l, 0))
                    l.signal = True
            o.extra = [(d, d.count) for d in dlast]
            self.ops.append(o)
        for b in self.bufs:
            b.last_writer = None
            b.readers = {}

    def emit(self):
        nc = self.nc
        cnt = {e: 0 for e in ENGS}
        for o in self.ops:
            if o.signal:
                cnt[o.eng] += 1
                o.sidx = cnt[o.eng]
        esems = {}
        for e in ENGS:
            n = cnt[e] // EPOCH + 1
            esems[e] = [self.new_sem(f"e_{e}") for _ in range(n)]

        def semval(dv):
            d, v = dv
            if d.dsem is not None:
                return d.dsem.sem, v
            ep = (d.sidx - 1) // EPOCH
            return esems[d.eng][ep], d.sidx - ep * EPOCH

        per = {e: [o for o in self.ops if o.eng == e] for e in ENGS}
        stats = {"waits": 0, "ops": len(self.ops)}

        def run(engname, eng):
            known = {}
            for o in per[engname]:
                ws = {}
                for d in o.deps:
                    s, v = semval(d)
                    k = id(s)
                    if known.get(k, 0) >= v:
                        continue
                    if k not in ws or ws[k][1] < v:
                        ws[k] = (s, v)
                if o.extra:
                    for (d, v) in o.extra:
                        k = id(d.sem)
                        if known.get(k, 0) >= v:
                            continue
                        if k not in ws or ws[k][1] < v:
                            ws[k] = (d.sem, v)
                for k, (s, v) in ws.items():
                    eng.wait_ge(s, v)
                    known[k] = v
                    stats["waits"] += 1
                if o.fn is None:
                    continue
                ins = o.fn(eng)
                if o.dsem is not None:
                    ins.then_inc(o.dsem.sem, 16)
                elif o.signal:
                    ep = (o.sidx - 1) // EPOCH
                    ins.then_inc(esems[engname][ep], 1)

        with nc.Block() as block:
            @block.sync
            def _(e):
                run("sp", e)

            @block.tensor
            def _(e):
                run("pe", e)

            @block.scalar
            def _(e):
                run("act", e)

            @block.vector
            def _(e):
                run("dve", e)

            @block.gpsimd
            def _(e):
                run("pool", e)
        return stats


D = 1024
NH = 4
HD = 64
VD = 128
INW = 2560
NE = 32
FF = 512
CK = 31
CP = 15
EPS = 1e-6
ROPE_THETA = 500000.0
INV_FREQ = [float(ROPE_THETA ** (-(2 * i) / 16.0)) for i in range(8)]
LAM_INIT = 0.2
TWO_PI = 6.283185307179586
BIG = 4096.0


class K:
    def __init__(self, units, dbg=False):
        self.units = units
        self.dbg = dbg
        self.nc = nc = bass.Bass("TRN2", target_bir_lowering=False)
        self.st = ExitStack()
        self.P = Prog(nc, self.st)
        self.n_tok = sum(u[1] for u in units)
        self.n_tiles = self.n_tok // 128
        self.n_rows = 2 * self.n_tok + NE * 128
        self.n_blk = self.n_rows // 128
        self._uid = 0

    def uid(self, s):
        self._uid += 1
        return f"{s}{self._uid}"

    def cds(self, name, i=0):
        if not hasattr(self, "_cds"):
            self._cds = {}
        k = (name, i)
        if k not in self._cds:
            self._cds[k] = self.P.dsem(name)
        return self._cds[k]

    def ein(self, name, shape, dt):
        return self.P.buf(self.nc.dram_tensor(name, list(shape), dt, kind="ExternalInput").ap(), name)

    def eout(self, name, shape, dt):
        return self.P.buf(self.nc.dram_tensor(name, list(shape), dt, kind="ExternalOutput").ap(), name)

    def dma(self, q, out_b, out_ap, in_b, in_ap, ds, **kw):
        return self.P.op(q, lambda e: e.dma_start(out=out_ap, in_=in_ap, **kw), reads=[in_b], writes=[out_b], dsem=ds)

    def mm(self, ob, oap, lb, lap, rb, rap, start, stop):
        return self.P.op("pe", lambda e: e.matmul(oap, lap, rap, start=start, stop=stop), reads=[lb, rb], writes=[ob])

    def tp(self, ob, oap, ib, iap):
        ident = self.ident
        return self.P.op("pe", lambda e: e.transpose(oap, iap, ident[:, :]), reads=[ib, ident], writes=[ob])

    def act(self, ob, oap, ib, iap, func, reads=(), **kw):
        return self.P.op("act", lambda e: e.activation(out=oap, in_=iap, func=func, **kw), reads=[ib] + list(reads), writes=[ob])

    def tt(self, ob, oap, ab, aap, bb, bap, op, eng="dve"):
        return self.P.op(eng, lambda e: e.tensor_tensor(out=oap, in0=aap, in1=bap, op=op), reads=[ab, bb], writes=[ob])

    def ts(self, ob, oap, ib, iap, s1, op0, s2=None, op1=None, reads=(), eng="dve"):
        if op1 is None:
            return self.P.op(eng, lambda e: e.tensor_scalar(out=oap, in0=iap, scalar1=s1, scalar2=None, op0=op0), reads=[ib] + list(reads), writes=[ob])
        return self.P.op(eng, lambda e: e.tensor_scalar(out=oap, in0=iap, scalar1=s1, scalar2=s2, op0=op0, op1=op1), reads=[ib] + list(reads), writes=[ob])

    def stt(self, ob, oap, ab, aap, sc, bb, bap, op0, op1, reads=()):
        return self.P.op("dve", lambda e: e.scalar_tensor_tensor(out=oap, in0=aap, scalar=sc, in1=bap, op0=op0, op1=op1), reads=[ab, bb] + list(reads), writes=[ob])

    def cp(self, ob, oap, ib, iap, eng="dve"):
        if eng == "act":
            return self.P.op("act", lambda e: e.copy(out=oap, in_=iap), reads=[ib], writes=[ob])
        return self.P.op(eng, lambda e: e.tensor_copy(out=oap, in_=iap), reads=[ib], writes=[ob])

    def red(self, ob, oap, ib, iap, op=None):
        op = op or ALU.add
        return self.P.op("dve", lambda e: e.tensor_reduce(out=oap, in_=iap, axis=AX.X, op=op), reads=[ib], writes=[ob])

    def recip(self, ob, oap, ib, iap):
        return self.P.op("dve", lambda e: e.reciprocal(out=oap, in_=iap), reads=[ib], writes=[ob])

    def memset(self, ob, oap, val, eng="dve"):
        return self.P.op(eng, lambda e: e.memset(oap, val), writes=[ob])

    def rsqrt(self, ob, oap, ib, iap, tmpb, tmpap, scale=1.0, bias=0.0):
        if bias != 0.0:
            bt = self.epsb
            self.act(tmpb, tmpap, ib, iap, AF.Ln, reads=[bt], scale=scale, bias=bt[0:oap.shape[0], 0:1])
        else:
            self.act(tmpb, tmpap, ib, iap, AF.Ln, scale=scale)
        self.act(ob, oap, tmpb, tmpap, AF.Exp, scale=-0.5)

    def bload(self, st, name, src_b, src_ap, n, ds, dt=F32):
        t = self.P.sbuf(st, name, [128, n], dt)
        self.dma("sp", t, t[:, :], src_b, src_ap.partition_broadcast(128).rearrange("p o f -> p (o f)"), ds)
        return t

    def declare(self):
        nu = len(self.units)
        self.xu = [self.ein(f"x{u}", [S, D], F32) for u, (S, n, smp) in enumerate(self.units)]
        self.posu = [self.ein(f"pos{u}", [128, S // 128], F32) for u, (S, n, smp) in enumerate(self.units)]
        self.flg = self.ein("flags", [1, 2], F32)
        self.cT = self.ein("cT", [128, 8, nu], F32)
        w = {}
        for name, shape in [("w_ada", [D, 6 * D]), ("b_ada", [1, 6 * D]), ("g_norm1", [1, D]), ("w_in", [D, INW]),
                            ("g_q", [1, HD]), ("g_k", [1, HD]), ("lambda_q1", [1, HD]), ("lambda_k1", [1, HD]),
                            ("lambda_q2", [1, HD]), ("lambda_k2", [1, HD]), ("g_subln", [VD, 1]), ("w_dw", [CK, 512]),
                            ("b_dw", [512, 1]), ("g_conv_ln", [512, 1]), ("b_conv_ln", [512, 1]), ("w_out", [D, D]),
                            ("g_norm2", [1, D]), ("w_rg", [D, 4]), ("b_rg", [1, 4]), ("w_re", [D, NE]), ("b_re", [1, NE]),
                            ("w_gu", [NE, D, 2 * FF]), ("w_dn", [NE, FF, D])]:
            w[name] = self.ein(name, shape, F32)
        self.w = w
        self.yu = [self.eout(f"y{u}", [n, D], F32) for u, (S, n, smp) in enumerate(self.units)]
        P = self.P
        if self.dbg:
            _d = P.dram
            P.dram = lambda name, shape, dt, kind="Internal": _d(name, shape, dt, kind="ExternalOutput")
        self.MOD = P.dram("MOD", [nu, 6 * D], F32)
        self.KT = [P.dram(f"KT{u}", [NH, 128, S], BF16) for u, (S, n, smp) in enumerate(self.units)]
        self.VV = [P.dram(f"VV{u}", [128, S // 128, 512], BF16) for u, (S, n, smp) in enumerate(self.units)]
        self.QT = [P.dram(f"QT{u}", [NH, 128, n], BF16) for u, (S, n, smp) in enumerate(self.units)]
        self.ZT = [P.dram(f"ZT{u}", [4, 128, n + 2 * CP + 2], BF16) for u, (S, n, smp) in enumerate(self.units)]
        self.X1 = P.dram("X1", [self.n_tok, D], F32)
        self.H2 = P.dram("H2", [self.n_tok, D], BF16)
        self.XS = P.dram("XS", [self.n_rows, D], BF16)
        self.YB = P.dram("YB", [self.n_rows, D], F32)
        self.WGU = P.dram("WGU", [NE * 128, 8 * 2 * FF], BF16)
        self.WDN = P.dram("WDN", [NE * 128, 4 * D], BF16)
        if self.dbg:
            self.DROUT = P.dram("DROUT", [128, self.n_tiles * 6], F32)
            self.DMISC = P.dram("DMISC", [128, NE + self.n_blk + 1 + self.n_tiles * 2], F32)

    def setup(self):
        P, nc, st = self.P, self.nc, self.st
        nu = len(self.units)
        w = self.w
        ds_w = self.gp = P.dpool("gp", 8)
        cst = self.cst = ExitStack()
        self.st.enter_context(cst)
        ident = self.ident = P.sbuf(cst, "ident", [128, 128], BF16)
        ones = self.ones = P.sbuf(cst, "ones", [128, 128], BF16)
        self.utri = P.sbuf(cst, "utri", [128, 128], BF16)
        self.iota32 = P.sbuf(cst, "iota32", [128, NE], F32)
        self.ROUT = P.sbuf(cst, "ROUT", [128, self.n_tiles, 6], F32)
        self.Rcnt = P.sbuf(cst, "Rcnt", [128, NE], F32)
        self.neglam = P.sbuf(cst, "neglam", [128, 1], F32)
        self.gsub = P.sbuf(cst, "gsub", [128, 1], F32)
        self.flags = P.sbuf(cst, "flagsb", [128, 2], F32)
        self.epsb = P.sbuf(cst, "epsb", [128, 1], F32)
        self.zero = P.sbuf(cst, "zero", [128, 4 * D], BF16)
        self.wst = ExitStack()
        self.win = P.sbuf(self.wst, "win", [128, 8, INW], BF16)
        dsc = self.gp
        with ExitStack() as t:
            ci = P.sbuf(t, "ci", [128, 128], I32)
            ri = P.sbuf(t, "ri", [128, 128], I32)
            cf = P.sbuf(t, "cf", [128, 128], F32)
            rf = P.sbuf(t, "rf", [128, 128], F32)
            P.op("pool", lambda e: e.iota(ci[:, :], pattern=[[1, 128]], base=0, channel_multiplier=0), writes=[ci])
            P.op("pool", lambda e: e.iota(ri[:, :], pattern=[[0, 128]], base=0, channel_multiplier=1), writes=[ri])
            self.cp(cf, cf[:, :], ci, ci[:, :])
            self.cp(rf, rf[:, :], ri, ri[:, :])
            self.cp(self.iota32, self.iota32[:, :], cf, cf[:, 0:NE])
            self.tt(ident, ident[:, :], rf, rf[:, :], cf, cf[:, :], ALU.is_equal)
            self.tt(self.utri, self.utri[:, :], rf, rf[:, :], cf, cf[:, :], ALU.is_lt)
            self.memset(ones, ones[:, :], 1.0)
            self.memset(self.epsb, self.epsb[:, :], EPS)
            self.memset(self.Rcnt, self.Rcnt[:, :], 0.0)
            self.dma("sp", self.flags, self.flags[:, :], self.flg, self.flg[0:1, :].partition_broadcast(128).rearrange("p o f -> p (o f)"), dsc)
            lt = [self.bload(t, f"lam{i}", w[n], w[n][0:1, :], HD, dsc) for i, n in enumerate(["lambda_q1", "lambda_k1", "lambda_q2", "lambda_k2"])]
            pr = P.sbuf(t, "lampr", [128, HD], F32)
            s1 = P.sbuf(t, "lams1", [128, 2], F32)
            self.tt(pr, pr[:, :], lt[0], lt[0][:, :], lt[1], lt[1][:, :], ALU.mult)
            self.red(s1, s1[:, 0:1], pr, pr[:, :])
            self.tt(pr, pr[:, :], lt[2], lt[2][:, :], lt[3], lt[3][:, :], ALU.mult)
            self.red(s1, s1[:, 1:2], pr, pr[:, :])
            e1 = P.sbuf(t, "lame1", [128, 2], F32)
            self.act(e1, e1[:, :], s1, s1[:, :], AF.Exp)
            self.tt(self.neglam, self.neglam[:, :], e1, e1[:, 1:2], e1, e1[:, 0:1], ALU.subtract)
            self.ts(self.neglam, self.neglam[:, :], self.neglam, self.neglam[:, :], -LAM_INIT, ALU.add)
            gs = P.sbuf(t, "gs", [128, 1], F32)
            self.dma("sp", gs, gs[:, :], w["g_subln"], w["g_subln"][:, :], dsc)
            self.ts(self.gsub, self.gsub[:, :], gs, gs[:, :], 1.0 - LAM_INIT, ALU.mult)
            dsw = ds_w
            wv = w["w_in"][:, :].rearrange("(kc p) n -> p kc n", p=128)
            for kc in range(8):
                for cb in range(INW // 512):
                    self.dma("pool", self.win, self.win[:, kc, cb * 512:(cb + 1) * 512], w["w_in"], wv[:, kc, cb * 512:(cb + 1) * 512], dsw)
            bgp = DPool([], [P.dsem("bgw") for _ in range(6)])
            for d_ in bgp.sw_sems:
                d_.bg = True
            wgu_v = self.WGU[:, :].rearrange("(e p) (kc n) -> e p kc n", p=128, kc=8)
            wdn_v = self.WDN[:, :].rearrange("(e p) (kc n) -> e p kc n", p=128, kc=4)
            for e in range(NE):
                for kc in range(8):
                    self.dma("pool", self.WGU, wgu_v[e, :, kc, :], w["w_gu"], w["w_gu"][e, kc * 128:(kc + 1) * 128, :], bgp)
                for kc in range(4):
                    self.dma("pool", self.WDN, wdn_v[e, :, kc, :], w["w_dn"], w["w_dn"][e, kc * 128:(kc + 1) * 128, :], bgp)
            zero = self.zero
            self.memset(zero, zero[:, :], 0.0)
            xsv = self.XS[:, :].rearrange("(a p r) d -> a p (r d)", p=128, r=4)
            for a in range(self.n_rows // 512):
                self.dma("pool", self.XS, xsv[a], zero, zero[:, :], bgp)
            ct = P.sbuf(t, "ct", [128, 8, nu], F32)
            ce = P.sbuf(t, "ce", [128, 8, nu], F32)
            sc = P.sbuf(t, "sc", [128, 8, nu], F32)
            self.dma("sp", ct, ct[:, :, :], self.cT, self.cT[:, :, :], dsc)
            self.act(ce, ce[:, :, :], ct, ct[:, :, :], AF.Exp, scale=-1.0)
            self.ts(ce, ce[:, :, :], ce, ce[:, :, :], 1.0, ALU.add)
            self.recip(ce, ce[:, :, :], ce, ce[:, :, :])
            self.tt(sc, sc[:, :, :], ct, ct[:, :, :], ce, ce[:, :, :], ALU.mult)
            bada = P.sbuf(t, "bada", [nu, 6 * D], F32)
            self.dma("sp", bada, bada[:, :], w["b_ada"], w["b_ada"][0:1, :].partition_broadcast(nu).rearrange("p o f -> p (o f)"), dsc)
            modsb = P.sbuf(t, "modsb", [nu, 6 * D], F32)
            wab = [P.sbuf(t, f"wab{i}", [128, 8, 512], F32) for i in range(2)]
            dsa = [P.dsem("wab") for i in range(2)]
            pm = [P.psum(t, f"pmod{i}", [nu, 512], F32) for i in range(2)]
            wav = w["w_ada"][:, :].rearrange("(kc p) n -> p kc n", p=128)
            for cb in range(12):
                b = cb % 2
                self.dma("sp", wab[b], wab[b][:, :, :], w["w_ada"], wav[:, :, cb * 512:(cb + 1) * 512], dsa[b])
                for kc in range(8):
                    self.mm(pm[b], pm[b][:, :], sc, sc[:, kc, :], wab[b], wab[b][:, kc, :], kc == 0, kc == 7)
                self.tt(modsb, modsb[:, cb * 512:(cb + 1) * 512], pm[b], pm[b][:, :], bada, bada[:, cb * 512:(cb + 1) * 512], ALU.add)
            self.dma("sp", self.MOD, self.MOD[:, :], modsb, modsb[:, :], dsc)
            P.barrier()

    def mod_tile(self, st, name, u, j, ds, gname=None):
        t = self.bload(st, self.uid(name), self.MOD, self.MOD[u:u + 1, j * D:(j + 1) * D], D, ds)
        if gname is not None:
            g = self.bload(st, self.uid(name + "g"), self.w[gname], self.w[gname][0:1, :], D, ds)
            self.stt(t, t[:, :], t, t[:, :], 1.0, g, g[:, :], ALU.add, ALU.mult)
        return t

    def rms_rstd(self, xb, xap, n, junkb, junkap, ssb, ssap, tmpb, tmpap, outb, outap):
        self.P.op("act", lambda e: e.activation(out=junkap, in_=xap, func=AF.Square, accum_out=ssap), reads=[xb], writes=[junkb, ssb])
        self.rsqrt(outb, outap, ssb, ssap, tmpb, tmpap, scale=1.0 / n, bias=EPS)

    def pass1(self, u):
        P = self.P
        S, n_own, smp = self.units[u]
        nt, not_ = S // 128, n_own // 128
        w = self.w
        with ExitStack() as st:
            ds = self.gp
            A1 = self.mod_tile(st, "A1", u, 1, ds, "g_norm1")
            B1 = self.mod_tile(st, "B1", u, 0, ds)
            gq = P.sbuf(st, "gq", [128, 8, HD], F32)
            gk = P.sbuf(st, "gk", [128, 8, HD], F32)
            for g in range(8):
                self.dma("sp", gq, gq[:, g, :], w["g_q"], w["g_q"][0:1, :].partition_broadcast(128).rearrange("p o f -> p (o f)"), ds)
                self.dma("sp", gk, gk[:, g, :], w["g_k"], w["g_k"][0:1, :].partition_broadcast(128).rearrange("p o f -> p (o f)"), ds)
            pos = P.sbuf(st, "pos", [128, nt], F32)
            self.dma("sp", pos, pos[:, :], self.posu[u], self.posu[u][:, :], ds)
            cosT = P.sbuf(st, "cosT", [128, nt, 8], F32)
            sinT = P.sbuf(st, "sinT", [128, nt, 8], F32)
            if True:
                t2 = st
                ang = P.sbuf(t2, "ang", [128, nt, 8], F32)
                ki = P.sbuf(t2, "ki", [128, nt, 8], I32)
                kf = P.sbuf(t2, "kf", [128, nt, 8], F32)
                fr = P.sbuf(t2, "fr", [128, nt, 8], F32)
                adj = P.sbuf(t2, "adj", [128, nt, 8], F32)
                for i in range(8):
                    self.ts(ang, ang[:, :, i], pos, pos[:, :], INV_FREQ[i] / TWO_PI, ALU.mult)
                for (tab, off) in ((sinT, 0.0), (cosT, 0.25)):
                    self.ts(fr, fr[:, :, :], ang, ang[:, :, :], off, ALU.add)
                    self.cp(ki, ki[:, :, :], fr, fr[:, :, :])
                    self.cp(kf, kf[:, :, :], ki, ki[:, :, :])
                    self.tt(fr, fr[:, :, :], fr, fr[:, :, :], kf, kf[:, :, :], ALU.subtract)
                    self.ts(adj, adj[:, :, :], fr, fr[:, :, :], 0.5, ALU.is_gt)
                    self.tt(fr, fr[:, :, :], fr, fr[:, :, :], adj, adj[:, :, :], ALU.subtract)
                    self.ts(adj, adj[:, :, :], fr, fr[:, :, :], -0.5, ALU.is_lt)
                    self.tt(fr, fr[:, :, :], fr, fr[:, :, :], adj, adj[:, :, :], ALU.add)
                    self.act(tab, tab[:, :, :], fr, fr[:, :, :], AF.Sin, scale=TWO_PI)
            xt = [P.sbuf(st, f"xt{i}", [128, D], F32) for i in range(3)]
            dsx = [self.cds("xt", i) for i in range(3)]
            junk = P.sbuf(st, "junk", [128, D], BF16)
            sm = [P.sbuf(st, f"sm{i}", [128, 8], F32) for i in range(2)]
            xn = P.sbuf(st, "xn", [128, D], F32)
            hb = [P.sbuf(st, f"hb{i}", [128, D], BF16) for i in range(2)]
            hT = [P.sbuf(st, f"hT{i}", [128, 8, 128], BF16) for i in range(2)]
            pTa = P.psum(st, "pTa", [128, 8, 128], BF16)
            pTb = P.psum(st, "pTb", [128, 8, 128], BF16)
            NBK = 6
            pb = [P.psum(st, f"pb{i}", [128, 512], F32) for i in range(NBK)]
            sq = P.sbuf(st, "sq", [128, 8, HD], F32)
            ss8 = P.sbuf(st, "ss8", [128, 8], F32)
            ln8 = P.sbuf(st, "ln8", [128, 8], F32)
            rs8 = P.sbuf(st, "rs8", [128, 8], F32)
            qn = P.sbuf(st, "qn", [128, 8, HD], F32)
            xr = P.sbuf(st, "xr", [128, 8, 16], F32)
            r1 = P.sbuf(st, "r1", [128, 8, 8], F32)
            r2 = P.sbuf(st, "r2", [128, 8, 8], F32)
            kb_ = [P.sbuf(st, f"kb{i}", [128, 8, HD], BF16) for i in range(2)]
            qb_ = [P.sbuf(st, f"qb{i}", [128, 8, HD], BF16) for i in range(2)]
            G = 4
            qst = P.sbuf(st, "qst", [128, NH, G * 128], BF16)
            kst = P.sbuf(st, "kst", [128, NH, G * 128], BF16)
            zst = P.sbuf(st, "zst", [128, 4, G * 128], BF16)
            dq, dk, dz = self.cds("qst"), self.cds("kst"), self.cds("zst")
            vs = [P.sbuf(st, f"vs{i}", [128, 512], BF16) for i in range(2)]
            dv = [self.cds("vs", i) for i in range(2)]
            eg = P.sbuf(st, "eg", [128, 512], F32)
            zb = [P.sbuf(st, f"zb{i}", [128, 512], BF16) for i in range(2)]
            zh = P.sbuf(st, "zh", [128, 4, 16], BF16)
            dzh = self.cds("zh")
            nbank = [0]
            banks = {}

            def bank():
                nbank[0] += 1
                return pb[nbank[0] % NBK]

            def proj(hTt, cb):
                b = bank()
                for kc in range(8):
                    self.mm(b, b[:, :], hTt, hTt[:, kc, :], self.win, self.win[:, kc, cb * 512:(cb + 1) * 512], kc == 0, kc == 7)
                return b

            def qk_post(b, gt, t, out):
                bv = b[:, :].rearrange("p (g d) -> p g d", g=8)
                self.act(sq, sq[:, :, :], b, bv, AF.Square)
                self.red(ss8, ss8[:, :], sq, sq[:, :, :])
                self.rsqrt(rs8, rs8[:, :], ss8, ss8[:, :], ln8, ln8[:, :], scale=1.0 / HD, bias=EPS)
                self.tt(qn, qn[:, :, :], b, bv, rs8, rs8[:, :].unsqueeze(2).to_broadcast([128, 8, HD]), ALU.mult)
                self.tt(out, out[:, :, :], qn, qn[:, :, :], gt, gt[:, :, :], ALU.mult)
                self.tt(xr, xr[:, :, :], qn, qn[:, :, 0:16], gt, gt[:, :, 0:16], ALU.mult)
                cb_ = cosT[:, t, :].unsqueeze(1).to_broadcast([128, 8, 8])
                sb_ = sinT[:, t, :].unsqueeze(1).to_broadcast([128, 8, 8])
                self.tt(r1, r1[:, :, :], xr, xr[:, :, 0:8], cosT, cb_, ALU.mult)
                self.tt(r2, r2[:, :, :], xr, xr[:, :, 8:16], sinT, sb_, ALU.mult)
                self.tt(out, out[:, :, 0:8], r1, r1[:, :, :], r2, r2[:, :, :], ALU.subtract)
                self.tt(r1, r1[:, :, :], xr, xr[:, :, 8:16], cosT, cb_, ALU.mult)
                self.tt(r2, r2[:, :, :], xr, xr[:, :, 0:8], sinT, sb_, ALU.mult)
                self.tt(out, out[:, :, 8:16], r1, r1[:, :, :], r2, r2[:, :, :], ALU.add)

            def kind(t):
                own = t < not_
                halo_r = smp and t == not_
                halo_l = smp and t == nt - 1
                return own, halo_r, halo_l

            def loadx(t):
                if t < nt:
                    self.dma("sp", xt[t % 3], xt[t % 3][:, :], self.xu[u], self.xu[u][t * 128:(t + 1) * 128, :], dsx[t % 3])

            def stA(t):
                s = t % 2
                x_ = xt[t % 3]
                loadx(t + 2)
                self.rms_rstd(x_, x_[:, :], D, junk, junk[:, :], sm[s], sm[s][:, 0:1], sm[s], sm[s][:, 1:2], sm[s], sm[s][:, 2:3])
                self.stt(xn, xn[:, :], x_, x_[:, :], sm[s][:, 2:3], A1, A1[:, :], ALU.mult, ALU.mult, reads=[sm[s]])
                self.tt(hb[s], hb[s][:, :], xn, xn[:, :], B1, B1[:, :], ALU.add)
                for kc in range(8):
                    self.tp(pTa, pTa[:, kc, :], hb[s], hb[s][:, kc * 128:(kc + 1) * 128])
                self.cp(hT[s], hT[s][:, :, :], pTa, pTa[:, :, :], eng="act")

            def stP(t):
                s = t % 2
                own, halo_r, halo_l = kind(t)
                bk = {"k": proj(hT[s], 1)}
                if own:
                    bk["q"] = proj(hT[s], 0)
                bk["v"] = proj(hT[s], 2)
                if own or halo_r or halo_l:
                    bk["a"] = proj(hT[s], 3)
                    bk["g"] = proj(hT[s], 4)
                banks[t] = bk

            def stPost(t):
                s = t % 2
                bk = banks[t]
                qk_post(bk["k"], gk, t, kb_[s])
                if "q" in bk:
                    qk_post(bk["q"], gq, t, qb_[s])
                self.cp(vs[s], vs[s][:, :], bk["v"], bk["v"][:, :])
                self.dma("sp", self.VV[u], self.VV[u][:, t, :], vs[s], vs[s][:, :], dv[s])
                if "a" in bk:
                    ba, bg = bk["a"], bk["g"]
                    self.act(eg, eg[:, :], bg, bg[:, :], AF.Exp, scale=-1.0)
                    self.ts(eg, eg[:, :], eg, eg[:, :], 1.0, ALU.add)
                    self.recip(eg, eg[:, :], eg, eg[:, :])
                    self.tt(zb[s], zb[s][:, :], ba, ba[:, :], eg, eg[:, :], ALU.mult)

            def stBack(t):
                s = t % 2
                own, halo_r, halo_l = kind(t)
                bk = banks.pop(t)
                g0 = (t % G) * 128
                for h in range(NH):
                    self.tp(pTb, pTb[:, h, :], kb_[s], kb_[s][:, 2 * h:2 * h + 2, :].rearrange("p g d -> p (g d)"))
                if own:
                    for h in range(NH):
                        self.tp(pTb, pTb[:, 4 + h, :], qb_[s], qb_[s][:, 2 * h:2 * h + 2, :].rearrange("p g d -> p (g d)"))
                self.cp(kst, kst[:, :, g0:g0 + 128], pTb, pTb[:, 0:NH, :])
                if t % G == G - 1 or t == nt - 1:
                    t0 = (t // G) * G
                    nn = (t - t0 + 1) * 128
                    self.dma("sp", self.KT[u], self.KT[u][:, :, t0 * 128:t0 * 128 + nn].rearrange("h p n -> p h n"), kst, kst[:, :, 0:nn], dk)
                if own:
                    self.cp(qst, qst[:, :, g0:g0 + 128], pTb, pTb[:, 4:4 + NH, :])
                    if t % G == G - 1 or t == not_ - 1:
                        t0 = (t // G) * G
                        nn = (t - t0 + 1) * 128
                        self.dma("sp", self.QT[u], self.QT[u][:, :, t0 * 128:t0 * 128 + nn].rearrange("h p n -> p h n"), qst, qst[:, :, 0:nn], dq)
                if "a" in bk:
                    for c in range(4):
                        self.tp(pTb, pTb[:, c, :], zb[s], zb[s][:, c * 128:(c + 1) * 128])
                    if own:
                        self.cp(zst, zst[:, :, g0:g0 + 128], pTb, pTb[:, 0:4, :])
                        if t % G == G - 1 or t == not_ - 1:
                            t0 = (t // G) * G
                            nn = (t - t0 + 1) * 128
                            self.dma("sp", self.ZT[u], self.ZT[u][:, :, CP + 1 + t0 * 128:CP + 1 + t0 * 128 + nn].rearrange("c p n -> p c n"), zst, zst[:, :, 0:nn], dz)
                    elif halo_r:
                        self.ts(zh, zh[:, :, 0:CP], pTb, pTb[:, 0:4, 0:CP], self.flags[:, 1:2], ALU.mult, reads=[self.flags])
                        self.dma("sp", self.ZT[u], self.ZT[u][:, :, CP + 1 + n_own:CP + 1 + n_own + CP].rearrange("c p n -> p c n"), zh, zh[:, :, 0:CP], dzh)
                    else:
                        self.ts(zh, zh[:, :, 0:CP], pTb, pTb[:, 0:4, 128 - CP:128], self.flags[:, 0:1], ALU.mult, reads=[self.flags])
                        self.dma("sp", self.ZT[u], self.ZT[u][:, :, 1:1 + CP].rearrange("c p n -> p c n"), zh, zh[:, :, 0:CP], dzh)

            loadx(0)
            loadx(1)
            stA(0)
            for t in range(nt):
                stP(t)
                if t > 0:
                    stBack(t - 1)
                if t + 1 < nt:
                    stA(t + 1)
                stPost(t)
            stBack(nt - 1)
            P.barrier()

    def pass2(self, u, st_ot):
        P = self.P
        S, n_own, smp = self.units[u]
        nkc = S // 128
        QB = min(512, n_own)
        OT = P.sbuf(st_ot, self.uid("OT"), [128, NH, n_own], BF16)
        with ExitStack() as st:
            kt = P.sbuf(st, "kt", [128, S], BF16)
            vv = P.sbuf(st, "vv", [128, nkc, VD], BF16)
            qt = P.sbuf(st, "qt", [128, n_own], BF16)
            dl = self.gp
            psc = [[P.psum(st, f"psc{i}{c}", [128, QB], F32) for c in range(2)] for i in range(2)]
            pO = [P.psum(st, f"pO{c}", [128, QB], F32) for c in range(2)]
            pS = [P.psum(st, f"pS{c}", [128, QB], F32) for c in range(2)]
            et = [[P.sbuf(st, f"et{i}{c}", [128, QB], BF16) for c in range(2)] for i in range(3)]
            rr = [P.sbuf(st, f"rr{c}", [128, QB], F32) for c in range(2)]
            oo = [P.sbuf(st, f"oo{c}", [128, QB], F32) for c in range(2)]
            osq = P.sbuf(st, "osq", [128, QB], BF16)
            lnv = P.sbuf(st, "lnv", [128, QB], F32)
            pending = [None]
            for h in range(NH):
                self.dma("sp", kt, kt[:, :], self.KT[u], self.KT[u][h, :, :], dl)
                for k0 in range(0, nkc, 16):
                    k1 = min(nkc, k0 + 16)
                    self.dma("sp", vv, vv[:, k0:k1, :], self.VV[u], self.VV[u][:, k0:k1, h * VD:(h + 1) * VD], dl)
                self.dma("sp", qt, qt[:, :], self.QT[u], self.QT[u][h, :, :], dl)
                for q0 in range(0, n_own, QB):
                    def scores(kc, i):
                        for c in range(2):
                            self.mm(psc[i][c], psc[i][c][:, :], kt, kt[c * 64:(c + 1) * 64, kc * 128:(kc + 1) * 128],
                                    qt, qt[c * 64:(c + 1) * 64, q0:q0 + QB], True, True)
                    scores(0, 0)
                    kd = min(8, nkc - 1)
                    for kc in range(nkc):
                        i = kc % 2
                        if kc == kd and pending[0] is not None:
                            pending[0](psc[1 - i][0])
                            pending[0] = None
                        if kc + 1 < nkc:
                            scores(kc + 1, 1 - i)
                        j = kc % 3
                        for c in range(2):
                            self.act(et[j][c], et[j][c][:, :], psc[i][c], psc[i][c][:, :], AF.Exp, scale=HD ** -0.5)
                        for c in range(2):
                            self.mm(pO[c], pO[c][:, :], vv, vv[:, kc, :], et[j][c], et[j][c][:, :], kc == 0, kc == nkc - 1)
                            self.mm(pS[c], pS[c][:, :], self.ones, self.ones[:, :], et[j][c], et[j][c][:, :], kc == 0, kc == nkc - 1)
                    for c in range(2):
                        self.cp(rr[c], rr[c][:, :], pS[c], pS[c][:, :], eng="act" if c else "dve")
                        self.cp(oo[c], oo[c][:, :], pO[c], pO[c][:, :], eng="act" if c else "dve")
                    for c in range(2):
                        self.recip(rr[c], rr[c][:, :], rr[c], rr[c][:, :])
                        self.tt(oo[c], oo[c][:, :], oo[c], oo[c][:, :], rr[c], rr[c][:, :], ALU.mult)
                    self.stt(oo[0], oo[0][:, :], oo[1], oo[1][:, :], self.neglam[:, 0:1], oo[0], oo[0][:, :], ALU.mult, ALU.add, reads=[self.neglam])
                    self.tt(osq, osq[:, :], oo[0], oo[0][:, :], oo[0], oo[0][:, :], ALU.mult)

                    def fin2(b, h=h, q0=q0):
                        self.mm(b, b[:, :], self.ones, self.ones[:, :], osq, osq[:, :], True, True)
                        self.rsqrt(rr[1], rr[1][:, :], b, b[:, :], lnv, lnv[:, :], scale=1.0 / VD, bias=EPS)
                        self.stt(OT, OT[:, h, q0:q0 + QB], oo[0], oo[0][:, :], self.gsub[:, 0:1], rr[1], rr[1][:, :], ALU.mult, ALU.mult, reads=[self.gsub])
                    pending[0] = fin2
            if pending[0] is not None:
                pending[0](psc[0][0])
                pending[0] = None
            P.barrier()
        return OT

    def pass3(self, u, OT, tile_base):
        P = self.P
        S, n_own, smp = self.units[u]
        w = self.w
        GQ = min(512, n_own)
        ZW = n_own + 2 * CP + 2
        with ExitStack() as st:
            ds = self.gp
            G1 = self.mod_tile(st, "G1", u, 2, ds)
            A2 = self.mod_tile(st, "A2", u, 4, ds, "g_norm2")
            B2 = self.mod_tile(st, "B2", u, 3, ds)
            wout = P.sbuf(st, "wout", [128, 8, D], BF16)
            wov = w["w_out"][:, :].rearrange("(kc p) n -> p kc n", p=128)
            for kc in range(8):
                for cb in range(2):
                    self.dma("pool", wout, wout[:, kc, cb * 512:(cb + 1) * 512], w["w_out"], wov[:, kc, cb * 512:(cb + 1) * 512], ds)
            wr = P.sbuf(st, "wr", [128, 8, 36], BF16)
            self.dma("pool", wr, wr[:, :, 0:4], w["w_rg"], w["w_rg"][:, :].rearrange("(kc p) n -> p kc n", p=128), ds)
            self.dma("pool", wr, wr[:, :, 4:36], w["w_re"], w["w_re"][:, :].rearrange("(kc p) n -> p kc n", p=128), ds)
            brt = P.sbuf(st, "brt", [128, 36], F32)
            self.dma("sp", brt, brt[:, 0:4], w["b_rg"], w["b_rg"][0:1, :].partition_broadcast(128).rearrange("p o f -> p (o f)"), ds)
            self.dma("sp", brt, brt[:, 4:36], w["b_re"], w["b_re"][0:1, :].partition_broadcast(128).rearrange("p o f -> p (o f)"), ds)
            cpar = P.sbuf(st, "cpar", [128, 4, 3], F32)
            for c in range(4):
                for i, nme in enumerate(["b_dw", "g_conv_ln", "b_conv_ln"]):
                    self.dma("sp", cpar, cpar[:, c, i:i + 1], w[nme], w[nme][c * 128:(c + 1) * 128, :], ds)
            wdw = P.sbuf(st, "wdw", [128, 4, CK], F32)
            for c in range(4):
                self.dma("sp", wdw, wdw[:, c, :], w["w_dw"], w["w_dw"][:, c * 128:(c + 1) * 128].rearrange("j p -> p j"), ds, allow_slow_non_contiguous=True)
            diag = P.sbuf(st, "diag", [128, 4, CK, 128], BF16)
            for c in range(4):
                for j in range(CK):
                    self.ts(diag, diag[:, c, j, :], self.ident, self.ident[:, :], wdw[:, c, j:j + 1], ALU.mult, reads=[wdw])
            zt = P.sbuf(st, "zt", [128, 4, ZW], BF16)
            if smp:
                self.memset(zt, zt[:, :, 0:1], 0.0)
                self.memset(zt, zt[:, :, ZW - 1:ZW], 0.0)
                self.dma("sp", zt, zt[:, :, 1:ZW - 1], self.ZT[u], self.ZT[u][:, :, 1:ZW - 1].rearrange("c p n -> p c n"), ds)
            else:
                self.memset(zt, zt[:, :, 0:CP + 1], 0.0)
                self.memset(zt, zt[:, :, CP + 1 + n_own:ZW], 0.0)
                self.dma("sp", zt, zt[:, :, CP + 1:CP + 1 + n_own], self.ZT[u], self.ZT[u][:, :, CP + 1:CP + 1 + n_own].rearrange("c p n -> p c n"), ds)
            pc = [P.psum(st, f"pc{i}", [128, GQ], F32) for i in range(2)]
            pst = [P.psum(st, f"pst{i}", [128, GQ], F32) for i in range(2)]
            po = [P.psum(st, f"po{i}", [128, 512], F32) for i in range(2)]
            pT = P.psum(st, "pT3", [128, 8, 128], BF16)
            pr = P.psum(st, "pr", [128, 512], F32)
            zc = [P.sbuf(st, f"zc{c}", [128, GQ], F32) for c in range(4)]
            zcb = P.sbuf(st, "zcb", [128, GQ], BF16)
            zsq = P.sbuf(st, "zsq", [128, GQ], BF16)
            mean = P.sbuf(st, "mean", [128, GQ], F32)
            var = P.sbuf(st, "var", [128, GQ], F32)
            rstd = P.sbuf(st, "rstd", [128, GQ], F32)
            tmpf = P.sbuf(st, "tmpf", [128, GQ], F32)
            tmpe = P.sbuf(st, "tmpe", [128, GQ], F32)
            cvT = P.sbuf(st, "cvT", [128, 4, GQ], BF16)
            xt = [P.sbuf(st, f"x3{i}", [128, D], F32) for i in range(3)]
            dsx = [self.cds("xt", i) for i in range(3)]
            n_own_t = n_own // 128

            def loadx3(ti):
                if ti < n_own_t:
                    self.dma("sp", xt[ti % 3], xt[ti % 3][:, :], self.xu[u], self.xu[u][ti * 128:(ti + 1) * 128, :], dsx[ti % 3])
            loadx3(0)
            loadx3(1)
            x1 = [P.sbuf(st, f"x1{i}", [128, D], F32) for i in range(2)]
            dx1 = [self.cds("x1", i) for i in range(2)]
            h2f = P.sbuf(st, "h2f", [128, D], F32)
            h2 = [P.sbuf(st, f"h2{i}", [128, D], BF16) for i in range(2)]
            dh2 = [self.cds("h2", i) for i in range(2)]
            h2T = P.sbuf(st, "h2T", [128, 8, 128], BF16)
            junk = P.sbuf(st, "junk3", [128, D], BF16)
            sm = P.sbuf(st, "sm3", [128, 8], F32)
            R = {k: P.sbuf(st, "r_" + k, shp, F32) for k, shp in [("lg", [128, 36]), ("gm", [128, 4]), ("goh", [128, 4]), ("ex", [128, 4]),
                                                                   ("leg4", [128, 4, 8]), ("leg", [128, 8]), ("oh1", [128, 8]), ("oh2", [128, 8]),
                                                                   ("msk", [128, 8]), ("sc", [128, 8]), ("E1", [128, 4, 8]), ("E2", [128, 4, 8]),
                                                                   ("pos", [128, NE]), ("tmp32", [128, NE])]}
            mskb = P.sbuf(st, "mskb", [128, NE], BF16)
            ntl = GQ // 128
            R4 = {k: P.sbuf(st, "r4_" + k, shp, F32) for k, shp in [
                ("lg", [128, ntl, 36]), ("gm", [128, ntl]), ("goh", [128, ntl, 4]), ("dd", [128, ntl, 4]), ("ex", [128, ntl, 4]),
                ("se", [128, ntl]), ("pg", [128, ntl]), ("leg4", [128, ntl, 4, 8]), ("leg", [128, ntl, 8]), ("m1", [128, ntl]),
                ("m2", [128, ntl]), ("oh1", [128, ntl, 8]), ("oh2", [128, ntl, 8]), ("msk", [128, ntl, 8]), ("d21", [128, ntl]),
                ("ed", [128, ntl]), ("p1", [128, ntl]), ("p2", [128, ntl]), ("E1", [128, ntl, 4, 8]), ("E2", [128, ntl, 4, 8]),
                ("pos", [128, ntl, NE]), ("tmp", [128, ntl, NE])]}
            mskb4 = P.sbuf(st, "mskb4", [128, ntl, NE], BF16)

            for g0 in range(0, n_own, GQ):
                for c in range(4):
                    b = pc[c % 2]
                    for j in range(CK):
                        self.mm(b, b[:, :], diag, diag[:, c, j, :], zt, zt[:, c, g0 + j + 1:g0 + j + 1 + GQ], j == 0, j == CK - 1)
                    self.ts(zc[c], zc[c][:, :], b, b[:, :], cpar[:, c, 0:1], ALU.add, reads=[cpar])
                    self.cp(zcb, zcb[:, :], zc[c], zc[c][:, :])
                    self.tt(zsq, zsq[:, :], zc[c], zc[c][:, :], zc[c], zc[c][:, :], ALU.mult)
                    self.mm(pst[0], pst[0][:, :], self.ones, self.ones[:, :], zcb, zcb[:, :], c == 0, c == 3)
                    self.mm(pst[1], pst[1][:, :], self.ones, self.ones[:, :], zsq, zsq[:, :], c == 0, c == 3)
                self.ts(mean, mean[:, :], pst[0], pst[0][:, :], 1.0 / 512, ALU.mult)
                self.tt(tmpf, tmpf[:, :], mean, mean[:, :], mean, mean[:, :], ALU.mult)
                self.stt(var, var[:, :], pst[1], pst[1][:, :], 1.0 / 512, tmpf, tmpf[:, :], ALU.mult, ALU.subtract)
                self.ts(var, var[:, :], var, var[:, :], 0.0, ALU.max, EPS, ALU.add)
                self.rsqrt(rstd, rstd[:, :], var, var[:, :], tmpe, tmpe[:, :])
                for c in range(4):
                    self.tt(tmpf, tmpf[:, :], zc[c], zc[c][:, :], mean, mean[:, :], ALU.subtract)
                    self.tt(tmpf, tmpf[:, :], tmpf, tmpf[:, :], rstd, rstd[:, :], ALU.mult)
                    self.ts(tmpf, tmpf[:, :], tmpf, tmpf[:, :], cpar[:, c, 1:2], ALU.mult, cpar[:, c, 2:3], ALU.add, reads=[cpar])
                    self.act(cvT, cvT[:, c, :], tmpf, tmpf[:, :], AF.Silu)
                def stA(tl):
                    tk = g0 + tl * 128
                    loadx3(tk // 128 + 2)
                    for cb in range(2):
                        for kc in range(8):
                            if kc < 4:
                                lb, lap = OT, OT[:, kc, tk:tk + 128]
                            else:
                                lb, lap = cvT, cvT[:, kc - 4, tl * 128:(tl + 1) * 128]
                            self.mm(po[cb], po[cb][:, :], lb, lap, wout, wout[:, kc, cb * 512:(cb + 1) * 512], kc == 0, kc == 7)

                def stB(tl):
                    tk = g0 + tl * 128
                    T = tile_base + tk // 128
                    s = (tk // 128) % 2
                    for cb in range(2):
                        self.tt(x1[s], x1[s][:, cb * 512:(cb + 1) * 512], po[cb], po[cb][:, :], G1, G1[:, cb * 512:(cb + 1) * 512], ALU.mult)
                    x_ = xt[(tk // 128) % 3]
                    self.tt(x1[s], x1[s][:, :], x1[s], x1[s][:, :], x_, x_[:, :], ALU.add)
                    self.dma("sp", self.X1, self.X1[T * 128:(T + 1) * 128, :], x1[s], x1[s][:, :], dx1[s])
                    self.rms_rstd(x1[s], x1[s][:, :], D, junk, junk[:, :], sm, sm[:, 0:1], sm, sm[:, 1:2], sm, sm[:, 2:3])
                    self.stt(h2f, h2f[:, :], x1[s], x1[s][:, :], sm[:, 2:3], A2, A2[:, :], ALU.mult, ALU.mult, reads=[sm])
                    self.tt(h2[s], h2[s][:, :], h2f, h2f[:, :], B2, B2[:, :], ALU.add)
                    self.dma("sp", self.H2, self.H2[T * 128:(T + 1) * 128, :], h2[s], h2[s][:, :], dh2[s])
                    for kc in range(8):
                        self.tp(pT, pT[:, kc, :], h2[s], h2[s][:, kc * 128:(kc + 1) * 128])
                    self.cp(h2T, h2T[:, :, :], pT, pT[:, :, :], eng="act")
                    for kc in range(8):
                        self.mm(pr, pr[:, tl * 64:tl * 64 + 36], h2T, h2T[:, kc, :], wr, wr[:, kc, :], kc == 0, kc == 7)

                stA(0)
                for tl in range(ntl):
                    stB(tl)
                    if tl + 1 < ntl:
                        stA(tl + 1)
                self.route4(R4, pr, brt, mskb4, tile_base + g0 // 128, ntl)
            P.barrier()

    def route4(self, R, pr, brt, mskb, T0, n):
        g = lambda k: R[k]
        lg, gm, goh, dd, ex, se, pg, leg4, leg, m1, m2, oh1, oh2, msk, d21, ed, p1, p2, E1, E2, pos, tmp = [g(k) for k in
            ("lg", "gm", "goh", "dd", "ex", "se", "pg", "leg4", "leg", "m1", "m2", "oh1", "oh2", "msk", "d21", "ed", "p1", "p2", "E1", "E2", "pos", "tmp")]
        ROUT = self.ROUT
        prv = pr[:, 0:n * 64].rearrange("p (t c) -> p t c", c=64)
        self.tt(lg, lg[:, :, :], pr, prv[:, :, 0:36], brt, brt[:, :].unsqueeze(1).to_broadcast([128, n, 36]), ALU.add)
        self.red(gm, gm[:, :], lg, lg[:, :, 0:4], ALU.max)
        gmb = gm[:, :].unsqueeze(2).to_broadcast([128, n, 4])
        self.tt(goh, goh[:, :, :], lg, lg[:, :, 0:4], gm, gmb, ALU.is_ge)
        self.tt(dd, dd[:, :, :], lg, lg[:, :, 0:4], gm, gmb, ALU.subtract)
        self.act(ex, ex[:, :, :], dd, dd[:, :, :], AF.Exp)
        self.red(se, se[:, :], ex, ex[:, :, :])
        self.recip(pg, pg[:, :], se, se[:, :])
        self.tt(leg4, leg4[:, :, :, :], lg, lg[:, :, 4:36].rearrange("p t (g e) -> p t g e", g=4), goh, goh[:, :, :].unsqueeze(3).to_broadcast([128, n, 4, 8]), ALU.mult)
        self.red(leg, leg[:, :, :], leg4, leg4[:, :, :, :].rearrange("p t g e -> p t e g"))
        self.red(m1, m1[:, :], leg, leg[:, :, :], ALU.max)
        self.tt(oh1, oh1[:, :, :], leg, leg[:, :, :], m1, m1[:, :].unsqueeze(2).to_broadcast([128, n, 8]), ALU.is_ge)
        self.stt(msk, msk[:, :, :], oh1, oh1[:, :, :], -1.0e30, leg, leg[:, :, :], ALU.mult, ALU.add)
        self.red(m2, m2[:, :], msk, msk[:, :, :], ALU.max)
        self.tt(oh2, oh2[:, :, :], msk, msk[:, :, :], m2, m2[:, :].unsqueeze(2).to_broadcast([128, n, 8]), ALU.is_ge)
        self.tt(d21, d21[:, :], m2, m2[:, :], m1, m1[:, :], ALU.subtract)
        self.act(ed, ed[:, :], d21, d21[:, :], AF.Exp)
        self.ts(p1, p1[:, :], ed, ed[:, :], 1.0, ALU.add)
        self.recip(p1, p1[:, :], p1, p1[:, :])
        self.tt(p2, p2[:, :], ed, ed[:, :], p1, p1[:, :], ALU.mult)
        self.tt(ROUT, ROUT[:, T0:T0 + n, 4], p1, p1[:, :], pg, pg[:, :], ALU.mult)
        self.tt(ROUT, ROUT[:, T0:T0 + n, 5], p2, p2[:, :], pg, pg[:, :], ALU.mult)
        gb = goh[:, :, :].unsqueeze(3).to_broadcast([128, n, 4, 8])
        self.tt(E1, E1[:, :, :, :], oh1, oh1[:, :, :].unsqueeze(2).to_broadcast([128, n, 4, 8]), goh, gb, ALU.mult)
        self.tt(E2, E2[:, :, :, :], oh2, oh2[:, :, :].unsqueeze(2).to_broadcast([128, n, 4, 8]), goh, gb, ALU.mult)
        E1f = E1[:, :, :, :].rearrange("p t g e -> p t (g e)")
        E2f = E2[:, :, :, :].rearrange("p t g e -> p t (g e)")
        self.tt(mskb, mskb[:, :, :], E1, E1f, E2, E2f, ALU.add)
        mf = mskb[:, :, :].rearrange("p t e -> p (t e)")
        self.mm(pr, pr[:, 256:256 + n * NE], self.utri, self.utri[:, :], mskb, mf, True, True)
        self.mm(pr, pr[:, 384:384 + n * NE], self.ones, self.ones[:, :], mskb, mf, True, True)
        for t in range(n):
            self.tt(pos, pos[:, t, :], pr, pr[:, 256 + t * NE:256 + (t + 1) * NE], self.Rcnt, self.Rcnt[:, :], ALU.add)
            self.tt(self.Rcnt, self.Rcnt[:, :], self.Rcnt, self.Rcnt[:, :], pr, pr[:, 384 + t * NE:384 + (t + 1) * NE], ALU.add)
        iob = self.iota32[:, :].unsqueeze(1).to_broadcast([128, n, NE])
        for k, (Eb, Ef) in enumerate(((E1, E1f), (E2, E2f))):
            self.tt(tmp, tmp[:, :, :], Eb, Ef, self.iota32, iob, ALU.mult)
            self.red(ROUT, ROUT[:, T0:T0 + n, k], tmp, tmp[:, :, :])
            self.tt(tmp, tmp[:, :, :], Eb, Ef, pos, pos[:, :, :], ALU.mult)
            self.red(ROUT, ROUT[:, T0:T0 + n, 2 + k], tmp, tmp[:, :, :])

    def route(self, R, pr, brt, mskb, T):
        lg, gm, goh, ex, leg4, leg, oh1, oh2, msk, sc, E1, E2, pos, tmp32 = [R[k] for k in
            ("lg", "gm", "goh", "ex", "leg4", "leg", "oh1", "oh2", "msk", "sc", "E1", "E2", "pos", "tmp32")]
        ROUT = self.ROUT
        self.tt(lg, lg[:, :], pr, pr[:, 0:36], brt, brt[:, :], ALU.add)
        self.red(gm, gm[:, 0:1], lg, lg[:, 0:4], ALU.max)
        self.ts(goh, goh[:, :], lg, lg[:, 0:4], gm[:, 0:1], ALU.is_ge, reads=[gm])
        self.ts(gm, gm[:, 1:2], gm, gm[:, 0:1], -1.0, ALU.mult)
        self.P.op("act", lambda e: e.activation(out=ex[:, :], in_=lg[:, 0:4], func=AF.Exp, bias=gm[:, 1:2], accum_out=gm[:, 2:3]), reads=[lg, gm], writes=[ex, gm])
        self.recip(gm, gm[:, 3:4], gm, gm[:, 2:3])
        self.tt(leg4, leg4[:, :, :], lg, lg[:, 4:36].rearrange("p (g e) -> p g e", g=4), goh, goh[:, :].unsqueeze(2).to_broadcast([128, 4, 8]), ALU.mult)
        self.red(leg, leg[:, :], leg4, leg4[:, :, :].rearrange("p g e -> p e g"))
        self.red(sc, sc[:, 0:1], leg, leg[:, :], ALU.max)
        self.ts(oh1, oh1[:, :], leg, leg[:, :], sc[:, 0:1], ALU.is_ge, reads=[sc])
        self.stt(msk, msk[:, :], oh1, oh1[:, :], -1.0e30, leg, leg[:, :], ALU.mult, ALU.add)
        self.red(sc, sc[:, 1:2], msk, msk[:, :], ALU.max)
        self.ts(oh2, oh2[:, :], msk, msk[:, :], sc[:, 1:2], ALU.is_ge, reads=[sc])
        self.tt(sc, sc[:, 2:3], sc, sc[:, 1:2], sc, sc[:, 0:1], ALU.subtract)
        self.act(sc, sc[:, 3:4], sc, sc[:, 2:3], AF.Exp)
        self.ts(sc, sc[:, 4:5], sc, sc[:, 3:4], 1.0, ALU.add)
        self.recip(sc, sc[:, 5:6], sc, sc[:, 4:5])
        self.tt(sc, sc[:, 6:7], sc, sc[:, 3:4], sc, sc[:, 5:6], ALU.mult)
        self.tt(ROUT, ROUT[:, T, 4:5], sc, sc[:, 5:6], gm, gm[:, 3:4], ALU.mult)
        self.tt(ROUT, ROUT[:, T, 5:6], sc, sc[:, 6:7], gm, gm[:, 3:4], ALU.mult)
        gb = goh[:, :].unsqueeze(2).to_broadcast([128, 4, 8])
        self.tt(E1, E1[:, :, :], oh1, oh1[:, :].unsqueeze(1).to_broadcast([128, 4, 8]), goh, gb, ALU.mult)
        self.tt(E2, E2[:, :, :], oh2, oh2[:, :].unsqueeze(1).to_broadcast([128, 4, 8]), goh, gb, ALU.mult)
        E1f = E1[:, :, :].rearrange("p g e -> p (g e)")
        E2f = E2[:, :, :].rearrange("p g e -> p (g e)")
        self.tt(mskb, mskb[:, :], E1, E1f, E2, E2f, ALU.add)
        self.mm(pr, pr[:, 0:NE], self.utri, self.utri[:, :], mskb, mskb[:, :], True, True)
        self.tt(pos, pos[:, :], pr, pr[:, 0:NE], self.Rcnt, self.Rcnt[:, :], ALU.add)
        self.mm(pr, pr[:, 0:NE], self.ones, self.ones[:, :], mskb, mskb[:, :], True, True)
        self.tt(self.Rcnt, self.Rcnt[:, :], self.Rcnt, self.Rcnt[:, :], pr, pr[:, 0:NE], ALU.add)
        for k, (Eb, Ef) in enumerate(((E1, E1f), (E2, E2f))):
            self.tt(tmp32, tmp32[:, :], Eb, Ef, self.iota32, self.iota32[:, :], ALU.mult)
            self.red(ROUT, ROUT[:, T, k:k + 1], tmp32, tmp32[:, :])
            self.tt(tmp32, tmp32[:, :], Eb, Ef, pos, pos[:, :], ALU.mult)
            self.red(ROUT, ROUT[:, T, 2 + k:3 + k], tmp32, tmp32[:, :])

    def moe(self):
        P = self.P
        NT, NB = self.n_tiles, self.n_blk
        ROUT = self.ROUT
        P.barrier(include_bg=True)
        with ExitStack() as st:
            ds = self.gp
            MB = 256
            assert MB * 128 >= 2 * self.n_tok
            wid = P.sbuf(st, "wid", [128, NB], I32)
            dst = P.sbuf(st, "dst", [128, NT, 2], I32)
            sA = ExitStack()
            st_outer, st = st, sA
            thr = P.sbuf(st, "thr", [128, MB], F32)
            thi = P.sbuf(st, "thi", [128, MB], I32)
            P.op("pool", lambda e: e.iota(thi[:, :], pattern=[[128, MB]], base=0, channel_multiplier=0), writes=[thi])
            self.cp(thr, thr[:, :], thi, thi[:, :])
            big = P.sbuf(st, "bigc", [128, NE, MB], F32)
            nblk = P.sbuf(st, "nblk", [128, NE], F32)
            self.tt(big, big[:, :, :], self.Rcnt, self.Rcnt[:, :].unsqueeze(2).to_broadcast([128, NE, MB]),
                    thr, thr[:, :].unsqueeze(1).to_broadcast([128, NE, MB]), ALU.is_gt)
            self.red(nblk, nblk[:, :], big, big[:, :, :])
            L = P.sbuf(st, "Ltri", [128, NE, NE], F32)
            self.tt(L, L[:, :, :], self.iota32, self.iota32[:, :].unsqueeze(1).to_broadcast([128, NE, NE]),
                    self.iota32, self.iota32[:, :].unsqueeze(2).to_broadcast([128, NE, NE]), ALU.is_le)
            self.tt(L, L[:, :, :], L, L[:, :, :], nblk, nblk[:, :].unsqueeze(1).to_broadcast([128, NE, NE]), ALU.mult)
            pend = P.sbuf(st, "pend", [128, NE], F32)
            pstart = P.sbuf(st, "pstart", [128, NE], F32)
            self.red(pend, pend[:, :], L, L[:, :, :])
            self.tt(pstart, pstart[:, :], pend, pend[:, :], nblk, nblk[:, :], ALU.subtract)
            self.ts(pend, pend[:, :], pend, pend[:, :], 128.0, ALU.mult)
            self.ts(pstart, pstart[:, :], pstart, pstart[:, :], 128.0, ALU.mult)
            cmpb = P.sbuf(st, "cmpb", [128, NB, NE], F32)
            eid = P.sbuf(st, "eid", [128, NB + 2], F32)
            self.tt(cmpb, cmpb[:, :, :], pend, pend[:, :].unsqueeze(1).to_broadcast([128, NB, NE]),
                    thr, thr[:, 0:NB].unsqueeze(2).to_broadcast([128, NB, NE]), ALU.is_le)
            self.memset(eid, eid[:, 0:2], -1.0)
            self.red(eid, eid[:, 2:NB + 2], cmpb, cmpb[:, :, :])
            self.ts(eid, eid[:, 2:NB + 2], eid, eid[:, 2:NB + 2], float(NE - 1), ALU.min)
            same = P.sbuf(st, "same", [128, NB], F32)
            widf = P.sbuf(st, "widf", [128, NB], F32)
            pidx = P.sbuf(st, "pidx", [128, 1], F32)
            pidi = P.sbuf(st, "pidi", [128, 1], I32)
            P.op("pool", lambda e: e.iota(pidi[:, :], pattern=[[0, 1]], base=0, channel_multiplier=1), writes=[pidi])
            self.cp(pidx, pidx[:, :], pidi, pidi[:, :])
            self.tt(same, same[:, :], eid, eid[:, 2:NB + 2], eid, eid[:, 0:NB], ALU.is_equal)
            self.ts(widf, widf[:, :], eid, eid[:, 2:NB + 2], 128.0, ALU.mult, pidx[:, 0:1], ALU.add, reads=[pidx])
            self.stt(widf, widf[:, :], same, same[:, :], BIG, widf, widf[:, :], ALU.mult, ALU.add)
            self.cp(wid, wid[:, :], widf, widf[:, :])
            oh = P.sbuf(st, "ohd", [128, NT * 2, NE], F32)
            ev = ROUT[:, :, 0:2]
            self.tt(oh, oh[:, :, :].rearrange("p (t k) e -> p t k e", k=2), ROUT, ev.unsqueeze(3).to_broadcast([128, NT, 2, NE]),
                    self.iota32, self.iota32[:, :].unsqueeze(1).unsqueeze(1).to_broadcast([128, NT, 2, NE]), ALU.is_equal)
            self.tt(oh, oh[:, :, :], oh, oh[:, :, :], pstart, pstart[:, :].unsqueeze(1).to_broadcast([128, NT * 2, NE]), ALU.mult)
            dstf = P.sbuf(st, "dstf", [128, NT, 2], F32)
            self.red(dstf, dstf[:, :, :].rearrange("p t k -> p (t k)"), oh, oh[:, :, :])
            self.tt(dstf, dstf[:, :, :], dstf, dstf[:, :, :], ROUT, ROUT[:, :, 2:4], ALU.add)
            self.cp(dst, dst[:, :, :], dstf, dstf[:, :, :])
            if self.dbg:
                self.dma("sp", self.DROUT, self.DROUT[:, :], ROUT, ROUT[:, :, :].rearrange("p t k -> p (t k)"), ds)
                self.dma("sp", self.DMISC, self.DMISC[:, 0:NE], self.Rcnt, self.Rcnt[:, :], ds)
                self.dma("sp", self.DMISC, self.DMISC[:, NE:NE + NB + 1], eid, eid[:, 1:NB + 2], ds)
                self.dma("sp", self.DMISC, self.DMISC[:, NE + NB + 1:], dstf, dstf[:, :, :].rearrange("p t k -> p (t k)"), ds)
            hl = [P.sbuf(st, f"hl{i}", [128, D], BF16) for i in range(3)]
            dhl = [P.dsem("hl") for i in range(3)]
            dsc = [[P.dsem("scat") for k in range(2)] for i in range(3)]
            XS = self.XS
            for T in range(NT):
                s = T % 3
                self.dma("sp", hl[s], hl[s][:, :], self.H2, self.H2[T * 128:(T + 1) * 128, :], dhl[s])
                for k in range(2):
                    P.op("pool", (lambda hb, ia: lambda e: e.indirect_dma_start(out=XS[:, :], out_offset=bass.IndirectOffsetOnAxis(ap=ia, axis=0), in_=hb[:, :], in_offset=None))(hl[s], dst[:, T, k:k + 1]),
                         reads=[hl[s], dst], writes=[XS], dsem=dsc[s][k])
            P.barrier()
            sA.close()
            st = st_outer
            wgu = [P.sbuf(st, f"wgu{i}", [128, 8, 2 * FF], BF16) for i in range(2)]
            wdn = [P.sbuf(st, f"wdn{i}", [128, 4, D], BF16) for i in range(2)]
            dwg = [P.dsem("wgu") for i in range(2)]
            dwd = [P.dsem("wdn") for i in range(2)]
            xb = [P.sbuf(st, f"xb{i}", [128, D], BF16) for i in range(3)]
            dxb = [P.dsem("xb") for i in range(3)]
            xT = [P.sbuf(st, f"xT{i}", [128, 8, 128], BF16) for i in range(2)]
            pT = [P.psum(st, f"pT6{i}", [128, 8, 128], BF16) for i in range(2)]
            py = [P.psum(st, f"py{i}", [128, 512], F32) for i in range(2)]
            pg = [[P.psum(st, f"pg{i}{c}", [128, 512], F32) for c in range(2)] for i in range(2)]
            eg = [P.sbuf(st, f"eg6{i}", [128, FF], F32) for i in range(2)]
            ab = [P.sbuf(st, f"ab6{i}", [128, FF], BF16) for i in range(2)]
            aT = [P.sbuf(st, f"aT6{i}", [128, 4, 128], BF16) for i in range(2)]
            yb = [P.sbuf(st, f"yb{i}", [128, D], F32) for i in range(2)]
            dyb = [P.dsem("yb") for i in range(2)]
            WGU, WDN = self.WGU, self.WDN
            bound = NE * 128 - 1
            regbox = {}

            def breg(e):
                if "r" not in regbox:
                    r = e.alloc_register("wbound")
                    e.reg_mov(r, bound)
                    regbox["r"] = r
                return regbox["r"]

            def gather_w(j, which):
                if j >= NB:
                    return
                ia = wid[:, j:j + 1]
                wg_, wd_ = wgu[j % 2], wdn[j % 2]
                if which == 0:
                    P.op("pool", lambda e: e.indirect_dma_start(out=wg_[:, :, :].rearrange("p k n -> p (k n)"), out_offset=None, in_=WGU[:, :], in_offset=bass.IndirectOffsetOnAxis(ap=ia, axis=0), bounds_check=breg(e), oob_is_err=False),
                         reads=[WGU, wid], writes=[wg_], dsem=dwg[j % 2])
                else:
                    P.op("pool", lambda e: e.indirect_dma_start(out=wd_[:, :, :].rearrange("p k n -> p (k n)"), out_offset=None, in_=WDN[:, :], in_offset=bass.IndirectOffsetOnAxis(ap=ia, axis=0), bounds_check=breg(e), oob_is_err=False),
                         reads=[WDN, wid], writes=[wd_], dsem=dwd[j % 2])

            def loadx(j):
                if j < NB:
                    self.dma("sp", xb[j % 3], xb[j % 3][:, :], self.XS, self.XS[j * 128:(j + 1) * 128, :], dxb[j % 3])

            def front(j):
                s = j % 2
                loadx(j + 2)
                for kc in range(8):
                    self.tp(pT[s], pT[s][:, kc, :], xb[j % 3], xb[j % 3][:, kc * 128:(kc + 1) * 128])
                self.cp(xT[s], xT[s][:, :, :], pT[s], pT[s][:, :, :], eng="act")
                for cb in range(2):
                    for kc in range(8):
                        self.mm(pg[s][cb], pg[s][cb][:, :], xT[s], xT[s][:, kc, :], wgu[s], wgu[s][:, kc, cb * 512:(cb + 1) * 512], kc == 0, kc == 7)
                gather_w(j + 2, 0)

            def mid(j):
                s = j % 2
                self.act(eg[s], eg[s][:, :], pg[s][0], pg[s][0][:, :], AF.Silu)
                self.tt(ab[s], ab[s][:, :], eg[s], eg[s][:, :], pg[s][1], pg[s][1][:, :], ALU.mult)

            def back1(j):
                s = j % 2
                for c in range(4):
                    self.tp(pT[s], pT[s][:, c, :], ab[s], ab[s][:, c * 128:(c + 1) * 128])
                self.cp(aT[s], aT[s][:, :, :], pT[s], pT[s][:, 0:4, :])
                for cb in range(2):
                    for kc in range(4):
                        self.mm(py[cb], py[cb][:, :], aT[s], aT[s][:, kc, :], wdn[s], wdn[s][:, kc, cb * 512:(cb + 1) * 512], kc == 0, kc == 3)
                gather_w(j + 2, 1)

            def back2(j):
                s = j % 2
                for cb in range(2):
                    self.cp(yb[s], yb[s][:, cb * 512:(cb + 1) * 512], py[cb], py[cb][:, :], eng="act" if cb else "dve")
                self.dma("sp", self.YB, self.YB[j * 128:(j + 1) * 128, :], yb[s], yb[s][:, :], dyb[s])

            gather_w(0, 0)
            gather_w(0, 1)
            gather_w(1, 0)
            gather_w(1, 1)
            loadx(0)
            loadx(1)
            front(0)
            mid(0)
            for j in range(NB):
                if j + 1 < NB:
                    front(j + 1)
                back1(j)
                if j + 1 < NB:
                    mid(j + 1)
                back2(j)
            P.barrier()
            yg = [[P.sbuf(st, f"yg{i}{k}", [128, D], F32) for k in range(2)] for i in range(2)]
            dyg = [[P.dsem("yg") for k in range(2)] for i in range(2)]
            x1 = [P.sbuf(st, f"x7{i}", [128, D], F32) for i in range(2)]
            dx7 = [P.dsem("x7") for i in range(2)]
            ot = [P.sbuf(st, f"ot{i}", [128, D], F32) for i in range(2)]
            dot = [P.dsem("ot") for i in range(2)]
            YB = self.YB
            tiles = []
            for u, (S, n_own, smp) in enumerate(self.units):
                for tl in range(n_own // 128):
                    tiles.append((u, tl))

            def cload(T):
                if T >= len(tiles):
                    return
                s = T % 2
                self.dma("sp", x1[s], x1[s][:, :], self.X1, self.X1[T * 128:(T + 1) * 128, :], dx7[s])
                for k in range(2):
                    P.op("pool", (lambda ob, ia: lambda e: e.indirect_dma_start(out=ob[:, :], out_offset=None, in_=YB[:, :], in_offset=bass.IndirectOffsetOnAxis(ap=ia, axis=0)))(yg[s][k], dst[:, T, k:k + 1]),
                         reads=[YB, dst], writes=[yg[s][k]], dsem=dyg[s][k])

            G2s = [self.mod_tile(st, "G2", u, 5, ds) for u in range(len(self.units))]
            cload(0)
            for T, (u, tl) in enumerate(tiles):
                s = T % 2
                G2 = G2s[u]
                self.ts(yg[s][0], yg[s][0][:, :], yg[s][0], yg[s][0][:, :], ROUT[:, T, 4:5], ALU.mult, reads=[ROUT])
                self.stt(yg[s][0], yg[s][0][:, :], yg[s][1], yg[s][1][:, :], ROUT[:, T, 5:6], yg[s][0], yg[s][0][:, :], ALU.mult, ALU.add, reads=[ROUT])
                self.tt(yg[s][0], yg[s][0][:, :], yg[s][0], yg[s][0][:, :], G2, G2[:, :], ALU.mult)
                cload(T + 1)
                self.tt(ot[s], ot[s][:, :], yg[s][0], yg[s][0][:, :], x1[s], x1[s][:, :], ALU.add)
                self.dma("sp", self.yu[u], self.yu[u][tl * 128:(tl + 1) * 128, :], ot[s], ot[s][:, :], dot[s])
            P.barrier()

    def build(self):
        self.declare()
        self.setup()
        import os
        stage = int(os.environ.get("KSTAGE", "9"))
        for u in range(len(self.units)):
            self.pass1(u)
        self.wst.close()
        tb = 0
        if stage >= 2:
            for u, (S, n_own, smp) in enumerate(self.units):
                with ExitStack() as so:
                    OT = self.pass2(u, so)
                    self.pass3(u, OT, tb)
                tb += n_own // 128
        if stage >= 3:
            self.moe()
        self.P.barrier()
        stats = self.P.emit()
        self.st.close()
        return self.nc, stats


_CACHE = {}


def _run(x_prompt, x_sample, c_prompt, c_sample, W, n_cores, nq, dbg=False):
    Bp, Sp, _ = x_prompt.shape
    Bs, Ss, _ = x_sample.shape
    assert Bs * nq == n_cores and Bp % n_cores == 0
    ppc = Bp // n_cores
    own = Ss // nq
    units = [(Sp, Sp, False)] * ppc + [(Ss, own, True)]
    key = (tuple(units), dbg)
    if key not in _CACHE:
        k = K(units, dbg=dbg)
        nc, stats = k.build()
        _CACHE[key] = (nc, stats)
    nc, stats = _CACHE[key]
    base = dict(W)
    in_maps = []
    for c in range(n_cores):
        m = dict(base)
        cs = []
        for i in range(ppc):
            b = c * ppc + i
            m[f"x{i}"] = np.ascontiguousarray(x_prompt[b])
            m[f"pos{i}"] = np.ascontiguousarray(np.arange(Sp, dtype=np.float32).reshape(Sp // 128, 128).T)
            cs.append(c_prompt[b])
        sq, qq = c // nq, c % nq
        order = (qq * own + np.arange(Ss)) % Ss
        m[f"x{ppc}"] = np.ascontiguousarray(x_sample[sq][order])
        m[f"pos{ppc}"] = np.ascontiguousarray(order.astype(np.float32).reshape(Ss // 128, 128).T)
        cs.append(c_sample[sq])
        m["flags"] = np.array([[1.0 if qq > 0 else 0.0, 1.0 if qq < nq - 1 else 0.0]], np.float32)
        cu = np.stack(cs, 0)
        m["cT"] = np.ascontiguousarray(cu.reshape(len(cs), 8, 128).transpose(2, 1, 0))
        in_maps.append(m)
    res = run_bass_kernel_spmd(nc, in_maps, core_ids=list(range(n_cores)))
    if dbg:
        _CACHE["last"] = res.results
    yp = np.empty((Bp, Sp, D), np.float32)
    ys = np.empty((Bs, Ss, D), np.float32)
    for c in range(n_cores):
        r = res.results[c]
        for i in range(ppc):
            yp[c * ppc + i] = r[f"y{i}"]
        sq, qq = c // nq, c % nq
        ys[sq, qq * own:(qq + 1) * own] = r[f"y{ppc}"]
    return yp, ys


def _weights(w_ada, b_ada, g_norm1, w_in, g_q, g_k, lambda_q1, lambda_k1, lambda_q2, lambda_k2, g_subln, w_dw, b_dw,
             g_conv_ln, b_conv_ln, w_out, g_norm2, w_router_group, b_router_group, w_router_expert, b_router_expert,
             w_gate_up, w_down):
    f = lambda a: np.ascontiguousarray(np.asarray(a, dtype=np.float32))
    return dict(
        w_ada=f(w_ada[0]), b_ada=f(b_ada[0]).reshape(1, -1), g_norm1=f(g_norm1[0]).reshape(1, -1), w_in=f(w_in[0]),
        g_q=f(g_q[0]).reshape(1, -1), g_k=f(g_k[0]).reshape(1, -1),
        lambda_q1=f(lambda_q1[0]).reshape(1, -1), lambda_k1=f(lambda_k1[0]).reshape(1, -1),
        lambda_q2=f(lambda_q2[0]).reshape(1, -1), lambda_k2=f(lambda_k2[0]).reshape(1, -1),
        g_subln=f(g_subln[0]).reshape(-1, 1), w_dw=f(w_dw[0]), b_dw=f(b_dw[0]).reshape(-1, 1),
        g_conv_ln=f(g_conv_ln[0]).reshape(-1, 1), b_conv_ln=f(b_conv_ln[0]).reshape(-1, 1), w_out=f(w_out[0]),
        g_norm2=f(g_norm2[0]).reshape(1, -1), w_rg=f(w_router_group[0]), b_rg=f(b_router_group[0]).reshape(1, -1),
        w_re=f(w_router_expert[0]), b_re=f(b_router_expert[0]).reshape(1, -1), w_gu=f(w_gate_up[0]), w_dn=f(w_down[0]))


def kernel(x_prompt, x_sample, c_prompt, c_sample, **weights):
    W = _weights(**weights)
    f = lambda a: np.asarray(a, dtype=np.float32)
    yp, ys = _run(f(x_prompt), f(x_sample), f(c_prompt), f(c_sample), W, n_cores=8, nq=4)
    return yp, ys
```

```python
import numpy as np
from contextlib import ExitStack
import concourse.bass as bass
import concourse.mybir as mybir
from concourse.bass_utils import run_bass_kernel_spmd

F32 = mybir.dt.float32
BF16 = mybir.dt.bfloat16
I32 = mybir.dt.int32
AF = mybir.ActivationFunctionType
ALU = mybir.AluOpType
AX = mybir.AxisListType

EPOCH = 24000


class Buf:
    def __init__(self, t, name):
        self.t = t
        self.name = name
        self.last_writer = None
        self.readers = {}

    def __getitem__(self, k):
        return self.t[k]


class DSem:
    def __init__(self, sem):
        self.sem = sem
        self.count = 0
        self.last_op = None
        self.bg = False


class DPool:
    def __init__(self, sems, sw_sems):
        self.sems = sems
        self.sw_sems = sw_sems
        self.i = 0

    def next(self, eng):
        self.i += 1
        lst = self.sw_sems if eng == "pool" else self.sems
        return lst[self.i % len(lst)]


class Op:
    __slots__ = ("eng", "fn", "reads", "writes", "dsem", "dval", "deps", "signal", "sidx", "idx", "extra")

    def __init__(self, eng, fn, reads, writes, dsem):
        self.eng = eng
        self.fn = fn
        self.reads = reads
        self.writes = writes
        self.dsem = dsem
        self.dval = 0
        self.deps = []
        self.signal = False
        self.sidx = 0
        self.extra = None


ENGS = ("pe", "act", "dve", "pool", "sp")


class Prog:
    def __init__(self, nc, stack):
        self.nc = nc
        self.stack = stack
        self.ops = []
        self.bufs = []
        self.dsems = []
        self.last = {e: None for e in ENGS}
        self.nsem = 0

    def new_sem(self, name):
        self.nsem += 1
        return self.stack.enter_context(self.nc.semaphore(f"{name}_{self.nsem}"))

    def dsem(self, name="d"):
        d = DSem(self.new_sem(name))
        self.dsems.append(d)
        return d

    def dpool(self, name="dp", n=6):
        return DPool([self.dsem(name) for _ in range(n)], [self.dsem(name + "sw") for _ in range(n)])

    def buf(self, t, name):
        b = Buf(t, name)
        self.bufs.append(b)
        return b

    def sbuf(self, stack, name, shape, dt):
        self.nsem += 1
        name = f"{name}_{self.nsem}"
        t = stack.enter_context(self.nc.sbuf_tensor(name, list(shape), dt))
        return self.buf(t, name)

    def psum(self, stack, name, shape, dt):
        self.nsem += 1
        name = f"{name}_{self.nsem}"
        t = stack.enter_context(self.nc.psum_tensor(name, list(shape), dt))
        return self.buf(t, name)

    def dram(self, name, shape, dt, kind="Internal"):
        t = self.nc.dram_tensor(name, list(shape), dt, kind=kind)
        return self.buf(t.ap(), name)

    def op(self, eng, fn, reads=(), writes=(), dsem=None):
        if isinstance(dsem, DPool):
            dsem = dsem.next(eng)
        o = Op(eng, fn, list(reads), list(writes), dsem)
        o.idx = len(self.ops)
        deps = {}
        if dsem is not None:
            if dsem.last_op is not None:
                deps[dsem.last_op.idx] = (dsem.last_op, "sem")
            dsem.last_op = o
            dsem.count += 16
            o.dval = dsem.count
        for b in o.reads:
            w = b.last_writer
            if w is not None:
                deps[w.idx] = (w, "raw")
        for b in o.writes:
            w = b.last_writer
            if w is not None and w.idx not in deps:
                deps[w.idx] = (w, "waw")
            for r in b.readers.values():
                if r.idx not in deps:
                    deps[r.idx] = (r, "war")
        for (d, kind) in deps.values():
            if d is o:
                continue
            if d.dsem is None and d.eng == eng and dsem is None:
                if eng == "pe":
                    continue
            o.deps.append((d, d.dval if d.dsem is not None else 0))
            if d.dsem is None:
                d.signal = True
        for b in o.writes:
            b.last_writer = o
            b.readers = {}
        for b in o.reads:
            key = ("d", id(dsem)) if dsem is not None else eng
            b.readers[key] = o
        self.ops.append(o)
        if dsem is None:
            self.last[eng] = o
        return o

    def barrier(self, include_bg=False):
        lasts = [o for o in self.last.values() if o is not None]
        dlast = []
        for d in self.dsems:
            if d.count > 0 and (include_bg or not d.bg):
                dlast.append(d)
        for e in ENGS:
            o = Op(e, None, [], [], None)
            o.idx = len(self.ops)
            for l in lasts:
                if l.eng != e:
                    o.deps.append((l, 0))
                    l.signal = True
            o.extra = [(d, d.count) for d in dlast]
            self.ops.append(o)
        for b in self.bufs:
            b.last_writer = None
            b.readers = {}

    def emit(self):
        nc = self.nc
        cnt = {e: 0 for e in ENGS}
        for o in self.ops:
            if o.signal:
                cnt[o.eng] += 1
                o.sidx = cnt[o.eng]
        esems = {}
        for e in ENGS:
            n = cnt[e] // EPOCH + 1
            esems[e] = [self.new_sem(f"e_{e}") for _ in range(n)]

        def semval(dv):
            d, v = dv
            if d.dsem is not None:
                return d.dsem.sem, v
            ep = (d.sidx - 1) // EPOCH
            return esems[d.eng][ep], d.sidx - ep * EPOCH

        per = {e: [o for o in self.ops if o.eng == e] for e in ENGS}
        stats = {"waits": 0, "ops": len(self.ops)}

        def run(engname, eng):
            known = {}
            for o in per[engname]:
                ws = {}
                for d in o.deps:
                    s, v = semval(d)
                    k = id(s)
                    if known.get(k, 0) >= v:
                        continue
                    if k not in ws or ws[k][1] < v:
                        ws[k] = (s, v)
                if o.extra:
                    for (d, v) in o.extra:
                        k = id(d.sem)
                        if known.get(k, 0) >= v:
                            continue
                        if k not in ws or ws[k][1] < v:
                            ws[k] = (d.sem, v)
                for k, (s, v) in ws.items():
                    eng.wait_ge(s, v)
                    known[k] = v
                    stats["waits"] += 1
                if o.fn is None:
                    continue
                ins = o.fn(eng)
                if o.dsem is not None:
                    ins.then_inc(o.dsem.sem, 16)
                elif o.signal:
                    ep = (o.sidx - 1) // EPOCH
                    ins.then_inc(esems[engname][ep], 1)

        with nc.Block() as block:
            @block.sync
            def _(e):
                run("sp", e)

            @block.tensor
            def _(e):
                run("pe", e)

            @block.scalar
            def _(e):
                run("act", e)

            @block.vector
            def _(e):
                run("dve", e)

            @block.gpsimd
            def _(e):
                run("pool", e)
        return stats


D = 1024
NH = 4
HD = 64
VD = 128
INW = 2560
NE = 32
FF = 512
CK = 31
CP = 15
EPS = 1e-6
ROPE_THETA = 500000.0
INV_FREQ = [float(ROPE_THETA ** (-(2 * i) / 16.0)) for i in range(8)]
LAM_INIT = 0.2
TWO_PI = 6.283185307179586
BIG = 4096.0


class K:
    def __init__(self, units, dbg=False):
        self.units = units
        self.dbg = dbg
        self.nc = nc = bass.Bass("TRN2", target_bir_lowering=False)
        self.st = ExitStack()
        self.P = Prog(nc, self.st)
        self.n_tok = sum(u[1] for u in units)
        self.n_tiles = self.n_tok // 128
        self.n_rows = 2 * self.n_tok + NE * 128
        self.n_blk = self.n_rows // 128
        self._uid = 0

    def uid(self, s):
        self._uid += 1
        return f"{s}{self._uid}"

    def cds(self, name, i=0):
        if not hasattr(self, "_cds"):
            self._cds = {}
        k = (name, i)
        if k not in self._cds:
            self._cds[k] = self.P.dsem(name)
        return self._cds[k]

    def ein(self, name, shape, dt):
        return self.P.buf(self.nc.dram_tensor(name, list(shape), dt, kind="ExternalInput").ap(), name)

    def eout(self, name, shape, dt):
        return self.P.buf(self.nc.dram_tensor(name, list(shape), dt, kind="ExternalOutput").ap(), name)

    def dma(self, q, out_b, out_ap, in_b, in_ap, ds, **kw):
        return self.P.op(q, lambda e: e.dma_start(out=out_ap, in_=in_ap, **kw), reads=[in_b], writes=[out_b], dsem=ds)

    def mm(self, ob, oap, lb, lap, rb, rap, start, stop):
        return self.P.op("pe", lambda e: e.matmul(oap, lap, rap, start=start, stop=stop), reads=[lb, rb], writes=[ob])

    def tp(self, ob, oap, ib, iap):
        ident = self.ident
        return self.P.op("pe", lambda e: e.transpose(oap, iap, ident[:, :]), reads=[ib, ident], writes=[ob])

    def act(self, ob, oap, ib, iap, func, reads=(), **kw):
        return self.P.op("act", lambda e: e.activation(out=oap, in_=iap, func=func, **kw), reads=[ib] + list(reads), writes=[ob])

    def tt(self, ob, oap, ab, aap, bb, bap, op, eng="dve"):
        return self.P.op(eng, lambda e: e.tensor_tensor(out=oap, in0=aap, in1=bap, op=op), reads=[ab, bb], writes=[ob])

    def ts(self, ob, oap, ib, iap, s1, op0, s2=None, op1=None, reads=(), eng="dve"):
        if op1 is None:
            return self.P.op(eng, lambda e: e.tensor_scalar(out=oap, in0=iap, scalar1=s1, scalar2=None, op0=op0), reads=[ib] + list(reads), writes=[ob])
        return self.P.op(eng, lambda e: e.tensor_scalar(out=oap, in0=iap, scalar1=s1, scalar2=s2, op0=op0, op1=op1), reads=[ib] + list(reads), writes=[ob])

    def stt(self, ob, oap, ab, aap, sc, bb, bap, op0, op1, reads=()):
        return self.P.op("dve", lambda e: e.scalar_tensor_tensor(out=oap, in0=aap, scalar=sc, in1=bap, op0=op0, op1=op1), reads=[ab, bb] + list(reads), writes=[ob])

    def cp(self, ob, oap, ib, iap, eng="dve"):
        if eng == "act":
            return self.P.op("act", lambda e: e.copy(out=oap, in_=iap), reads=[ib], writes=[ob])
        return self.P.op(eng, lambda e: e.tensor_copy(out=oap, in_=iap), reads=[ib], writes=[ob])

    def red(self, ob, oap, ib, iap, op=None):
        op = op or ALU.add
        return self.P.op("dve", lambda e: e.tensor_reduce(out=oap, in_=iap, axis=AX.X, op=op), reads=[ib], writes=[ob])

    def recip(self, ob, oap, ib, iap):
        return self.P.op("dve", lambda e: e.reciprocal(out=oap, in_=iap), reads=[ib], writes=[ob])

    def memset(self, ob, oap, val, eng="dve"):
        return self.P.op(eng, lambda e: e.memset(oap, val), writes=[ob])

    def rsqrt(self, ob, oap, ib, iap, tmpb, tmpap, scale=1.0, bias=0.0):
        if bias != 0.0:
            bt = self.epsb
            self.act(tmpb, tmpap, ib, iap, AF.Ln, reads=[bt], scale=scale, bias=bt[0:oap.shape[0], 0:1])
        else:
            self.act(tmpb, tmpap, ib, iap, AF.Ln, scale=scale)
        self.act(ob, oap, tmpb, tmpap, AF.Exp, scale=-0.5)

    def bload(self, st, name, src_b, src_ap, n, ds, dt=F32):
        t = self.P.sbuf(st, name, [128, n], dt)
        self.dma("sp", t, t[:, :], src_b, src_ap.partition_broadcast(128).rearrange("p o f -> p (o f)"), ds)
        return t

    def declare(self):
        nu = len(self.units)
        self.xu = [self.ein(f"x{u}", [S, D], F32) for u, (S, n, smp) in enumerate(self.units)]
        self.posu = [self.ein(f"pos{u}", [128, S // 128], F32) for u, (S, n, smp) in enumerate(self.units)]
        self.flg = self.ein("flags", [1, 2], F32)
        self.cT = self.ein("cT", [128, 8, nu], F32)
        w = {}
        for name, shape in [("w_ada", [D, 6 * D]), ("b_ada", [1, 6 * D]), ("g_norm1", [1, D]), ("w_in", [D, INW]),
                            ("g_q", [1, HD]), ("g_k", [1, HD]), ("lambda_q1", [1, HD]), ("lambda_k1", [1, HD]),
                            ("lambda_q2", [1, HD]), ("lambda_k2", [1, HD]), ("g_subln", [VD, 1]), ("w_dw", [CK, 512]),
                            ("b_dw", [512, 1]), ("g_conv_ln", [512, 1]), ("b_conv_ln", [512, 1]), ("w_out", [D, D]),
                            ("g_norm2", [1, D]), ("w_rg", [D, 4]), ("b_rg", [1, 4]), ("w_re", [D, NE]), ("b_re", [1, NE]),
                            ("w_gu", [NE, D, 2 * FF]), ("w_dn", [NE, FF, D])]:
            w[name] = self.ein(name, shape, F32)
        self.w = w
        self.yu = [self.eout(f"y{u}", [n, D], F32) for u, (S, n, smp) in enumerate(self.units)]
        P = self.P
        if self.dbg:
            _d = P.dram
            P.dram = lambda name, shape, dt, kind="Internal": _d(name, shape, dt, kind="ExternalOutput")
        self.MOD = P.dram("MOD", [nu, 6 * D], F32)
        self.KT = [P.dram(f"KT{u}", [NH, 128, S], BF16) for u, (S, n, smp) in enumerate(self.units)]
        self.VV = [P.dram(f"VV{u}", [128, S // 128, 512], BF16) for u, (S, n, smp) in enumerate(self.units)]
        self.QT = [P.dram(f"QT{u}", [NH, 128, n], BF16) for u, (S, n, smp) in enumerate(self.units)]
        self.ZT = [P.dram(f"ZT{u}", [4, 128, n + 2 * CP + 2], BF16) for u, (S, n, smp) in enumerate(self.units)]
        self.X1 = P.dram("X1", [self.n_tok, D], F32)
        self.H2 = P.dram("H2", [self.n_tok, D], BF16)
        self.XS = P.dram("XS", [self.n_rows, D], BF16)
        self.YB = P.dram("YB", [self.n_rows, D], F32)
        self.WGU = P.dram("WGU", [NE * 128, 8 * 2 * FF], BF16)
        self.WDN = P.dram("WDN", [NE * 128, 4 * D], BF16)
        if self.dbg:
            self.DROUT = P.dram("DROUT", [128, self.n_tiles * 6], F32)
            self.DMISC = P.dram("DMISC", [128, NE + self.n_blk + 1 + self.n_tiles * 2], F32)

    def setup(self):
        P, nc, st = self.P, self.nc, self.st
        nu = len(self.units)
        w = self.w
        ds_w = self.gp = P.dpool("gp", 8)
        cst = self.cst = ExitStack()
        self.st.enter_context(cst)
        ident = self.ident = P.sbuf(cst, "ident", [128, 128], BF16)
        ones = self.ones = P.sbuf(cst, "ones", [128, 128], BF16)
        self.utri = P.sbuf(cst, "utri", [128, 128], BF16)
        self.iota32 = P.sbuf(cst, "iota32", [128, NE], F32)
        self.ROUT = P.sbuf(cst, "ROUT", [128, self.n_tiles, 6], F32)
        self.Rcnt = P.sbuf(cst, "Rcnt", [128, NE], F32)
        self.neglam = P.sbuf(cst, "neglam", [128, 1], F32)
        self.gsub = P.sbuf(cst, "gsub", [128, 1], F32)
        self.flags = P.sbuf(cst, "flagsb", [128, 2], F32)
        self.epsb = P.sbuf(cst, "epsb", [128, 1], F32)
        self.zero = P.sbuf(cst, "zero", [128, 4 * D], BF16)
        self.wst = ExitStack()
        self.win = P.sbuf(self.wst, "win", [128, 8, INW], BF16)
        dsc = self.gp
        with ExitStack() as t:
            ci = P.sbuf(t, "ci", [128, 128], I32)
            ri = P.sbuf(t, "ri", [128, 128], I32)
            cf = P.sbuf(t, "cf", [128, 128], F32)
            rf = P.sbuf(t, "rf", [128, 128], F32)
            P.op("pool", lambda e: e.iota(ci[:, :], pattern=[[1, 128]], base=0, channel_multiplier=0), writes=[ci])
            P.op("pool", lambda e: e.iota(ri[:, :], pattern=[[0, 128]], base=0, channel_multiplier=1), writes=[ri])
            self.cp(cf, cf[:, :], ci, ci[:, :])
            self.cp(rf, rf[:, :], ri, ri[:, :])
            self.cp(self.iota32, self.iota32[:, :], cf, cf[:, 0:NE])
            self.tt(ident, ident[:, :], rf, rf[:, :], cf, cf[:, :], ALU.is_equal)
            self.tt(self.utri, self.utri[:, :], rf, rf[:, :], cf, cf[:, :], ALU.is_lt)
            self.memset(ones, ones[:, :], 1.0)
            self.memset(self.epsb, self.epsb[:, :], EPS)
            self.memset(self.Rcnt, self.Rcnt[:, :], 0.0)
            self.dma("sp", self.flags, self.flags[:, :], self.flg, self.flg[0:1, :].partition_broadcast(128).rearrange("p o f -> p (o f)"), dsc)
            lt = [self.bload(t, f"lam{i}", w[n], w[n][0:1, :], HD, dsc) for i, n in enumerate(["lambda_q1", "lambda_k1", "lambda_q2", "lambda_k2"])]
            pr = P.sbuf(t, "lampr", [128, HD], F32)
            s1 = P.sbuf(t, "lams1", [128, 2], F32)
            self.tt(pr, pr[:, :], lt[0], lt[0][:, :], lt[1], lt[1][:, :], ALU.mult)
            self.red(s1, s1[:, 0:1], pr, pr[:, :])
            self.tt(pr, pr[:, :], lt[2], lt[2][:, :], lt[3], lt[3][:, :], ALU.mult)
            self.red(s1, s1[:, 1:2], pr, pr[:, :])
            e1 = P.sbuf(t, "lame1", [128, 2], F32)
            self.act(e1, e1[:, :], s1, s1[:, :], AF.Exp)
            self.tt(self.neglam, self.neglam[:, :], e1, e1[:, 1:2], e1, e1[:, 0:1], ALU.subtract)
            self.ts(self.neglam, self.neglam[:, :], self.neglam, self.neglam[:, :], -LAM_INIT, ALU.add)
            gs = P.sbuf(t, "gs", [128, 1], F32)
            self.dma("sp", gs, gs[:, :], w["g_subln"], w["g_subln"][:, :], dsc)
            self.ts(self.gsub, self.gsub[:, :], gs, gs[:, :], 1.0 - LAM_INIT, ALU.mult)
            dsw = ds_w
            wv = w["w_in"][:, :].rearrange("(kc p) n -> p kc n", p=128)
            for kc in range(8):
                for cb in range(INW // 512):
                    self.dma("pool", self.win, self.win[:, kc, cb * 512:(cb + 1) * 512], w["w_in"], wv[:, kc, cb * 512:(cb + 1) * 512], dsw)
            bgp = DPool([], [P.dsem("bgw") for _ in range(6)])
            for d_ in bgp.sw_sems:
                d_.bg = True
            wgu_v = self.WGU[:, :].rearrange("(e p) (kc n) -> e p kc n", p=128, kc=8)
            wdn_v = self.WDN[:, :].rearrange("(e p) (kc n) -> e p kc n", p=128, kc=4)
            for e in range(NE):
                for kc in range(8):
                    self.dma("pool", self.WGU, wgu_v[e, :, kc, :], w["w_gu"], w["w_gu"][e, kc * 128:(kc + 1) * 128, :], bgp)
                for kc in range(4):
                    self.dma("pool", self.WDN, wdn_v[e, :, kc, :], w["w_dn"], w["w_dn"][e, kc * 128:(kc + 1) * 128, :], bgp)
            zero = self.zero
            self.memset(zero, zero[:, :], 0.0)
            xsv = self.XS[:, :].rearrange("(a p r) d -> a p (r d)", p=128, r=4)
            for a in range(self.n_rows // 512):
                self.dma("pool", self.XS, xsv[a], zero, zero[:, :], bgp)
            ct = P.sbuf(t, "ct", [128, 8, nu], F32)
            ce = P.sbuf(t, "ce", [128, 8, nu], F32)
            sc = P.sbuf(t, "sc", [128, 8, nu], F32)
            self.dma("sp", ct, ct[:, :, :], self.cT, self.cT[:, :, :], dsc)
            self.act(ce, ce[:, :, :], ct, ct[:, :, :], AF.Exp, scale=-1.0)
            self.ts(ce, ce[:, :, :], ce, ce[:, :, :], 1.0, ALU.add)
            self.recip(ce, ce[:, :, :], ce, ce[:, :, :])
            self.tt(sc, sc[:, :, :], ct, ct[:, :, :], ce, ce[:, :, :], ALU.mult)
            bada = P.sbuf(t, "bada", [nu, 6 * D], F32)
            self.dma("sp", bada, bada[:, :], w["b_ada"], w["b_ada"][0:1, :].partition_broadcast(nu).rearrange("p o f -> p (o f)"), dsc)
            modsb = P.sbuf(t, "modsb", [nu, 6 * D], F32)
            wab = [P.sbuf(t, f"wab{i}", [128, 8, 512], F32) for i in range(2)]
            dsa = [P.dsem("wab") for i in range(2)]
            pm = [P.psum(t, f"pmod{i}", [nu, 512], F32) for i in range(2)]
            wav = w["w_ada"][:, :].rearrange("(kc p) n -> p kc n", p=128)
            for cb in range(12):
                b = cb % 2
                self.dma("sp", wab[b], wab[b][:, :, :], w["w_ada"], wav[:, :, cb * 512:(cb + 1) * 512], dsa[b])
                for kc in range(8):
                    self.mm(pm[b], pm[b][:, :], sc, sc[:, kc, :], wab[b], wab[b][:, kc, :], kc == 0, kc == 7)
                self.tt(modsb, modsb[:, cb * 512:(cb + 1) * 512], pm[b], pm[b][:, :], bada, bada[:, cb * 512:(cb + 1) * 512], ALU.add)
            self.dma("sp", self.MOD, self.MOD[:, :], modsb, modsb[:, :], dsc)
            P.barrier()

    def mod_tile(self, st, name, u, j, ds, gname=None):
        t = self.bload(st, self.uid(name), self.MOD, self.MOD[u:u + 1, j * D:(j + 1) * D], D, ds)
        if gname is not None:
            g = self.bload(st, self.uid(name + "g"), self.w[gname], self.w[gname][0:1, :], D, ds)
            self.stt(t, t[:, :], t, t[:, :], 1.0, g, g[:, :], ALU.add, ALU.mult)
        return t

    def rms_rstd(self, xb, xap, n, junkb, junkap, ssb, ssap, tmpb, tmpap, outb, outap):
        self.P.op("act", lambda e: e.activation(out=junkap, in_=xap, func=AF.Square, accum_out=ssap), reads=[xb], writes=[junkb, ssb])
        self.rsqrt(outb, outap, ssb, ssap, tmpb, tmpap, scale=1.0 / n, bias=EPS)

    def pass1(self, u):
        P = self.P
        S, n_own, smp = self.units[u]
        nt, not_ = S // 128, n_own // 128
        w = self.w
        with ExitStack() as st:
            ds = self.gp
            A1 = self.mod_tile(st, "A1", u, 1, ds, "g_norm1")
            B1 = self.mod_tile(st, "B1", u, 0, ds)
            gq = P.sbuf(st, "gq", [128, 8, HD], F32)
            gk = P.sbuf(st, "gk", [128, 8, HD], F32)
            for g in range(8):
                self.dma("sp", gq, gq[:, g, :], w["g_q"], w["g_q"][0:1, :].partition_broadcast(128).rearrange("p o f -> p (o f)"), ds)
                self.dma("sp", gk, gk[:, g, :], w["g_k"], w["g_k"][0:1, :].partition_broadcast(128).rearrange("p o f -> p (o f)"), ds)
            pos = P.sbuf(st, "pos", [128, nt], F32)
            self.dma("sp", pos, pos[:, :], self.posu[u], self.posu[u][:, :], ds)
            cosT = P.sbuf(st, "cosT", [128, nt, 8], F32)
            sinT = P.sbuf(st, "sinT", [128, nt, 8], F32)
            if True:
                t2 = st
                ang = P.sbuf(t2, "ang", [128, nt, 8], F32)
                ki = P.sbuf(t2, "ki", [128, nt, 8], I32)
                kf = P.sbuf(t2, "kf", [128, nt, 8], F32)
                fr = P.sbuf(t2, "fr", [128, nt, 8], F32)
                adj = P.sbuf(t2, "adj", [128, nt, 8], F32)
                for i in range(8):
                    self.ts(ang, ang[:, :, i], pos, pos[:, :], INV_FREQ[i] / TWO_PI, ALU.mult)
                for (tab, off) in ((sinT, 0.0), (cosT, 0.25)):
                    self.ts(fr, fr[:, :, :], ang, ang[:, :, :], off, ALU.add)
                    self.cp(ki, ki[:, :, :], fr, fr[:, :, :])
                    self.cp(kf, kf[:, :, :], ki, ki[:, :, :])
                    self.tt(fr, fr[:, :, :], fr, fr[:, :, :], kf, kf[:, :, :], ALU.subtract)
                    self.ts(adj, adj[:, :, :], fr, fr[:, :, :], 0.5, ALU.is_gt)
                    self.tt(fr, fr[:, :, :], fr, fr[:, :, :], adj, adj[:, :, :], ALU.subtract)
                    self.ts(adj, adj[:, :, :], fr, fr[:, :, :], -0.5, ALU.is_lt)
                    self.tt(fr, fr[:, :, :], fr, fr[:, :, :], adj, adj[:, :, :], ALU.add)
                    self.act(tab, tab[:, :, :], fr, fr[:, :, :], AF.Sin, scale=TWO_PI)
            xt = [P.sbuf(st, f"xt{i}", [128, D], F32) for i in range(3)]
            dsx = [self.cds("xt", i) for i in range(3)]
            junk = P.sbuf(st, "junk", [128, D], BF16)
            sm = [P.sbuf(st, f"sm{i}", [128, 8], F32) for i in range(2)]
            xn = P.sbuf(st, "xn", [128, D], F32)
            hb = [P.sbuf(st, f"hb{i}", [128, D], BF16) for i in range(2)]
            hT = [P.sbuf(st, f"hT{i}", [128, 8, 128], BF16) for i in range(2)]
            pTa = P.psum(st, "pTa", [128, 8, 128], BF16)
            pTb = P.psum(st, "pTb", [128, 8, 128], BF16)
            NBK = 6
            pb = [P.psum(st, f"pb{i}", [128, 512], F32) for i in range(NBK)]
            sq = P.sbuf(st, "sq", [128, 8, HD], F32)
            ss8 = P.sbuf(st, "ss8", [128, 8], F32)
            ln8 = P.sbuf(st, "ln8", [128, 8], F32)
            rs8 = P.sbuf(st, "rs8", [128, 8], F32)
            qn = P.sbuf(st, "qn", [128, 8, HD], F32)
            xr = P.sbuf(st, "xr", [128, 8, 16], F32)
            r1 = P.sbuf(st, "r1", [128, 8, 8], F32)
            r2 = P.sbuf(st, "r2", [128, 8, 8], F32)
            kb_ = [P.sbuf(st, f"kb{i}", [128, 8, HD], BF16) for i in range(2)]
            qb_ = [P.sbuf(st, f"qb{i}", [128, 8, HD], BF16) for i in range(2)]
            G = 4
            qst = P.sbuf(st, "qst", [128, NH, G * 128], BF16)
            kst = P.sbuf(st, "kst", [128, NH, G * 128], BF16)
            zst = P.sbuf(st, "zst", [128, 4, G * 128], BF16)
            dq, dk, dz = self.cds("qst"), self.cds("kst"), self.cds("zst")
            vs = [P.sbuf(st, f"vs{i}", [128, 512], BF16) for i in range(2)]
            dv = [self.cds("vs", i) for i in range(2)]
            eg = P.sbuf(st, "eg", [128, 512], F32)
            zb = [P.sbuf(st, f"zb{i}", [128, 512], BF16) for i in range(2)]
            zh = P.sbuf(st, "zh", [128, 4, 16], BF16)
            dzh = self.cds("zh")
            nbank = [0]
            banks = {}

            def bank():
                nbank[0] += 1
                return pb[nbank[0] % NBK]

            def proj(hTt, cb):
                b = bank()
                for kc in range(8):
                    self.mm(b, b[:, :], hTt, hTt[:, kc, :], self.win, self.win[:, kc, cb * 512:(cb + 1) * 512], kc == 0, kc == 7)
                return b

            def qk_post(b, gt, t, out):
                bv = b[:, :].rearrange("p (g d) -> p g d", g=8)
                self.act(sq, sq[:, :, :], b, bv, AF.Square)
                self.red(ss8, ss8[:, :], sq, sq[:, :, :])
                self.rsqrt(rs8, rs8[:, :], ss8, ss8[:, :], ln8, ln8[:, :], scale=1.0 / HD, bias=EPS)
                self.tt(qn, qn[:, :, :], b, bv, rs8, rs8[:, :].unsqueeze(2).to_broadcast([128, 8, HD]), ALU.mult)
                self.tt(out, out[:, :, :], qn, qn[:, :, :], gt, gt[:, :, :], ALU.mult)
                self.tt(xr, xr[:, :, :], qn, qn[:, :, 0:16], gt, gt[:, :, 0:16], ALU.mult)
                cb_ = cosT[:, t, :].unsqueeze(1).to_broadcast([128, 8, 8])
                sb_ = sinT[:, t, :].unsqueeze(1).to_broadcast([128, 8, 8])
                self.tt(r1, r1[:, :, :], xr, xr[:, :, 0:8], cosT, cb_, ALU.mult)
                self.tt(r2, r2[:, :, :], xr, xr[:, :, 8:16], sinT, sb_, ALU.mult)
                self.tt(out, out[:, :, 0:8], r1, r1[:, :, :], r2, r2[:, :, :], ALU.subtract)
                self.tt(r1, r1[:, :, :], xr, xr[:, :, 8:16], cosT, cb_, ALU.mult)
                self.tt(r2, r2[:, :, :], xr, xr[:, :, 0:8], sinT, sb_, ALU.mult)
                self.tt(out, out[:, :, 8:16], r1, r1[:, :, :], r2, r2[:, :, :], ALU.add)

            def kind(t):
                own = t < not_
                halo_r = smp and t == not_
                halo_l = smp and t == nt - 1
                return own, halo_r, halo_l

            def loadx(t):
                if t < nt:
                    self.dma("sp", xt[t % 3], xt[t % 3][:, :], self.xu[u], self.xu[u][t * 128:(t + 1) * 128, :], dsx[t % 3])

            def stA(t):
                s = t % 2
                x_ = xt[t % 3]
                loadx(t + 2)
                self.rms_rstd(x_, x_[:, :], D, junk, junk[:, :], sm[s], sm[s][:, 0:1], sm[s], sm[s][:, 1:2], sm[s], sm[s][:, 2:3])
                self.stt(xn, xn[:, :], x_, x_[:, :], sm[s][:, 2:3], A1, A1[:, :], ALU.mult, ALU.mult, reads=[sm[s]])
                self.tt(hb[s], hb[s][:, :], xn, xn[:, :], B1, B1[:, :], ALU.add)
                for kc in range(8):
                    self.tp(pTa, pTa[:, kc, :], hb[s], hb[s][:, kc * 128:(kc + 1) * 128])
                self.cp(hT[s], hT[s][:, :, :], pTa, pTa[:, :, :], eng="act")

            def stP(t):
                s = t % 2
                own, halo_r, halo_l = kind(t)
                bk = {"k": proj(hT[s], 1)}
                if own:
                    bk["q"] = proj(hT[s], 0)
                bk["v"] = proj(hT[s], 2)
                if own or halo_r or halo_l:
                    bk["a"] = proj(hT[s], 3)
                    bk["g"] = proj(hT[s], 4)
                banks[t] = bk

            def stPost(t):
                s = t % 2
                bk = banks[t]
                qk_post(bk["k"], gk, t, kb_[s])
                if "q" in bk:
                    qk_post(bk["q"], gq, t, qb_[s])
                self.cp(vs[s], vs[s][:, :], bk["v"], bk["v"][:, :])
                self.dma("sp", self.VV[u], self.VV[u][:, t, :], vs[s], vs[s][:, :], dv[s])
                if "a" in bk:
                    ba, bg = bk["a"], bk["g"]
                    self.act(eg, eg[:, :], bg, bg[:, :], AF.Exp, scale=-1.0)
                    self.ts(eg, eg[:, :], eg, eg[:, :], 1.0, ALU.add)
                    self.recip(eg, eg[:, :], eg, eg[:, :])
                    self.tt(zb[s], zb[s][:, :], ba, ba[:, :], eg, eg[:, :], ALU.mult)

            def stBack(t):
                s = t % 2
                own, halo_r, halo_l = kind(t)
                bk = banks.pop(t)
                g0 = (t % G) * 128
                for h in range(NH):
                    self.tp(pTb, pTb[:, h, :], kb_[s], kb_[s][:, 2 * h:2 * h + 2, :].rearrange("p g d -> p (g d)"))
                if own:
                    for h in range(NH):
                        self.tp(pTb, pTb[:, 4 + h, :], qb_[s], qb_[s][:, 2 * h:2 * h + 2, :].rearrange("p g d -> p (g d)"))
                self.cp(kst, kst[:, :, g0:g0 + 128], pTb, pTb[:, 0:NH, :])
                if t % G == G - 1 or t == nt - 1:
                    t0 = (t // G) * G
                    nn = (t - t0 + 1) * 128
                    self.dma("sp", self.KT[u], self.KT[u][:, :, t0 * 128:t0 * 128 + nn].rearrange("h p n -> p h n"), kst, kst[:, :, 0:nn], dk)
                if own:
                    self.cp(qst, qst[:, :, g0:g0 + 128], pTb, pTb[:, 4:4 + NH, :])
                    if t % G == G - 1 or t == not_ - 1:
                        t0 = (t // G) * G
                        nn = (t - t0 + 1) * 128
                        self.dma("sp", self.QT[u], self.QT[u][:, :, t0 * 128:t0 * 128 + nn].rearrange("h p n -> p h n"), qst, qst[:, :, 0:nn], dq)
                if "a" in bk:
                    for c in range(4):
                        self.tp(pTb, pTb[:, c, :], zb[s], zb[s][:, c * 128:(c + 1) * 128])
                    if own:
                        self.cp(zst, zst[:, :, g0:g0 + 128], pTb, pTb[:, 0:4, :])
                        if t % G == G - 1 or t == not_ - 1:
                            t0 = (t // G) * G
                            nn = (t - t0 + 1) * 128
                            self.dma("sp", self.ZT[u], self.ZT[u][:, :, CP + 1 + t0 * 128:CP + 1 + t0 * 128 + nn].rearrange("c p n -> p c n"), zst, zst[:, :, 0:nn], dz)
                    elif halo_r:
                        self.ts(zh, zh[:, :, 0:CP], pTb, pTb[:, 0:4, 0:CP], self.flags[:, 1:2], ALU.mult, reads=[self.flags])
                        self.dma("sp", self.ZT[u], self.ZT[u][:, :, CP + 1 + n_own:CP + 1 + n_own + CP].rearrange("c p n -> p c n"), zh, zh[:, :, 0:CP], dzh)
                    else:
                        self.ts(zh, zh[:, :, 0:CP], pTb, pTb[:, 0:4, 128 - CP:128], self.flags[:, 0:1], ALU.mult, reads=[self.flags])
                        self.dma("sp", self.ZT[u], self.ZT[u][:, :, 1:1 + CP].rearrange("c p n -> p c n"), zh, zh[:, :, 0:CP], dzh)

            loadx(0)
            loadx(1)
            stA(0)
            for t in range(nt):
                stP(t)
                if t > 0:
                    stBack(t - 1)
                if t + 1 < nt:
                    stA(t + 1)
                stPost(t)
            stBack(nt - 1)
            P.barrier()

    def pass2(self, u, st_ot):
        P = self.P
        S, n_own, smp = self.units[u]
        nkc = S // 128
        QB = min(512, n_own)
        OT = P.sbuf(st_ot, self.uid("OT"), [128, NH, n_own], BF16)
        with ExitStack() as st:
            kt = P.sbuf(st, "kt", [128, S], BF16)
            vv = P.sbuf(st, "vv", [128, nkc, VD], BF16)
            qt = P.sbuf(st, "qt", [128, n_own], BF16)
            dl = self.gp
            psc = [[P.psum(st, f"psc{i}{c}", [128, QB], F32) for c in range(2)] for i in range(2)]
            pO = [P.psum(st, f"pO{c}", [128, QB], F32) for c in range(2)]
            pS = [P.psum(st, f"pS{c}", [128, QB], F32) for c in range(2)]
            et = [[P.sbuf(st, f"et{i}{c}", [128, QB], BF16) for c in range(2)] for i in range(3)]
            rr = [P.sbuf(st, f"rr{c}", [128, QB], F32) for c in range(2)]
            oo = [P.sbuf(st, f"oo{c}", [128, QB], F32) for c in range(2)]
            osq = P.sbuf(st, "osq", [128, QB], BF16)
            lnv = P.sbuf(st, "lnv", [128, QB], F32)
            pending = [None]
            for h in range(NH):
                self.dma("sp", kt, kt[:, :], self.KT[u], self.KT[u][h, :, :], dl)
                for k0 in range(0, nkc, 16):
                    k1 = min(nkc, k0 + 16)
                    self.dma("sp", vv, vv[:, k0:k1, :], self.VV[u], self.VV[u][:, k0:k1, h * VD:(h + 1) * VD], dl)
                self.dma("sp", qt, qt[:, :], self.QT[u], self.QT[u][h, :, :], dl)
                for q0 in range(0, n_own, QB):
                    def scores(kc, i):
                        for c in range(2):
                            self.mm(psc[i][c], psc[i][c][:, :], kt, kt[c * 64:(c + 1) * 64, kc * 128:(kc + 1) * 128],
                                    qt, qt[c * 64:(c + 1) * 64, q0:q0 + QB], True, True)
                    scores(0, 0)
                    kd = min(8, nkc - 1)
                    for kc in range(nkc):
                        i = kc % 2
                        if kc == kd and pending[0] is not None:
                            pending[0](psc[1 - i][0])
                            pending[0] = None
                        if kc + 1 < nkc:
                            scores(kc + 1, 1 - i)
                        j = kc % 3
                        for c in range(2):
                            self.act(et[j][c], et[j][c][:, :], psc[i][c], psc[i][c][:, :], AF.Exp, scale=HD ** -0.5)
                        for c in range(2):
                            self.mm(pO[c], pO[c][:, :], vv, vv[:, kc, :], et[j][c], et[j][c][:, :], kc == 0, kc == nkc - 1)
                            self.mm(pS[c], pS[c][:, :], self.ones, self.ones[:, :], et[j][c], et[j][c][:, :], kc == 0, kc == nkc - 1)
                    for c in range(2):
                        self.cp(rr[c], rr[c][:, :], pS[c], pS[c][:, :], eng="act" if c else "dve")
                        self.cp(oo[c], oo[c][:, :], pO[c], pO[c][:, :], eng="act" if c else "dve")
                    for c in range(2):
                        self.recip(rr[c], rr[c][:, :], rr[c], rr[c][:, :])
                        self.tt(oo[c], oo[c][:, :], oo[c], oo[c][:, :], rr[c], rr[c][:, :], ALU.mult)
                    self.stt(oo[0], oo[0][:, :], oo[1], oo[1][:, :], self.neglam[:, 0:1], oo[0], oo[0][:, :], ALU.mult, ALU.add, reads=[self.neglam])
                    self.tt(osq, osq[:, :], oo[0], oo[0][:, :], oo[0], oo[0][:, :], ALU.mult)

                    def fin2(b, h=h, q0=q0):
                        self.mm(b, b[:, :], self.ones, self.ones[:, :], osq, osq[:, :], True, True)
                        self.rsqrt(rr[1], rr[1][:, :], b, b[:, :], lnv, lnv[:, :], scale=1.0 / VD, bias=EPS)
                        self.stt(OT, OT[:, h, q0:q0 + QB], oo[0], oo[0][:, :], self.gsub[:, 0:1], rr[1], rr[1][:, :], ALU.mult, ALU.mult, reads=[self.gsub])
                    pending[0] = fin2
            if pending[0] is not None:
                pending[0](psc[0][0])
                pending[0] = None
            P.barrier()
        return OT

    def pass3(self, u, OT, tile_base):
        P = self.P
        S, n_own, smp = self.units[u]
        w = self.w
        GQ = min(512, n_own)
        ZW = n_own + 2 * CP + 2
        with ExitStack() as st:
            ds = self.gp
            G1 = self.mod_tile(st, "G1", u, 2, ds)
            A2 = self.mod_tile(st, "A2", u, 4, ds, "g_norm2")
            B2 = self.mod_tile(st, "B2", u, 3, ds)
            wout = P.sbuf(st, "wout", [128, 8, D], BF16)
            wov = w["w_out"][:, :].rearrange("(kc p) n -> p kc n", p=128)
            for kc in range(8):
                for cb in range(2):
                    self.dma("pool", wout, wout[:, kc, cb * 512:(cb + 1) * 512], w["w_out"], wov[:, kc, cb * 512:(cb + 1) * 512], ds)
            wr = P.sbuf(st, "wr", [128, 8, 36], BF16)
            self.dma("pool", wr, wr[:, :, 0:4], w["w_rg"], w["w_rg"][:, :].rearrange("(kc p) n -> p kc n", p=128), ds)
            self.dma("pool", wr, wr[:, :, 4:36], w["w_re"], w["w_re"][:, :].rearrange("(kc p) n -> p kc n", p=128), ds)
            brt = P.sbuf(st, "brt", [128, 36], F32)
            self.dma("sp", brt, brt[:, 0:4], w["b_rg"], w["b_rg"][0:1, :].partition_broadcast(128).rearrange("p o f -> p (o f)"), ds)
            self.dma("sp", brt, brt[:, 4:36], w["b_re"], w["b_re"][0:1, :].partition_broadcast(128).rearrange("p o f -> p (o f)"), ds)
            cpar = P.sbuf(st, "cpar", [128, 4, 3], F32)
            for c in range(4):
                for i, nme in enumerate(["b_dw", "g_conv_ln", "b_conv_ln"]):
                    self.dma("sp", cpar, cpar[:, c, i:i + 1], w[nme], w[nme][c * 128:(c + 1) * 128, :], ds)
            wdw = P.sbuf(st, "wdw", [128, 4, CK], F32)
            for c in range(4):
                self.dma("sp", wdw, wdw[:, c, :], w["w_dw"], w["w_dw"][:, c * 128:(c + 1) * 128].rearrange("j p -> p j"), ds, allow_slow_non_contiguous=True)
            diag = P.sbuf(st, "diag", [128, 4, CK, 128], BF16)
            for c in range(4):
                for j in range(CK):
                    self.ts(diag, diag[:, c, j, :], self.ident, self.ident[:, :], wdw[:, c, j:j + 1], ALU.mult, reads=[wdw])
            zt = P.sbuf(st, "zt", [128, 4, ZW], BF16)
            if smp:
                self.memset(zt, zt[:, :, 0:1], 0.0)
                self.memset(zt, zt[:, :, ZW - 1:ZW], 0.0)
                self.dma("sp", zt, zt[:, :, 1:ZW - 1], self.ZT[u], self.ZT[u][:, :, 1:ZW - 1].rearrange("c p n -> p c n"), ds)
            else:
                self.memset(zt, zt[:, :, 0:CP + 1], 0.0)
                self.memset(zt, zt[:, :, CP + 1 + n_own:ZW], 0.0)
                self.dma("sp", zt, zt[:, :, CP + 1:CP + 1 + n_own], self.ZT[u], self.ZT[u][:, :, CP + 1:CP + 1 + n_own].rearrange("c p n -> p c n"), ds)
            pc = [P.psum(st, f"pc{i}", [128, GQ], F32) for i in range(2)]
            pst = [P.psum(st, f"pst{i}", [128, GQ], F32) for i in range(2)]
            po = [P.psum(st, f"po{i}", [128, 512], F32) for i in range(2)]
            pT = P.psum(st, "pT3", [128, 8, 128], BF16)
            pr = P.psum(st, "pr", [128, 512], F32)
            zc = [P.sbuf(st, f"zc{c}", [128, GQ], F32) for c in range(4)]
            zcb = P.sbuf(st, "zcb", [128, GQ], BF16)
            zsq = P.sbuf(st, "zsq", [128, GQ], BF16)
            mean = P.sbuf(st, "mean", [128, GQ], F32)
            var = P.sbuf(st, "var", [128, GQ], F32)
            rstd = P.sbuf(st, "rstd", [128, GQ], F32)
            tmpf = P.sbuf(st, "tmpf", [128, GQ], F32)
            tmpe = P.sbuf(st, "tmpe", [128, GQ], F32)
            cvT = P.sbuf(st, "cvT", [128, 4, GQ], BF16)
            xt = [P.sbuf(st, f"x3{i}", [128, D], F32) for i in range(3)]
            dsx = [self.cds("xt", i) for i in range(3)]
            n_own_t = n_own // 128

            def loadx3(ti):
                if ti < n_own_t:
                    self.dma("sp", xt[ti % 3], xt[ti % 3][:, :], self.xu[u], self.xu[u][ti * 128:(ti + 1) * 128, :], dsx[ti % 3])
            loadx3(0)
            loadx3(1)
            x1 = [P.sbuf(st, f"x1{i}", [128, D], F32) for i in range(2)]
            dx1 = [self.cds("x1", i) for i in range(2)]
            h2f = P.sbuf(st, "h2f", [128, D], F32)
            h2 = [P.sbuf(st, f"h2{i}", [128, D], BF16) for i in range(2)]
            dh2 = [self.cds("h2", i) for i in range(2)]
            h2T = P.sbuf(st, "h2T", [128, 8, 128], BF16)
            junk = P.sbuf(st, "junk3", [128, D], BF16)
            sm = P.sbuf(st, "sm3", [128, 8], F32)
            R = {k: P.sbuf(st, "r_" + k, shp, F32) for k, shp in [("lg", [128, 36]), ("gm", [128, 4]), ("goh", [128, 4]), ("ex", [128, 4]),
                                                                   ("leg4", [128, 4, 8]), ("leg", [128, 8]), ("oh1", [128, 8]), ("oh2", [128, 8]),
                                                                   ("msk", [128, 8]), ("sc", [128, 8]), ("E1", [128, 4, 8]), ("E2", [128, 4, 8]),
                                                                   ("pos", [128, NE]), ("tmp32", [128, NE])]}
            mskb = P.sbuf(st, "mskb", [128, NE], BF16)
            ntl = GQ // 128
            R4 = {k: P.sbuf(st, "r4_" + k, shp, F32) for k, shp in [
                ("lg", [128, ntl, 36]), ("gm", [128, ntl]), ("goh", [128, ntl, 4]), ("dd", [128, ntl, 4]), ("ex", [128, ntl, 4]),
                ("se", [128, ntl]), ("pg", [128, ntl]), ("leg4", [128, ntl, 4, 8]), ("leg", [128, ntl, 8]), ("m1", [128, ntl]),
                ("m2", [128, ntl]), ("oh1", [128, ntl, 8]), ("oh2", [128, ntl, 8]), ("msk", [128, ntl, 8]), ("d21", [128, ntl]),
                ("ed", [128, ntl]), ("p1", [128, ntl]), ("p2", [128, ntl]), ("E1", [128, ntl, 4, 8]), ("E2", [128, ntl, 4, 8]),
                ("pos", [128, ntl, NE]), ("tmp", [128, ntl, NE])]}
            mskb4 = P.sbuf(st, "mskb4", [128, ntl, NE], BF16)

            for g0 in range(0, n_own, GQ):
                for c in range(4):
                    b = pc[c % 2]
                    for j in range(CK):
                        self.mm(b, b[:, :], diag, diag[:, c, j, :], zt, zt[:, c, g0 + j + 1:g0 + j + 1 + GQ], j == 0, j == CK - 1)
                    self.ts(zc[c], zc[c][:, :], b, b[:, :], cpar[:, c, 0:1], ALU.add, reads=[cpar])
                    self.cp(zcb, zcb[:, :], zc[c], zc[c][:, :])
                    self.tt(zsq, zsq[:, :], zc[c], zc[c][:, :], zc[c], zc[c][:, :], ALU.mult)
                    self.mm(pst[0], pst[0][:, :], self.ones, self.ones[:, :], zcb, zcb[:, :], c == 0, c == 3)
                    self.mm(pst[1], pst[1][:, :], self.ones, self.ones[:, :], zsq, zsq[:, :], c == 0, c == 3)
                self.ts(mean, mean[:, :], pst[0], pst[0][:, :], 1.0 / 512, ALU.mult)
                self.tt(tmpf, tmpf[:, :], mean, mean[:, :], mean, mean[:, :], ALU.mult)
                self.stt(var, var[:, :], pst[1], pst[1][:, :], 1.0 / 512, tmpf, tmpf[:, :], ALU.mult, ALU.subtract)
                self.ts(var, var[:, :], var, var[:, :], 0.0, ALU.max, EPS, ALU.add)
                self.rsqrt(rstd, rstd[:, :], var, var[:, :], tmpe, tmpe[:, :])
                for c in range(4):
                    self.tt(tmpf, tmpf[:, :], zc[c], zc[c][:, :], mean, mean[:, :], ALU.subtract)
                    self.tt(tmpf, tmpf[:, :], tmpf, tmpf[:, :], rstd, rstd[:, :], ALU.mult)
                    self.ts(tmpf, tmpf[:, :], tmpf, tmpf[:, :], cpar[:, c, 1:2], ALU.mult, cpar[:, c, 2:3], ALU.add, reads=[cpar])
                    self.act(cvT, cvT[:, c, :], tmpf, tmpf[:, :], AF.Silu)
                def stA(tl):
                    tk = g0 + tl * 128
                    loadx3(tk // 128 + 2)
                    for cb in range(2):
                        for kc in range(8):
                            if kc < 4:
                                lb, lap = OT, OT[:, kc, tk:tk + 128]
                            else:
                                lb, lap = cvT, cvT[:, kc - 4, tl * 128:(tl + 1) * 128]
                            self.mm(po[cb], po[cb][:, :], lb, lap, wout, wout[:, kc, cb * 512:(cb + 1) * 512], kc == 0, kc == 7)

                def stB(tl):
                    tk = g0 + tl * 128
                    T = tile_base + tk // 128
                    s = (tk // 128) % 2
                    for cb in range(2):
                        self.tt(x1[s], x1[s][:, cb * 512:(cb + 1) * 512], po[cb], po[cb][:, :], G1, G1[:, cb * 512:(cb + 1) * 512], ALU.mult)
                    x_ = xt[(tk // 128) % 3]
                    self.tt(x1[s], x1[s][:, :], x1[s], x1[s][:, :], x_, x_[:, :], ALU.add)
                    self.dma("sp", self.X1, self.X1[T * 128:(T + 1) * 128, :], x1[s], x1[s][:, :], dx1[s])
                    self.rms_rstd(x1[s], x1[s][:, :], D, junk, junk[:, :], sm, sm[:, 0:1], sm, sm[:, 1:2], sm, sm[:, 2:3])
                    self.stt(h2f, h2f[:, :], x1[s], x1[s][:, :], sm[:, 2:3], A2, A2[:, :], ALU.mult, ALU.mult, reads=[sm])
                    self.tt(h2[s], h2[s][:, :], h2f, h2f[:, :], B2, B2[:, :], ALU.add)
                    self.dma("sp", self.H2, self.H2[T * 128:(T + 1) * 128, :], h2[s], h2[s][:, :], dh2[s])
                    for kc in range(8):
                        self.tp(pT, pT[:, kc, :], h2[s], h2[s][:, kc * 128:(kc + 1) * 128])
                    self.cp(h2T, h2T[:, :, :], pT, pT[:, :, :], eng="act")
                    for kc in range(8):
                        self.mm(pr, pr[:, tl * 64:tl * 64 + 36], h2T, h2T[:, kc, :], wr, wr[:, kc, :], kc == 0, kc == 7)

                stA(0)
                for tl in range(ntl):
                    stB(tl)
                    if tl + 1 < ntl:
                        stA(tl + 1)
                self.route4(R4, pr, brt, mskb4, tile_base + g0 // 128, ntl)
            P.barrier()

    def route4(self, R, pr, brt, mskb, T0, n):
        g = lambda k: R[k]
        lg, gm, goh, dd, ex, se, pg, leg4, leg, m1, m2, oh1, oh2, msk, d21, ed, p1, p2, E1, E2, pos, tmp = [g(k) for k in
            ("lg", "gm", "goh", "dd", "ex", "se", "pg", "leg4", "leg", "m1", "m2", "oh1", "oh2", "msk", "d21", "ed", "p1", "p2", "E1", "E2", "pos", "tmp")]
        ROUT = self.ROUT
        prv = pr[:, 0:n * 64].rearrange("p (t c) -> p t c", c=64)
        self.tt(lg, lg[:, :, :], pr, prv[:, :, 0:36], brt, brt[:, :].unsqueeze(1).to_broadcast([128, n, 36]), ALU.add)
        self.red(gm, gm[:, :], lg, lg[:, :, 0:4], ALU.max)
        gmb = gm[:, :].unsqueeze(2).to_broadcast([128, n, 4])
        self.tt(goh, goh[:, :, :], lg, lg[:, :, 0:4], gm, gmb, ALU.is_ge)
        self.tt(dd, dd[:, :, :], lg, lg[:, :, 0:4], gm, gmb, ALU.subtract)
        self.act(ex, ex[:, :, :], dd, dd[:, :, :], AF.Exp)
        self.red(se, se[:, :], ex, ex[:, :, :])
        self.recip(pg, pg[:, :], se, se[:, :])
        self.tt(leg4, leg4[:, :, :, :], lg, lg[:, :, 4:36].rearrange("p t (g e) -> p t g e", g=4), goh, goh[:, :, :].unsqueeze(3).to_broadcast([128, n, 4, 8]), ALU.mult)
        self.red(leg, leg[:, :, :], leg4, leg4[:, :, :, :].rearrange("p t g e -> p t e g"))
        self.red(m1, m1[:, :], leg, leg[:, :, :], ALU.max)
        self.tt(oh1, oh1[:, :, :], leg, leg[:, :, :], m1, m1[:, :].unsqueeze(2).to_broadcast([128, n, 8]), ALU.is_ge)
        self.stt(msk, msk[:, :, :], oh1, oh1[:, :, :], -1.0e30, leg, leg[:, :, :], ALU.mult, ALU.add)
        self.red(m2, m2[:, :], msk, msk[:, :, :], ALU.max)
        self.tt(oh2, oh2[:, :, :], msk, msk[:, :, :], m2, m2[:, :].unsqueeze(2).to_broadcast([128, n, 8]), ALU.is_ge)
        self.tt(d21, d21[:, :], m2, m2[:, :], m1, m1[:, :], ALU.subtract)
        self.act(ed, ed[:, :], d21, d21[:, :], AF.Exp)
        self.ts(p1, p1[:, :], ed, ed[:, :], 1.0, ALU.add)
        self.recip(p1, p1[:, :], p1, p1[:, :])
        self.tt(p2, p2[:, :], ed, ed[:, :], p1, p1[:, :], ALU.mult)
        self.tt(ROUT, ROUT[:, T0:T0 + n, 4], p1, p1[:, :], pg, pg[:, :], ALU.mult)
        self.tt(ROUT, ROUT[:, T0:T0 + n, 5], p2, p2[:, :], pg, pg[:, :], ALU.mult)
        gb = goh[:, :, :].unsqueeze(3).to_broadcast([128, n, 4, 8])
        self.tt(E1, E1[:, :, :, :], oh1, oh1[:, :, :].unsqueeze(2).to_broadcast([128, n, 4, 8]), goh, gb, ALU.mult)
        self.tt(E2, E2[:, :, :, :], oh2, oh2[:, :, :].unsqueeze(2).to_broadcast([128, n, 4, 8]), goh, gb, ALU.mult)
        E1f = E1[:, :, :, :].rearrange("p t g e -> p t (g e)")
        E2f = E2[:, :, :, :].rearrange("p t g e -> p t (g e)")
        self.tt(mskb, mskb[:, :, :], E1, E1f, E2, E2f, ALU.add)
        mf = mskb[:, :, :].rearrange("p t e -> p (t e)")
        self.mm(pr, pr[:, 256:256 + n * NE], self.utri, self.utri[:, :], mskb, mf, True, True)
        self.mm(pr, pr[:, 384:384 + n * NE], self.ones, self.ones[:, :], mskb, mf, True, True)
        for t in range(n):
            self.tt(pos, pos[:, t, :], pr, pr[:, 256 + t * NE:256 + (t + 1) * NE], self.Rcnt, self.Rcnt[:, :], ALU.add)
            self.tt(self.Rcnt, self.Rcnt[:, :], self.Rcnt, self.Rcnt[:, :], pr, pr[:, 384 + t * NE:384 + (t + 1) * NE], ALU.add)
        iob = self.iota32[:, :].unsqueeze(1).to_broadcast([128, n, NE])
        for k, (Eb, Ef) in enumerate(((E1, E1f), (E2, E2f))):
            self.tt(tmp, tmp[:, :, :], Eb, Ef, self.iota32, iob, ALU.mult)
            self.red(ROUT, ROUT[:, T0:T0 + n, k], tmp, tmp[:, :, :])
            self.tt(tmp, tmp[:, :, :], Eb, Ef, pos, pos[:, :, :], ALU.mult)
            self.red(ROUT, ROUT[:, T0:T0 + n, 2 + k], tmp, tmp[:, :, :])

    def route(self, R, pr, brt, mskb, T):
        lg, gm, goh, ex, leg4, leg, oh1, oh2, msk, sc, E1, E2, pos, tmp32 = [R[k] for k in
            ("lg", "gm", "goh", "ex", "leg4", "leg", "oh1", "oh2", "msk", "sc", "E1", "E2", "pos", "tmp32")]
        ROUT = self.ROUT
        self.tt(lg, lg[:, :], pr, pr[:, 0:36], brt, brt[:, :], ALU.add)
        self.red(gm, gm[:, 0:1], lg, lg[:, 0:4], ALU.max)
        self.ts(goh, goh[:, :], lg, lg[:, 0:4], gm[:, 0:1], ALU.is_ge, reads=[gm])
        self.ts(gm, gm[:, 1:2], gm, gm[:, 0:1], -1.0, ALU.mult)
        self.P.op("act", lambda e: e.activation(out=ex[:, :], in_=lg[:, 0:4], func=AF.Exp, bias=gm[:, 1:2], accum_out=gm[:, 2:3]), reads=[lg, gm], writes=[ex, gm])
        self.recip(gm, gm[:, 3:4], gm, gm[:, 2:3])
        self.tt(leg4, leg4[:, :, :], lg, lg[:, 4:36].rearrange("p (g e) -> p g e", g=4), goh, goh[:, :].unsqueeze(2).to_broadcast([128, 4, 8]), ALU.mult)
        self.red(leg, leg[:, :], leg4, leg4[:, :, :].rearrange("p g e -> p e g"))
        self.red(sc, sc[:, 0:1], leg, leg[:, :], ALU.max)
        self.ts(oh1, oh1[:, :], leg, leg[:, :], sc[:, 0:1], ALU.is_ge, reads=[sc])
        self.stt(msk, msk[:, :], oh1, oh1[:, :], -1.0e30, leg, leg[:, :], ALU.mult, ALU.add)
        self.red(sc, sc[:, 1:2], msk, msk[:, :], ALU.max)
        self.ts(oh2, oh2[:, :], msk, msk[:, :], sc[:, 1:2], ALU.is_ge, reads=[sc])
        self.tt(sc, sc[:, 2:3], sc, sc[:, 1:2], sc, sc[:, 0:1], ALU.subtract)
        self.act(sc, sc[:, 3:4], sc, sc[:, 2:3], AF.Exp)
        self.ts(sc, sc[:, 4:5], sc, sc[:, 3:4], 1.0, ALU.add)
        self.recip(sc, sc[:, 5:6], sc, sc[:, 4:5])
        self.tt(sc, sc[:, 6:7], sc, sc[:, 3:4], sc, sc[:, 5:6], ALU.mult)
        self.tt(ROUT, ROUT[:, T, 4:5], sc, sc[:, 5:6], gm, gm[:, 3:4], ALU.mult)
        self.tt(ROUT, ROUT[:, T, 5:6], sc, sc[:, 6:7], gm, gm[:, 3:4], ALU.mult)
        gb = goh[:, :].unsqueeze(2).to_broadcast([128, 4, 8])
        self.tt(E1, E1[:, :, :], oh1, oh1[:, :].unsqueeze(1).to_broadcast([128, 4, 8]), goh, gb, ALU.mult)
        self.tt(E2, E2[:, :, :], oh2, oh2[:, :].unsqueeze(1).to_broadcast([128, 4, 8]), goh, gb, ALU.mult)
        E1f = E1[:, :, :].rearrange("p g e -> p (g e)")
        E2f = E2[:, :, :].rearrange("p g e -> p (g e)")
        self.tt(mskb, mskb[:, :], E1, E1f, E2, E2f, ALU.add)
        self.mm(pr, pr[:, 0:NE], self.utri, self.utri[:, :], mskb, mskb[:, :], True, True)
        self.tt(pos, pos[:, :], pr, pr[:, 0:NE], self.Rcnt, self.Rcnt[:, :], ALU.add)
        self.mm(pr, pr[:, 0:NE], self.ones, self.ones[:, :], mskb, mskb[:, :], True, True)
        self.tt(self.Rcnt, self.Rcnt[:, :], self.Rcnt, self.Rcnt[:, :], pr, pr[:, 0:NE], ALU.add)
        for k, (Eb, Ef) in enumerate(((E1, E1f), (E2, E2f))):
            self.tt(tmp32, tmp32[:, :], Eb, Ef, self.iota32, self.iota32[:, :], ALU.mult)
            self.red(ROUT, ROUT[:, T, k:k + 1], tmp32, tmp32[:, :])
            self.tt(tmp32, tmp32[:, :], Eb, Ef, pos, pos[:, :], ALU.mult)
            self.red(ROUT, ROUT[:, T, 2 + k:3 + k], tmp32, tmp32[:, :])

    def moe(self):
        P = self.P
        NT, NB = self.n_tiles, self.n_blk
        ROUT = self.ROUT
        P.barrier(include_bg=True)
        with ExitStack() as st:
            ds = self.gp
            MB = 256
            assert MB * 128 >= 2 * self.n_tok
            wid = P.sbuf(st, "wid", [128, NB], I32)
            dst = P.sbuf(st, "dst", [128, NT, 2], I32)
            sA = ExitStack()
            st_outer, st = st, sA
            thr = P.sbuf(st, "thr", [128, MB], F32)
            thi = P.sbuf(st, "thi", [128, MB], I32)
            P.op("pool", lambda e: e.iota(thi[:, :], pattern=[[128, MB]], base=0, channel_multiplier=0), writes=[thi])
            self.cp(thr, thr[:, :], thi, thi[:, :])
            big = P.sbuf(st, "bigc", [128, NE, MB], F32)
            nblk = P.sbuf(st, "nblk", [128, NE], F32)
            self.tt(big, big[:, :, :], self.Rcnt, self.Rcnt[:, :].unsqueeze(2).to_broadcast([128, NE, MB]),
                    thr, thr[:, :].unsqueeze(1).to_broadcast([128, NE, MB]), ALU.is_gt)
            self.red(nblk, nblk[:, :], big, big[:, :, :])
            L = P.sbuf(st, "Ltri", [128, NE, NE], F32)
            self.tt(L, L[:, :, :], self.iota32, self.iota32[:, :].unsqueeze(1).to_broadcast([128, NE, NE]),
                    self.iota32, self.iota32[:, :].unsqueeze(2).to_broadcast([128, NE, NE]), ALU.is_le)
            self.tt(L, L[:, :, :], L, L[:, :, :], nblk, nblk[:, :].unsqueeze(1).to_broadcast([128, NE, NE]), ALU.mult)
            pend = P.sbuf(st, "pend", [128, NE], F32)
            pstart = P.sbuf(st, "pstart", [128, NE], F32)
            self.red(pend, pend[:, :], L, L[:, :, :])
            self.tt(pstart, pstart[:, :], pend, pend[:, :], nblk, nblk[:, :], ALU.subtract)
            self.ts(pend, pend[:, :], pend, pend[:, :], 128.0, ALU.mult)
            self.ts(pstart, pstart[:, :], pstart, pstart[:, :], 128.0, ALU.mult)
            cmpb = P.sbuf(st, "cmpb", [128, NB, NE], F32)
            eid = P.sbuf(st, "eid", [128, NB + 2], F32)
            self.tt(cmpb, cmpb[:, :, :], pend, pend[:, :].unsqueeze(1).to_broadcast([128, NB, NE]),
                    thr, thr[:, 0:NB].unsqueeze(2).to_broadcast([128, NB, NE]), ALU.is_le)
            self.memset(eid, eid[:, 0:2], -1.0)
            self.red(eid, eid[:, 2:NB + 2], cmpb, cmpb[:, :, :])
            self.ts(eid, eid[:, 2:NB + 2], eid, eid[:, 2:NB + 2], float(NE - 1), ALU.min)
            same = P.sbuf(st, "same", [128, NB], F32)
            widf = P.sbuf(st, "widf", [128, NB], F32)
            pidx = P.sbuf(st, "pidx", [128, 1], F32)
            pidi = P.sbuf(st, "pidi", [128, 1], I32)
            P.op("pool", lambda e: e.iota(pidi[:, :], pattern=[[0, 1]], base=0, channel_multiplier=1), writes=[pidi])
            self.cp(pidx, pidx[:, :], pidi, pidi[:, :])
            self.tt(same, same[:, :], eid, eid[:, 2:NB + 2], eid, eid[:, 0:NB], ALU.is_equal)
            self.ts(widf, widf[:, :], eid, eid[:, 2:NB + 2], 128.0, ALU.mult, pidx[:, 0:1], ALU.add, reads=[pidx])
            self.stt(widf, widf[:, :], same, same[:, :], BIG, widf, widf[:, :], ALU.mult, ALU.add)
            self.cp(wid, wid[:, :], widf, widf[:, :])
            oh = P.sbuf(st, "ohd", [128, NT * 2, NE], F32)
            ev = ROUT[:, :, 0:2]
            self.tt(oh, oh[:, :, :].rearrange("p (t k) e -> p t k e", k=2), ROUT, ev.unsqueeze(3).to_broadcast([128, NT, 2, NE]),
                    self.iota32, self.iota32[:, :].unsqueeze(1).unsqueeze(1).to_broadcast([128, NT, 2, NE]), ALU.is_equal)
            self.tt(oh, oh[:, :, :], oh, oh[:, :, :], pstart, pstart[:, :].unsqueeze(1).to_broadcast([128, NT * 2, NE]), ALU.mult)
            dstf = P.sbuf(st, "dstf", [128, NT, 2], F32)
            self.red(dstf, dstf[:, :, :].rearrange("p t k -> p (t k)"), oh, oh[:, :, :])
            self.tt(dstf, dstf[:, :, :], dstf, dstf[:, :, :], ROUT, ROUT[:, :, 2:4], ALU.add)
            self.cp(dst, dst[:, :, :], dstf, dstf[:, :, :])
            if self.dbg:
                self.dma("sp", self.DROUT, self.DROUT[:, :], ROUT, ROUT[:, :, :].rearrange("p t k -> p (t k)"), ds)
                self.dma("sp", self.DMISC, self.DMISC[:, 0:NE], self.Rcnt, self.Rcnt[:, :], ds)
                self.dma("sp", self.DMISC, self.DMISC[:, NE:NE + NB + 1], eid, eid[:, 1:NB + 2], ds)
                self.dma("sp", self.DMISC, self.DMISC[:, NE + NB + 1:], dstf, dstf[:, :, :].rearrange("p t k -> p (t k)"), ds)
            NSL = 6
            hl = [P.sbuf(st, f"hl{i}", [128, D], BF16) for i in range(NSL)]
            dhl = [P.dsem("hl") for i in range(NSL)]
            dsc = [[P.dsem("scat") for k in range(2)] for i in range(NSL)]
            XS = self.XS
            for T in range(NT):
                s = T % NSL
                self.dma("sp", hl[s], hl[s][:, :], self.H2, self.H2[T * 128:(T + 1) * 128, :], dhl[s])
                for k in range(2):
                    P.op("pool", (lambda hb, ia: lambda e: e.indirect_dma_start(out=XS[:, :], out_offset=bass.IndirectOffsetOnAxis(ap=ia, axis=0), in_=hb[:, :], in_offset=None))(hl[s], dst[:, T, k:k + 1]),
                         reads=[hl[s], dst], writes=[XS], dsem=dsc[s][k])
            P.barrier()
            sA.close()
            st = st_outer
            wgu = [P.sbuf(st, f"wgu{i}", [128, 8, 2 * FF], BF16) for i in range(2)]
            wdn = [P.sbuf(st, f"wdn{i}", [128, 4, D], BF16) for i in range(2)]
            dwg = [P.dsem("wgu") for i in range(2)]
            dwd = [P.dsem("wdn") for i in range(2)]
            xb = [P.sbuf(st, f"xb{i}", [128, D], BF16) for i in range(3)]
            dxb = [P.dsem("xb") for i in range(3)]
            xT = [P.sbuf(st, f"xT{i}", [128, 8, 128], BF16) for i in range(2)]
            pT = [P.psum(st, f"pT6{i}", [128, 8, 128], BF16) for i in range(2)]
            py = [P.psum(st, f"py{i}", [128, 512], F32) for i in range(2)]
            pg = [[P.psum(st, f"pg{i}{c}", [128, 512], F32) for c in range(2)] for i in range(2)]
            eg = [P.sbuf(st, f"eg6{i}", [128, FF], F32) for i in range(2)]
            ab = [P.sbuf(st, f"ab6{i}", [128, FF], BF16) for i in range(2)]
            aT = [P.sbuf(st, f"aT6{i}", [128, 4, 128], BF16) for i in range(2)]
            yb = [P.sbuf(st, f"yb{i}", [128, D], F32) for i in range(2)]
            dyb = [P.dsem("yb") for i in range(2)]
            WGU, WDN = self.WGU, self.WDN
            bound = NE * 128 - 1
            regbox = {}

            def breg(e):
                if "r" not in regbox:
                    r = e.alloc_register("wbound")
                    e.reg_mov(r, bound)
                    regbox["r"] = r
                return regbox["r"]

            def gather_w(j, which):
                if j >= NB:
                    return
                ia = wid[:, j:j + 1]
                wg_, wd_ = wgu[j % 2], wdn[j % 2]
                if which == 0:
                    P.op("pool", lambda e: e.indirect_dma_start(out=wg_[:, :, :].rearrange("p k n -> p (k n)"), out_offset=None, in_=WGU[:, :], in_offset=bass.IndirectOffsetOnAxis(ap=ia, axis=0), bounds_check=breg(e), oob_is_err=False),
                         reads=[WGU, wid], writes=[wg_], dsem=dwg[j % 2])
                else:
                    P.op("pool", lambda e: e.indirect_dma_start(out=wd_[:, :, :].rearrange("p k n -> p (k n)"), out_offset=None, in_=WDN[:, :], in_offset=bass.IndirectOffsetOnAxis(ap=ia, axis=0), bounds_check=breg(e), oob_is_err=False),
                         reads=[WDN, wid], writes=[wd_], dsem=dwd[j % 2])

            def loadx(j):
                if j < NB:
                    self.dma("sp", xb[j % 3], xb[j % 3][:, :], self.XS, self.XS[j * 128:(j + 1) * 128, :], dxb[j % 3])

            def front(j):
                s = j % 2
                loadx(j + 2)
                for kc in range(8):
                    self.tp(pT[s], pT[s][:, kc, :], xb[j % 3], xb[j % 3][:, kc * 128:(kc + 1) * 128])
                self.cp(xT[s], xT[s][:, :, :], pT[s], pT[s][:, :, :], eng="act")
                for cb in range(2):
                    for kc in range(8):
                        self.mm(pg[s][cb], pg[s][cb][:, :], xT[s], xT[s][:, kc, :], wgu[s], wgu[s][:, kc, cb * 512:(cb + 1) * 512], kc == 0, kc == 7)
                gather_w(j + 2, 0)

            def mid(j):
                s = j % 2
                self.act(eg[s], eg[s][:, :], pg[s][0], pg[s][0][:, :], AF.Silu)
                self.tt(ab[s], ab[s][:, :], eg[s], eg[s][:, :], pg[s][1], pg[s][1][:, :], ALU.mult)

            def back1(j):
                s = j % 2
                for c in range(4):
                    self.tp(pT[s], pT[s][:, c, :], ab[s], ab[s][:, c * 128:(c + 1) * 128])
                self.cp(aT[s], aT[s][:, :, :], pT[s], pT[s][:, 0:4, :])
                for cb in range(2):
                    for kc in range(4):
                        self.mm(py[cb], py[cb][:, :], aT[s], aT[s][:, kc, :], wdn[s], wdn[s][:, kc, cb * 512:(cb + 1) * 512], kc == 0, kc == 3)
                gather_w(j + 2, 1)

            def back2(j):
                s = j % 2
                for cb in range(2):
                    self.cp(yb[s], yb[s][:, cb * 512:(cb + 1) * 512], py[cb], py[cb][:, :], eng="act" if cb else "dve")
                self.dma("sp", self.YB, self.YB[j * 128:(j + 1) * 128, :], yb[s], yb[s][:, :], dyb[s])

            gather_w(0, 0)
            gather_w(0, 1)
            gather_w(1, 0)
            gather_w(1, 1)
            loadx(0)
            loadx(1)
            front(0)
            mid(0)
            for j in range(NB):
                if j + 1 < NB:
                    front(j + 1)
                back1(j)
                if j + 1 < NB:
                    mid(j + 1)
                back2(j)
            P.barrier()
            yg = [[P.sbuf(st, f"yg{i}{k}", [128, D], F32) for k in range(2)] for i in range(2)]
            dyg = [[P.dsem("yg") for k in range(2)] for i in range(2)]
            x1 = [P.sbuf(st, f"x7{i}", [128, D], F32) for i in range(2)]
            dx7 = [P.dsem("x7") for i in range(2)]
            ot = [P.sbuf(st, f"ot{i}", [128, D], F32) for i in range(2)]
            dot = [P.dsem("ot") for i in range(2)]
            YB = self.YB
            tiles = []
            for u, (S, n_own, smp) in enumerate(self.units):
                for tl in range(n_own // 128):
                    tiles.append((u, tl))

            def cload(T):
                if T >= len(tiles):
                    return
                s = T % 2
                self.dma("sp", x1[s], x1[s][:, :], self.X1, self.X1[T * 128:(T + 1) * 128, :], dx7[s])
                for k in range(2):
                    P.op("pool", (lambda ob, ia: lambda e: e.indirect_dma_start(out=ob[:, :], out_offset=None, in_=YB[:, :], in_offset=bass.IndirectOffsetOnAxis(ap=ia, axis=0)))(yg[s][k], dst[:, T, k:k + 1]),
                         reads=[YB, dst], writes=[yg[s][k]], dsem=dyg[s][k])

            G2s = [self.mod_tile(st, "G2", u, 5, ds) for u in range(len(self.units))]
            cload(0)
            for T, (u, tl) in enumerate(tiles):
                s = T % 2
                G2 = G2s[u]
                self.ts(yg[s][0], yg[s][0][:, :], yg[s][0], yg[s][0][:, :], ROUT[:, T, 4:5], ALU.mult, reads=[ROUT])
                self.stt(yg[s][0], yg[s][0][:, :], yg[s][1], yg[s][1][:, :], ROUT[:, T, 5:6], yg[s][0], yg[s][0][:, :], ALU.mult, ALU.add, reads=[ROUT])
                self.tt(yg[s][0], yg[s][0][:, :], yg[s][0], yg[s][0][:, :], G2, G2[:, :], ALU.mult)
                cload(T + 1)
                self.tt(ot[s], ot[s][:, :], yg[s][0], yg[s][0][:, :], x1[s], x1[s][:, :], ALU.add)
                self.dma("sp", self.yu[u], self.yu[u][tl * 128:(tl + 1) * 128, :], ot[s], ot[s][:, :], dot[s])
            P.barrier()

    def build(self):
        self.declare()
        self.setup()
        import os
        stage = int(os.environ.get("KSTAGE", "9"))
        for u in range(len(self.units)):
            self.pass1(u)
        self.wst.close()
        tb = 0
        if stage >= 2:
            for u, (S, n_own, smp) in enumerate(self.units):
                with ExitStack() as so:
                    OT = self.pass2(u, so)
                    self.pass3(u, OT, tb)
                tb += n_own // 128
        if stage >= 3:
            self.moe()
        self.P.barrier()
        stats = self.P.emit()
        self.st.close()
        return self.nc, stats


_CACHE = {}


def _run(x_prompt, x_sample, c_prompt, c_sample, W, n_cores, nq, dbg=False):
    Bp, Sp, _ = x_prompt.shape
    Bs, Ss, _ = x_sample.shape
    assert Bs * nq == n_cores and Bp % n_cores == 0
    ppc = Bp // n_cores
    own = Ss // nq
    units = [(Sp, Sp, False)] * ppc + [(Ss, own, True)]
    key = (tuple(units), dbg)
    if key not in _CACHE:
        k = K(units, dbg=dbg)
        nc, stats = k.build()
        _CACHE[key] = (nc, stats)
    nc, stats = _CACHE[key]
    base = dict(W)
    in_maps = []
    for c in range(n_cores):
        m = dict(base)
        cs = []
        for i in range(ppc):
            b = c * ppc + i
            m[f"x{i}"] = np.ascontiguousarray(x_prompt[b])
            m[f"pos{i}"] = np.ascontiguousarray(np.arange(Sp, dtype=np.float32).reshape(Sp // 128, 128).T)
            cs.append(c_prompt[b])
        sq, qq = c // nq, c % nq
        order = (qq * own + np.arange(Ss)) % Ss
        m[f"x{ppc}"] = np.ascontiguousarray(x_sample[sq][order])
        m[f"pos{ppc}"] = np.ascontiguousarray(order.astype(np.float32).reshape(Ss // 128, 128).T)
        cs.append(c_sample[sq])
        m["flags"] = np.array([[1.0 if qq > 0 else 0.0, 1.0 if qq < nq - 1 else 0.0]], np.float32)
        cu = np.stack(cs, 0)
        m["cT"] = np.ascontiguousarray(cu.reshape(len(cs), 8, 128).transpose(2, 1, 0))
        in_maps.append(m)
    res = run_bass_kernel_spmd(nc, in_maps, core_ids=list(range(n_cores)))
    if dbg:
        _CACHE["last"] = res.results
    yp = np.empty((Bp, Sp, D), np.float32)
    ys = np.empty((Bs, Ss, D), np.float32)
    for c in range(n_cores):
        r = res.results[c]
        for i in range(ppc):
            yp[c * ppc + i] = r[f"y{i}"]
        sq, qq = c // nq, c % nq
        ys[sq, qq * own:(qq + 1) * own] = r[f"y{ppc}"]
    return yp, ys


def _weights(w_ada, b_ada, g_norm1, w_in, g_q, g_k, lambda_q1, lambda_k1, lambda_q2, lambda_k2, g_subln, w_dw, b_dw,
             g_conv_ln, b_conv_ln, w_out, g_norm2, w_router_group, b_router_group, w_router_expert, b_router_expert,
             w_gate_up, w_down):
    f = lambda a: np.ascontiguousarray(np.asarray(a, dtype=np.float32))
    return dict(
        w_ada=f(w_ada[0]), b_ada=f(b_ada[0]).reshape(1, -1), g_norm1=f(g_norm1[0]).reshape(1, -1), w_in=f(w_in[0]),
        g_q=f(g_q[0]).reshape(1, -1), g_k=f(g_k[0]).reshape(1, -1),
        lambda_q1=f(lambda_q1[0]).reshape(1, -1), lambda_k1=f(lambda_k1[0]).reshape(1, -1),
        lambda_q2=f(lambda_q2[0]).reshape(1, -1), lambda_k2=f(lambda_k2[0]).reshape(1, -1),
        g_subln=f(g_subln[0]).reshape(-1, 1), w_dw=f(w_dw[0]), b_dw=f(b_dw[0]).reshape(-1, 1),
        g_conv_ln=f(g_conv_ln[0]).reshape(-1, 1), b_conv_ln=f(b_conv_ln[0]).reshape(-1, 1), w_out=f(w_out[0]),
        g_norm2=f(g_norm2[0]).reshape(1, -1), w_rg=f(w_router_group[0]), b_rg=f(b_router_group[0]).reshape(1, -1),
        w_re=f(w_router_expert[0]), b_re=f(b_router_expert[0]).reshape(1, -1), w_gu=f(w_gate_up[0]), w_dn=f(w_down[0]))


def kernel(x_prompt, x_sample, c_prompt, c_sample, **weights):
    W = _weights(**weights)
    f = lambda a: np.asarray(a, dtype=np.float32)
    yp, ys = _run(f(x_prompt), f(x_sample), f(c_prompt), f(c_sample), W, n_cores=8, nq=4)
    return yp, ys
```

```python
import numpy as np
from contextlib import ExitStack
import concourse.bass as bass
import concourse.mybir as mybir
from concourse.bass_utils import run_bass_kernel_spmd

F32 = mybir.dt.float32
BF16 = mybir.dt.bfloat16
I32 = mybir.dt.int32
AF = mybir.ActivationFunctionType
ALU = mybir.AluOpType
AX = mybir.AxisListType

EPOCH = 24000


class Buf:
    def __init__(self, t, name):
        self.t = t
        self.name = name
        self.last_writer = None
        self.readers = {}

    def __getitem__(self, k):
        return self.t[k]


class DSem:
    def __init__(self, sem):
        self.sem = sem
        self.count = 0
        self.last_op = None
        self.bg = False


class DPool:
    def __init__(self, sems, sw_sems):
        self.sems = sems
        self.sw_sems = sw_sems
        self.i = 0

    def next(self, eng):
        self.i += 1
        lst = self.sw_sems if eng == "pool" else self.sems
        return lst[self.i % len(lst)]


class Op:
    __slots__ = ("eng", "fn", "reads", "writes", "dsem", "dval", "deps", "signal", "sidx", "idx", "extra")

    def __init__(self, eng, fn, reads, writes, dsem):
        self.eng = eng
        self.fn = fn
        self.reads = reads
        self.writes = writes
        self.dsem = dsem
        self.dval = 0
        self.deps = []
        self.signal = False
        self.sidx = 0
        self.extra = None


ENGS = ("pe", "act", "dve", "pool", "sp")


class Prog:
    def __init__(self, nc, stack):
        self.nc = nc
        self.stack = stack
        self.ops = []
        self.bufs = []
        self.dsems = []
        self.last = {e: None for e in ENGS}
        self.nsem = 0

    def new_sem(self, name):
        self.nsem += 1
        return self.stack.enter_context(self.nc.semaphore(f"{name}_{self.nsem}"))

    def dsem(self, name="d"):
        d = DSem(self.new_sem(name))
        self.dsems.append(d)
        return d

    def dpool(self, name="dp", n=6):
        return DPool([self.dsem(name) for _ in range(n)], [self.dsem(name + "sw") for _ in range(n)])

    def buf(self, t, name):
        b = Buf(t, name)
        self.bufs.append(b)
        return b

    def sbuf(self, stack, name, shape, dt):
        self.nsem += 1
        name = f"{name}_{self.nsem}"
        t = stack.enter_context(self.nc.sbuf_tensor(name, list(shape), dt))
        return self.buf(t, name)

    def psum(self, stack, name, shape, dt):
        self.nsem += 1
        name = f"{name}_{self.nsem}"
        t = stack.enter_context(self.nc.psum_tensor(name, list(shape), dt))
        return self.buf(t, name)

    def dram(self, name, shape, dt, kind="Internal"):
        t = self.nc.dram_tensor(name, list(shape), dt, kind=kind)
        return self.buf(t.ap(), name)

    def op(self, eng, fn, reads=(), writes=(), dsem=None):
        if isinstance(dsem, DPool):
            dsem = dsem.next(eng)
        o = Op(eng, fn, list(reads), list(writes), dsem)
        o.idx = len(self.ops)
        deps = {}
        if dsem is not None:
            if dsem.last_op is not None:
                deps[dsem.last_op.idx] = (dsem.last_op, "sem")
            dsem.last_op = o
            dsem.count += 16
            o.dval = dsem.count
        for b in o.reads:
            w = b.last_writer
            if w is not None:
                deps[w.idx] = (w, "raw")
        for b in o.writes:
            w = b.last_writer
            if w is not None and w.idx not in deps:
                deps[w.idx] = (w, "waw")
            for r in b.readers.values():
                if r.idx not in deps:
                    deps[r.idx] = (r, "war")
        for (d, kind) in deps.values():
            if d is o:
                continue
            if d.dsem is None and d.eng == eng and dsem is None:
                if eng == "pe":
                    continue
            o.deps.append((d, d.dval if d.dsem is not None else 0))
            if d.dsem is None:
                d.signal = True
        for b in o.writes:
            b.last_writer = o
            b.readers = {}
        for b in o.reads:
            key = ("d", id(dsem)) if dsem is not None else eng
            b.readers[key] = o
        self.ops.append(o)
        if dsem is None:
            self.last[eng] = o
        return o

    def barrier(self, include_bg=False):
        lasts = [o for o in self.last.values() if o is not None]
        dlast = []
        for d in self.dsems:
            if d.count > 0 and (include_bg or not d.bg):
                dlast.append(d)
        for e in ENGS:
            o = Op(e, None, [], [], None)
            o.idx = len(self.ops)
            for l in lasts:
                if l.eng != e:
                    o.deps.append((l, 0))
                    l.signal = True
            o.extra = [(d, d.count) for d in dlast]
            self.ops.append(o)
        for b in self.bufs:
            b.last_writer = None
            b.readers = {}

    def emit(self):
        nc = self.nc
        cnt = {e: 0 for e in ENGS}
        for o in self.ops:
            if o.signal:
                cnt[o.eng] += 1
                o.sidx = cnt[o.eng]
        esems = {}
        for e in ENGS:
            n = cnt[e] // EPOCH + 1
            esems[e] = [self.new_sem(f"e_{e}") for _ in range(n)]

        def semval(dv):
            d, v = dv
            if d.dsem is not None:
                return d.dsem.sem, v
            ep = (d.sidx - 1) // EPOCH
            return esems[d.eng][ep], d.sidx - ep * EPOCH

        per = {e: [o for o in self.ops if o.eng == e] for e in ENGS}
        stats = {"waits": 0, "ops": len(self.ops)}

        def run(engname, eng):
            known = {}
            for o in per[engname]:
                ws = {}
                for d in o.deps:
                    s, v = semval(d)
                    k = id(s)
                    if known.get(k, 0) >= v:
                        continue
                    if k not in ws or ws[k][1] < v:
                        ws[k] = (s, v)
                if o.extra:
                    for (d, v) in o.extra:
                        k = id(d.sem)
                        if known.get(k, 0) >= v:
                            continue
                        if k not in ws or ws[k][1] < v:
                            ws[k] = (d.sem, v)
                for k, (s, v) in ws.items():
                    eng.wait_ge(s, v)
                    known[k] = v
                    stats["waits"] += 1
                if o.fn is None:
                    continue
                ins = o.fn(eng)
                if o.dsem is not None:
                    ins.then_inc(o.dsem.sem, 16)
                elif o.signal:
                    ep = (o.sidx - 1) // EPOCH
                    ins.then_inc(esems[engname][ep], 1)

        with nc.Block() as block:
            @block.sync
            def _(e):
                run("sp", e)

            @block.tensor
            def _(e):
                run("pe", e)

            @block.scalar
            def _(e):
                run("act", e)

            @block.vector
            def _(e):
                run("dve", e)

            @block.gpsimd
            def _(e):
                run("pool", e)
        return stats


D = 1024
NH = 4
HD = 64
VD = 128
INW = 2560
NE = 32
FF = 512
CK = 31
CP = 15
EPS = 1e-6
ROPE_THETA = 500000.0
INV_FREQ = [float(ROPE_THETA ** (-(2 * i) / 16.0)) for i in range(8)]
LAM_INIT = 0.2
TWO_PI = 6.283185307179586
BIG = 4096.0


class K:
    def __init__(self, units, dbg=False):
        self.units = units
        self.dbg = dbg
        self.nc = nc = bass.Bass("TRN2", target_bir_lowering=False)
        self.st = ExitStack()
        self.P = Prog(nc, self.st)
        self.n_tok = sum(u[1] for u in units)
        self.n_tiles = self.n_tok // 128
        self.n_rows = 2 * self.n_tok + NE * 128
        self.n_blk = self.n_rows // 128
        self._uid = 0

    def uid(self, s):
        self._uid += 1
        return f"{s}{self._uid}"

    def cds(self, name, i=0):
        if not hasattr(self, "_cds"):
            self._cds = {}
        k = (name, i)
        if k not in self._cds:
            self._cds[k] = self.P.dsem(name)
        return self._cds[k]

    def ein(self, name, shape, dt):
        return self.P.buf(self.nc.dram_tensor(name, list(shape), dt, kind="ExternalInput").ap(), name)

    def eout(self, name, shape, dt):
        return self.P.buf(self.nc.dram_tensor(name, list(shape), dt, kind="ExternalOutput").ap(), name)

    def dma(self, q, out_b, out_ap, in_b, in_ap, ds, **kw):
        return self.P.op(q, lambda e: e.dma_start(out=out_ap, in_=in_ap, **kw), reads=[in_b], writes=[out_b], dsem=ds)

    def mm(self, ob, oap, lb, lap, rb, rap, start, stop):
        return self.P.op("pe", lambda e: e.matmul(oap, lap, rap, start=start, stop=stop), reads=[lb, rb], writes=[ob])

    def tp(self, ob, oap, ib, iap):
        ident = self.ident
        return self.P.op("pe", lambda e: e.transpose(oap, iap, ident[:, :]), reads=[ib, ident], writes=[ob])

    def act(self, ob, oap, ib, iap, func, reads=(), **kw):
        return self.P.op("act", lambda e: e.activation(out=oap, in_=iap, func=func, **kw), reads=[ib] + list(reads), writes=[ob])

    def tt(self, ob, oap, ab, aap, bb, bap, op, eng="dve"):
        return self.P.op(eng, lambda e: e.tensor_tensor(out=oap, in0=aap, in1=bap, op=op), reads=[ab, bb], writes=[ob])

    def ts(self, ob, oap, ib, iap, s1, op0, s2=None, op1=None, reads=(), eng="dve"):
        if op1 is None:
            return self.P.op(eng, lambda e: e.tensor_scalar(out=oap, in0=iap, scalar1=s1, scalar2=None, op0=op0), reads=[ib] + list(reads), writes=[ob])
        return self.P.op(eng, lambda e: e.tensor_scalar(out=oap, in0=iap, scalar1=s1, scalar2=s2, op0=op0, op1=op1), reads=[ib] + list(reads), writes=[ob])

    def stt(self, ob, oap, ab, aap, sc, bb, bap, op0, op1, reads=()):
        return self.P.op("dve", lambda e: e.scalar_tensor_tensor(out=oap, in0=aap, scalar=sc, in1=bap, op0=op0, op1=op1), reads=[ab, bb] + list(reads), writes=[ob])

    def cp(self, ob, oap, ib, iap, eng="dve"):
        if eng == "act":
            return self.P.op("act", lambda e: e.copy(out=oap, in_=iap), reads=[ib], writes=[ob])
        return self.P.op(eng, lambda e: e.tensor_copy(out=oap, in_=iap), reads=[ib], writes=[ob])

    def red(self, ob, oap, ib, iap, op=None):
        op = op or ALU.add
        return self.P.op("dve", lambda e: e.tensor_reduce(out=oap, in_=iap, axis=AX.X, op=op), reads=[ib], writes=[ob])

    def recip(self, ob, oap, ib, iap):
        return self.P.op("dve", lambda e: e.reciprocal(out=oap, in_=iap), reads=[ib], writes=[ob])

    def memset(self, ob, oap, val, eng="dve"):
        return self.P.op(eng, lambda e: e.memset(oap, val), writes=[ob])

    def rsqrt(self, ob, oap, ib, iap, tmpb, tmpap, scale=1.0, bias=0.0):
        if bias != 0.0:
            bt = self.epsb
            self.act(tmpb, tmpap, ib, iap, AF.Ln, reads=[bt], scale=scale, bias=bt[0:oap.shape[0], 0:1])
        else:
            self.act(tmpb, tmpap, ib, iap, AF.Ln, scale=scale)
        self.act(ob, oap, tmpb, tmpap, AF.Exp, scale=-0.5)

    def bload(self, st, name, src_b, src_ap, n, ds, dt=F32):
        t = self.P.sbuf(st, name, [128, n], dt)
        self.dma("sp", t, t[:, :], src_b, src_ap.partition_broadcast(128).rearrange("p o f -> p (o f)"), ds)
        return t

    def declare(self):
        nu = len(self.units)
        self.xu = [self.ein(f"x{u}", [S, D], F32) for u, (S, n, smp) in enumerate(self.units)]
        self.posu = [self.ein(f"pos{u}", [128, S // 128], F32) for u, (S, n, smp) in enumerate(self.units)]
        self.flg = self.ein("flags", [1, 2], F32)
        self.cT = self.ein("cT", [128, 8, nu], F32)
        w = {}
        for name, shape in [("w_ada", [D, 6 * D]), ("b_ada", [1, 6 * D]), ("g_norm1", [1, D]), ("w_in", [D, INW]),
                            ("g_q", [1, HD]), ("g_k", [1, HD]), ("lambda_q1", [1, HD]), ("lambda_k1", [1, HD]),
                            ("lambda_q2", [1, HD]), ("lambda_k2", [1, HD]), ("g_subln", [VD, 1]), ("w_dw", [CK, 512]),
                            ("b_dw", [512, 1]), ("g_conv_ln", [512, 1]), ("b_conv_ln", [512, 1]), ("w_out", [D, D]),
                            ("g_norm2", [1, D]), ("w_rg", [D, 4]), ("b_rg", [1, 4]), ("w_re", [D, NE]), ("b_re", [1, NE]),
                            ("w_gu", [NE, D, 2 * FF]), ("w_dn", [NE, FF, D])]:
            w[name] = self.ein(name, shape, F32)
        self.w = w
        self.yu = [self.eout(f"y{u}", [n, D], F32) for u, (S, n, smp) in enumerate(self.units)]
        P = self.P
        if self.dbg:
            _d = P.dram
            P.dram = lambda name, shape, dt, kind="Internal": _d(name, shape, dt, kind="ExternalOutput")
        self.MOD = P.dram("MOD", [nu, 6 * D], F32)
        self.KT = [P.dram(f"KT{u}", [NH, 128, S], BF16) for u, (S, n, smp) in enumerate(self.units)]
        self.VV = [P.dram(f"VV{u}", [128, S // 128, 512], BF16) for u, (S, n, smp) in enumerate(self.units)]
        self.QT = [P.dram(f"QT{u}", [NH, 128, n], BF16) for u, (S, n, smp) in enumerate(self.units)]
        self.ZT = [P.dram(f"ZT{u}", [4, 128, n + 2 * CP + 2], BF16) for u, (S, n, smp) in enumerate(self.units)]
        self.X1 = P.dram("X1", [self.n_tok, D], F32)
        self.H2 = P.dram("H2", [self.n_tok, D], BF16)
        self.XS = P.dram("XS", [self.n_rows, D], BF16)
        self.YB = P.dram("YB", [self.n_rows, D], F32)
        self.WGU = P.dram("WGU", [NE * 128, 8 * 2 * FF], BF16)
        self.WDN = P.dram("WDN", [NE * 128, 4 * D], BF16)
        if self.dbg:
            self.DROUT = P.dram("DROUT", [128, self.n_tiles * 6], F32)
            self.DMISC = P.dram("DMISC", [128, NE + self.n_blk + 1 + self.n_tiles * 2], F32)

    def setup(self):
        P, nc, st = self.P, self.nc, self.st
        nu = len(self.units)
        w = self.w
        ds_w = self.gp = P.dpool("gp", 8)
        cst = self.cst = ExitStack()
        self.st.enter_context(cst)
        ident = self.ident = P.sbuf(cst, "ident", [128, 128], BF16)
        ones = self.ones = P.sbuf(cst, "ones", [128, 128], BF16)
        self.utri = P.sbuf(cst, "utri", [128, 128], BF16)
        self.iota32 = P.sbuf(cst, "iota32", [128, NE], F32)
        self.ROUT = P.sbuf(cst, "ROUT", [128, self.n_tiles, 6], F32)
        self.Rcnt = P.sbuf(cst, "Rcnt", [128, NE], F32)
        self.neglam = P.sbuf(cst, "neglam", [128, 1], F32)
        self.gsub = P.sbuf(cst, "gsub", [128, 1], F32)
        self.flags = P.sbuf(cst, "flagsb", [128, 2], F32)
        self.epsb = P.sbuf(cst, "epsb", [128, 1], F32)
        self.zero = P.sbuf(cst, "zero", [128, 4 * D], BF16)
        self.wst = ExitStack()
        self.win = P.sbuf(self.wst, "win", [128, 8, INW], BF16)
        dsc = self.gp
        with ExitStack() as t:
            ci = P.sbuf(t, "ci", [128, 128], I32)
            ri = P.sbuf(t, "ri", [128, 128], I32)
            cf = P.sbuf(t, "cf", [128, 128], F32)
            rf = P.sbuf(t, "rf", [128, 128], F32)
            P.op("pool", lambda e: e.iota(ci[:, :], pattern=[[1, 128]], base=0, channel_multiplier=0), writes=[ci])
            P.op("pool", lambda e: e.iota(ri[:, :], pattern=[[0, 128]], base=0, channel_multiplier=1), writes=[ri])
            self.cp(cf, cf[:, :], ci, ci[:, :])
            self.cp(rf, rf[:, :], ri, ri[:, :])
            self.cp(self.iota32, self.iota32[:, :], cf, cf[:, 0:NE])
            self.tt(ident, ident[:, :], rf, rf[:, :], cf, cf[:, :], ALU.is_equal)
            self.tt(self.utri, self.utri[:, :], rf, rf[:, :], cf, cf[:, :], ALU.is_lt)
            self.memset(ones, ones[:, :], 1.0)
            self.memset(self.epsb, self.epsb[:, :], EPS)
            self.memset(self.Rcnt, self.Rcnt[:, :], 0.0)
            self.dma("sp", self.flags, self.flags[:, :], self.flg, self.flg[0:1, :].partition_broadcast(128).rearrange("p o f -> p (o f)"), dsc)
            lt = [self.bload(t, f"lam{i}", w[n], w[n][0:1, :], HD, dsc) for i, n in enumerate(["lambda_q1", "lambda_k1", "lambda_q2", "lambda_k2"])]
            pr = P.sbuf(t, "lampr", [128, HD], F32)
            s1 = P.sbuf(t, "lams1", [128, 2], F32)
            self.tt(pr, pr[:, :], lt[0], lt[0][:, :], lt[1], lt[1][:, :], ALU.mult)
            self.red(s1, s1[:, 0:1], pr, pr[:, :])
            self.tt(pr, pr[:, :], lt[2], lt[2][:, :], lt[3], lt[3][:, :], ALU.mult)
            self.red(s1, s1[:, 1:2], pr, pr[:, :])
            e1 = P.sbuf(t, "lame1", [128, 2], F32)
            self.act(e1, e1[:, :], s1, s1[:, :], AF.Exp)
            self.tt(self.neglam, self.neglam[:, :], e1, e1[:, 1:2], e1, e1[:, 0:1], ALU.subtract)
            self.ts(self.neglam, self.neglam[:, :], self.neglam, self.neglam[:, :], -LAM_INIT, ALU.add)
            gs = P.sbuf(t, "gs", [128, 1], F32)
            self.dma("sp", gs, gs[:, :], w["g_subln"], w["g_subln"][:, :], dsc)
            self.ts(self.gsub, self.gsub[:, :], gs, gs[:, :], 1.0 - LAM_INIT, ALU.mult)
            dsw = ds_w
            wv = w["w_in"][:, :].rearrange("(kc p) n -> p kc n", p=128)
            for kc in range(8):
                for cb in range(INW // 512):
                    self.dma("pool", self.win, self.win[:, kc, cb * 512:(cb + 1) * 512], w["w_in"], wv[:, kc, cb * 512:(cb + 1) * 512], dsw)
            bgp = DPool([], [P.dsem("bgw") for _ in range(6)])
            for d_ in bgp.sw_sems:
                d_.bg = True
            wgu_v = self.WGU[:, :].rearrange("(e p) (kc n) -> e p kc n", p=128, kc=8)
            wdn_v = self.WDN[:, :].rearrange("(e p) (kc n) -> e p kc n", p=128, kc=4)
            for e in range(NE):
                for kc in range(8):
                    self.dma("pool", self.WGU, wgu_v[e, :, kc, :], w["w_gu"], w["w_gu"][e, kc * 128:(kc + 1) * 128, :], bgp)
                for kc in range(4):
                    self.dma("pool", self.WDN, wdn_v[e, :, kc, :], w["w_dn"], w["w_dn"][e, kc * 128:(kc + 1) * 128, :], bgp)
            zero = self.zero
            self.memset(zero, zero[:, :], 0.0)
            xsv = self.XS[:, :].rearrange("(a p r) d -> a p (r d)", p=128, r=4)
            for a in range(self.n_rows // 512):
                self.dma("pool", self.XS, xsv[a], zero, zero[:, :], bgp)
            ct = P.sbuf(t, "ct", [128, 8, nu], F32)
            ce = P.sbuf(t, "ce", [128, 8, nu], F32)
            sc = P.sbuf(t, "sc", [128, 8, nu], F32)
            self.dma("sp", ct, ct[:, :, :], self.cT, self.cT[:, :, :], dsc)
            self.act(ce, ce[:, :, :], ct, ct[:, :, :], AF.Exp, scale=-1.0)
            self.ts(ce, ce[:, :, :], ce, ce[:, :, :], 1.0, ALU.add)
            self.recip(ce, ce[:, :, :], ce, ce[:, :, :])
            self.tt(sc, sc[:, :, :], ct, ct[:, :, :], ce, ce[:, :, :], ALU.mult)
            bada = P.sbuf(t, "bada", [nu, 6 * D], F32)
            self.dma("sp", bada, bada[:, :], w["b_ada"], w["b_ada"][0:1, :].partition_broadcast(nu).rearrange("p o f -> p (o f)"), dsc)
            modsb = P.sbuf(t, "modsb", [nu, 6 * D], F32)
            wab = [P.sbuf(t, f"wab{i}", [128, 8, 512], F32) for i in range(2)]
            dsa = [P.dsem("wab") for i in range(2)]
            pm = [P.psum(t, f"pmod{i}", [nu, 512], F32) for i in range(2)]
            wav = w["w_ada"][:, :].rearrange("(kc p) n -> p kc n", p=128)
            for cb in range(12):
                b = cb % 2
                self.dma("sp", wab[b], wab[b][:, :, :], w["w_ada"], wav[:, :, cb * 512:(cb + 1) * 512], dsa[b])
                for kc in range(8):
                    self.mm(pm[b], pm[b][:, :], sc, sc[:, kc, :], wab[b], wab[b][:, kc, :], kc == 0, kc == 7)
                self.tt(modsb, modsb[:, cb * 512:(cb + 1) * 512], pm[b], pm[b][:, :], bada, bada[:, cb * 512:(cb + 1) * 512], ALU.add)
            self.dma("sp", self.MOD, self.MOD[:, :], modsb, modsb[:, :], dsc)
            P.barrier()

    def mod_tile(self, st, name, u, j, ds, gname=None):
        t = self.bload(st, self.uid(name), self.MOD, self.MOD[u:u + 1, j * D:(j + 1) * D], D, ds)
        if gname is not None:
            g = self.bload(st, self.uid(name + "g"), self.w[gname], self.w[gname][0:1, :], D, ds)
            self.stt(t, t[:, :], t, t[:, :], 1.0, g, g[:, :], ALU.add, ALU.mult)
        return t

    def rms_rstd(self, xb, xap, n, junkb, junkap, ssb, ssap, tmpb, tmpap, outb, outap):
        self.P.op("act", lambda e: e.activation(out=junkap, in_=xap, func=AF.Square, accum_out=ssap), reads=[xb], writes=[junkb, ssb])
        self.rsqrt(outb, outap, ssb, ssap, tmpb, tmpap, scale=1.0 / n, bias=EPS)

    def pass1(self, u):
        P = self.P
        S, n_own, smp = self.units[u]
        nt, not_ = S // 128, n_own // 128
        w = self.w
        with ExitStack() as st:
            ds = self.gp
            A1 = self.mod_tile(st, "A1", u, 1, ds, "g_norm1")
            B1 = self.mod_tile(st, "B1", u, 0, ds)
            gq = P.sbuf(st, "gq", [128, 8, HD], F32)
            gk = P.sbuf(st, "gk", [128, 8, HD], F32)
            for g in range(8):
                self.dma("sp", gq, gq[:, g, :], w["g_q"], w["g_q"][0:1, :].partition_broadcast(128).rearrange("p o f -> p (o f)"), ds)
                self.dma("sp", gk, gk[:, g, :], w["g_k"], w["g_k"][0:1, :].partition_broadcast(128).rearrange("p o f -> p (o f)"), ds)
            pos = P.sbuf(st, "pos", [128, nt], F32)
            self.dma("sp", pos, pos[:, :], self.posu[u], self.posu[u][:, :], ds)
            cosT = P.sbuf(st, "cosT", [128, nt, 8], F32)
            sinT = P.sbuf(st, "sinT", [128, nt, 8], F32)
            if True:
                t2 = st
                ang = P.sbuf(t2, "ang", [128, nt, 8], F32)
                ki = P.sbuf(t2, "ki", [128, nt, 8], I32)
                kf = P.sbuf(t2, "kf", [128, nt, 8], F32)
                fr = P.sbuf(t2, "fr", [128, nt, 8], F32)
                adj = P.sbuf(t2, "adj", [128, nt, 8], F32)
                for i in range(8):
                    self.ts(ang, ang[:, :, i], pos, pos[:, :], INV_FREQ[i] / TWO_PI, ALU.mult)
                for (tab, off) in ((sinT, 0.0), (cosT, 0.25)):
                    self.ts(fr, fr[:, :, :], ang, ang[:, :, :], off, ALU.add)
                    self.cp(ki, ki[:, :, :], fr, fr[:, :, :])
                    self.cp(kf, kf[:, :, :], ki, ki[:, :, :])
                    self.tt(fr, fr[:, :, :], fr, fr[:, :, :], kf, kf[:, :, :], ALU.subtract)
                    self.ts(adj, adj[:, :, :], fr, fr[:, :, :], 0.5, ALU.is_gt)
                    self.tt(fr, fr[:, :, :], fr, fr[:, :, :], adj, adj[:, :, :], ALU.subtract)
                    self.ts(adj, adj[:, :, :], fr, fr[:, :, :], -0.5, ALU.is_lt)
                    self.tt(fr, fr[:, :, :], fr, fr[:, :, :], adj, adj[:, :, :], ALU.add)
                    self.act(tab, tab[:, :, :], fr, fr[:, :, :], AF.Sin, scale=TWO_PI)
            xt = [P.sbuf(st, f"xt{i}", [128, D], F32) for i in range(3)]
            dsx = [self.cds("xt", i) for i in range(3)]
            junk = P.sbuf(st, "junk", [128, D], BF16)
            sm = [P.sbuf(st, f"sm{i}", [128, 8], F32) for i in range(2)]
            xn = P.sbuf(st, "xn", [128, D], F32)
            hb = [P.sbuf(st, f"hb{i}", [128, D], BF16) for i in range(2)]
            hT = [P.sbuf(st, f"hT{i}", [128, 8, 128], BF16) for i in range(2)]
            pTa = P.psum(st, "pTa", [128, 8, 128], BF16)
            pTb = P.psum(st, "pTb", [128, 8, 128], BF16)
            NBK = 6
            pb = [P.psum(st, f"pb{i}", [128, 512], F32) for i in range(NBK)]
            sq = P.sbuf(st, "sq", [128, 8, HD], F32)
            ss8 = P.sbuf(st, "ss8", [128, 8], F32)
            ln8 = P.sbuf(st, "ln8", [128, 8], F32)
            rs8 = P.sbuf(st, "rs8", [128, 8], F32)
            qn = P.sbuf(st, "qn", [128, 8, HD], F32)
            xr = P.sbuf(st, "xr", [128, 8, 16], F32)
            r1 = P.sbuf(st, "r1", [128, 8, 8], F32)
            r2 = P.sbuf(st, "r2", [128, 8, 8], F32)
            kb_ = [P.sbuf(st, f"kb{i}", [128, 8, HD], BF16) for i in range(2)]
            qb_ = [P.sbuf(st, f"qb{i}", [128, 8, HD], BF16) for i in range(2)]
            G = 4
            qst = P.sbuf(st, "qst", [128, NH, G * 128], BF16)
            kst = P.sbuf(st, "kst", [128, NH, G * 128], BF16)
            zst = P.sbuf(st, "zst", [128, 4, G * 128], BF16)
            dq, dk, dz = self.cds("qst"), self.cds("kst"), self.cds("zst")
            vs = [P.sbuf(st, f"vs{i}", [128, 512], BF16) for i in range(2)]
            dv = [self.cds("vs", i) for i in range(2)]
            eg = P.sbuf(st, "eg", [128, 512], F32)
            zb = [P.sbuf(st, f"zb{i}", [128, 512], BF16) for i in range(2)]
            zh = P.sbuf(st, "zh", [128, 4, 16], BF16)
            dzh = self.cds("zh")
            nbank = [0]
            banks = {}

            def bank():
                nbank[0] += 1
                return pb[nbank[0] % NBK]

            def proj(hTt, cb):
                b = bank()
                for kc in range(8):
                    self.mm(b, b[:, :], hTt, hTt[:, kc, :], self.win, self.win[:, kc, cb * 512:(cb + 1) * 512], kc == 0, kc == 7)
                return b

            def qk_post(b, gt, t, out):
                bv = b[:, :].rearrange("p (g d) -> p g d", g=8)
                self.act(sq, sq[:, :, :], b, bv, AF.Square)
                self.red(ss8, ss8[:, :], sq, sq[:, :, :])
                self.rsqrt(rs8, rs8[:, :], ss8, ss8[:, :], ln8, ln8[:, :], scale=1.0 / HD, bias=EPS)
                self.tt(qn, qn[:, :, :], b, bv, rs8, rs8[:, :].unsqueeze(2).to_broadcast([128, 8, HD]), ALU.mult)
                self.tt(out, out[:, :, :], qn, qn[:, :, :], gt, gt[:, :, :], ALU.mult)
                self.tt(xr, xr[:, :, :], qn, qn[:, :, 0:16], gt, gt[:, :, 0:16], ALU.mult)
                cb_ = cosT[:, t, :].unsqueeze(1).to_broadcast([128, 8, 8])
                sb_ = sinT[:, t, :].unsqueeze(1).to_broadcast([128, 8, 8])
                self.tt(r1, r1[:, :, :], xr, xr[:, :, 0:8], cosT, cb_, ALU.mult)
                self.tt(r2, r2[:, :, :], xr, xr[:, :, 8:16], sinT, sb_, ALU.mult)
                self.tt(out, out[:, :, 0:8], r1, r1[:, :, :], r2, r2[:, :, :], ALU.subtract)
                self.tt(r1, r1[:, :, :], xr, xr[:, :, 8:16], cosT, cb_, ALU.mult)
                self.tt(r2, r2[:, :, :], xr, xr[:, :, 0:8], sinT, sb_, ALU.mult)
                self.tt(out, out[:, :, 8:16], r1, r1[:, :, :], r2, r2[:, :, :], ALU.add)

            def kind(t):
                own = t < not_
                halo_r = smp and t == not_
                halo_l = smp and t == nt - 1
                return own, halo_r, halo_l

            def loadx(t):
                if t < nt:
                    self.dma("sp", xt[t % 3], xt[t % 3][:, :], self.xu[u], self.xu[u][t * 128:(t + 1) * 128, :], dsx[t % 3])

            def stA(t):
                s = t % 2
                x_ = xt[t % 3]
                loadx(t + 2)
                self.rms_rstd(x_, x_[:, :], D, junk, junk[:, :], sm[s], sm[s][:, 0:1], sm[s], sm[s][:, 1:2], sm[s], sm[s][:, 2:3])
                self.stt(xn, xn[:, :], x_, x_[:, :], sm[s][:, 2:3], A1, A1[:, :], ALU.mult, ALU.mult, reads=[sm[s]])
                self.tt(hb[s], hb[s][:, :], xn, xn[:, :], B1, B1[:, :], ALU.add)
                for kc in range(8):
                    self.tp(pTa, pTa[:, kc, :], hb[s], hb[s][:, kc * 128:(kc + 1) * 128])
                self.cp(hT[s], hT[s][:, :, :], pTa, pTa[:, :, :], eng="act")

            def stP(t):
                s = t % 2
                own, halo_r, halo_l = kind(t)
                bk = {"k": proj(hT[s], 1)}
                if own:
                    bk["q"] = proj(hT[s], 0)
                bk["v"] = proj(hT[s], 2)
                if own or halo_r or halo_l:
                    bk["a"] = proj(hT[s], 3)
                    bk["g"] = proj(hT[s], 4)
                banks[t] = bk

            def stPost(t):
                s = t % 2
                bk = banks[t]
                qk_post(bk["k"], gk, t, kb_[s])
                if "q" in bk:
                    qk_post(bk["q"], gq, t, qb_[s])
                self.cp(vs[s], vs[s][:, :], bk["v"], bk["v"][:, :])
                self.dma("sp", self.VV[u], self.VV[u][:, t, :], vs[s], vs[s][:, :], dv[s])
                if "a" in bk:
                    ba, bg = bk["a"], bk["g"]
                    self.act(eg, eg[:, :], bg, bg[:, :], AF.Exp, scale=-1.0)
                    self.ts(eg, eg[:, :], eg, eg[:, :], 1.0, ALU.add)
                    self.recip(eg, eg[:, :], eg, eg[:, :])
                    self.tt(zb[s], zb[s][:, :], ba, ba[:, :], eg, eg[:, :], ALU.mult)

            def stBack(t):
                s = t % 2
                own, halo_r, halo_l = kind(t)
                bk = banks.pop(t)
                g0 = (t % G) * 128
                for h in range(NH):
                    self.tp(pTb, pTb[:, h, :], kb_[s], kb_[s][:, 2 * h:2 * h + 2, :].rearrange("p g d -> p (g d)"))
                if own:
                    for h in range(NH):
                        self.tp(pTb, pTb[:, 4 + h, :], qb_[s], qb_[s][:, 2 * h:2 * h + 2, :].rearrange("p g d -> p (g d)"))
                self.cp(kst, kst[:, :, g0:g0 + 128], pTb, pTb[:, 0:NH, :])
                if t % G == G - 1 or t == nt - 1:
                    t0 = (t // G) * G
                    nn = (t - t0 + 1) * 128
                    self.dma("sp", self.KT[u], self.KT[u][:, :, t0 * 128:t0 * 128 + nn].rearrange("h p n -> p h n"), kst, kst[:, :, 0:nn], dk)
                if own:
                    self.cp(qst, qst[:, :, g0:g0 + 128], pTb, pTb[:, 4:4 + NH, :])
                    if t % G == G - 1 or t == not_ - 1:
                        t0 = (t // G) * G
                        nn = (t - t0 + 1) * 128
                        self.dma("sp", self.QT[u], self.QT[u][:, :, t0 * 128:t0 * 128 + nn].rearrange("h p n -> p h n"), qst, qst[:, :, 0:nn], dq)
                if "a" in bk:
                    for c in range(4):
                        self.tp(pTb, pTb[:, c, :], zb[s], zb[s][:, c * 128:(c + 1) * 128])
                    if own:
                        self.cp(zst, zst[:, :, g0:g0 + 128], pTb, pTb[:, 0:4, :])
                        if t % G == G - 1 or t == not_ - 1:
                            t0 = (t // G) * G
                            nn = (t - t0 + 1) * 128
                            self.dma("sp", self.ZT[u], self.ZT[u][:, :, CP + 1 + t0 * 128:CP + 1 + t0 * 128 + nn].rearrange("c p n -> p c n"), zst, zst[:, :, 0:nn], dz)
                    elif halo_r:
                        self.ts(zh, zh[:, :, 0:CP], pTb, pTb[:, 0:4, 0:CP], self.flags[:, 1:2], ALU.mult, reads=[self.flags])
                        self.dma("sp", self.ZT[u], self.ZT[u][:, :, CP + 1 + n_own:CP + 1 + n_own + CP].rearrange("c p n -> p c n"), zh, zh[:, :, 0:CP], dzh)
                    else:
                        self.ts(zh, zh[:, :, 0:CP], pTb, pTb[:, 0:4, 128 - CP:128], self.flags[:, 0:1], ALU.mult, reads=[self.flags])
                        self.dma("sp", self.ZT[u], self.ZT[u][:, :, 1:1 + CP].rearrange("c p n -> p c n"), zh, zh[:, :, 0:CP], dzh)

            loadx(0)
            loadx(1)
            stA(0)
            for t in range(nt):
                stP(t)
                if t > 0:
                    stBack(t - 1)
                if t + 1 < nt:
                    stA(t + 1)
                stPost(t)
            stBack(nt - 1)
            P.barrier()

    def pass2(self, u, st_ot):
        P = self.P
        S, n_own, smp = self.units[u]
        nkc = S // 128
        QB = min(512, n_own)
        OT = P.sbuf(st_ot, self.uid("OT"), [128, NH, n_own], BF16)
        with ExitStack() as st:
            kt = P.sbuf(st, "kt", [128, S], BF16)
            vv = P.sbuf(st, "vv", [128, nkc, VD], BF16)
            qt = P.sbuf(st, "qt", [128, n_own], BF16)
            dl = self.gp
            psc = [[P.psum(st, f"psc{i}{c}", [128, QB], F32) for c in range(2)] for i in range(2)]
            pO = [P.psum(st, f"pO{c}", [128, QB], F32) for c in range(2)]
            pS = [P.psum(st, f"pS{c}", [128, QB], F32) for c in range(2)]
            et = [[P.sbuf(st, f"et{i}{c}", [128, QB], BF16) for c in range(2)] for i in range(3)]
            rr = [P.sbuf(st, f"rr{c}", [128, QB], F32) for c in range(2)]
            oo = [P.sbuf(st, f"oo{c}", [128, QB], F32) for c in range(2)]
            osq = P.sbuf(st, "osq", [128, QB], BF16)
            lnv = P.sbuf(st, "lnv", [128, QB], F32)
            pending = [None]
            for h in range(NH):
                self.dma("sp", kt, kt[:, :], self.KT[u], self.KT[u][h, :, :], dl)
                for k0 in range(0, nkc, 16):
                    k1 = min(nkc, k0 + 16)
                    self.dma("sp", vv, vv[:, k0:k1, :], self.VV[u], self.VV[u][:, k0:k1, h * VD:(h + 1) * VD], dl)
                self.dma("sp", qt, qt[:, :], self.QT[u], self.QT[u][h, :, :], dl)
                for q0 in range(0, n_own, QB):
                    def scores(kc, i):
                        for c in range(2):
                            self.mm(psc[i][c], psc[i][c][:, :], kt, kt[c * 64:(c + 1) * 64, kc * 128:(kc + 1) * 128],
                                    qt, qt[c * 64:(c + 1) * 64, q0:q0 + QB], True, True)
                    scores(0, 0)
                    kd = min(8, nkc - 1)
                    for kc in range(nkc):
                        i = kc % 2
                        if kc == kd and pending[0] is not None:
                            pending[0](psc[1 - i][0])
                            pending[0] = None
                        if kc + 1 < nkc:
                            scores(kc + 1, 1 - i)
                        j = kc % 3
                        for c in range(2):
                            self.act(et[j][c], et[j][c][:, :], psc[i][c], psc[i][c][:, :], AF.Exp, scale=HD ** -0.5)
                        for c in range(2):
                            self.mm(pO[c], pO[c][:, :], vv, vv[:, kc, :], et[j][c], et[j][c][:, :], kc == 0, kc == nkc - 1)
                            self.mm(pS[c], pS[c][:, :], self.ones, self.ones[:, :], et[j][c], et[j][c][:, :], kc == 0, kc == nkc - 1)
                    for c in range(2):
                        self.cp(rr[c], rr[c][:, :], pS[c], pS[c][:, :], eng="act" if c else "dve")
                        self.cp(oo[c], oo[c][:, :], pO[c], pO[c][:, :], eng="act" if c else "dve")
                    for c in range(2):
                        self.recip(rr[c], rr[c][:, :], rr[c], rr[c][:, :])
                        self.tt(oo[c], oo[c][:, :], oo[c], oo[c][:, :], rr[c], rr[c][:, :], ALU.mult)
                    self.stt(oo[0], oo[0][:, :], oo[1], oo[1][:, :], self.neglam[:, 0:1], oo[0], oo[0][:, :], ALU.mult, ALU.add, reads=[self.neglam])
                    self.tt(osq, osq[:, :], oo[0], oo[0][:, :], oo[0], oo[0][:, :], ALU.mult)

                    def fin2(b, h=h, q0=q0):
                        self.mm(b, b[:, :], self.ones, self.ones[:, :], osq, osq[:, :], True, True)
                        self.rsqrt(rr[1], rr[1][:, :], b, b[:, :], lnv, lnv[:, :], scale=1.0 / VD, bias=EPS)
                        self.stt(OT, OT[:, h, q0:q0 + QB], oo[0], oo[0][:, :], self.gsub[:, 0:1], rr[1], rr[1][:, :], ALU.mult, ALU.mult, reads=[self.gsub])
                    pending[0] = fin2
            if pending[0] is not None:
                pending[0](psc[0][0])
                pending[0] = None
            P.barrier()
        return OT

    def p3_consts(self, st):
        P = self.P
        w = self.w
        ds = self.gp
        wout = P.sbuf(st, "wout", [128, 8, D], BF16)
        wov = w["w_out"][:, :].rearrange("(kc p) n -> p kc n", p=128)
        for kc in range(8):
            for cb in range(2):
                self.dma("pool", wout, wout[:, kc, cb * 512:(cb + 1) * 512], w["w_out"], wov[:, kc, cb * 512:(cb + 1) * 512], ds)
        wr = P.sbuf(st, "wr", [128, 8, 36], BF16)
        self.dma("pool", wr, wr[:, :, 0:4], w["w_rg"], w["w_rg"][:, :].rearrange("(kc p) n -> p kc n", p=128), ds)
        self.dma("pool", wr, wr[:, :, 4:36], w["w_re"], w["w_re"][:, :].rearrange("(kc p) n -> p kc n", p=128), ds)
        brt = P.sbuf(st, "brt", [128, 36], F32)
        self.dma("sp", brt, brt[:, 0:4], w["b_rg"], w["b_rg"][0:1, :].partition_broadcast(128).rearrange("p o f -> p (o f)"), ds)
        self.dma("sp", brt, brt[:, 4:36], w["b_re"], w["b_re"][0:1, :].partition_broadcast(128).rearrange("p o f -> p (o f)"), ds)
        cpar = P.sbuf(st, "cpar", [128, 4, 3], F32)
        for c in range(4):
            for i, nme in enumerate(["b_dw", "g_conv_ln", "b_conv_ln"]):
                self.dma("sp", cpar, cpar[:, c, i:i + 1], w[nme], w[nme][c * 128:(c + 1) * 128, :], ds)
        wdw = P.sbuf(st, "wdw", [128, 4, CK], F32)
        for c in range(4):
            self.dma("sp", wdw, wdw[:, c, :], w["w_dw"], w["w_dw"][:, c * 128:(c + 1) * 128].rearrange("j p -> p j"), ds, allow_slow_non_contiguous=True)
        diag = P.sbuf(st, "diag", [128, 4, CK, 128], BF16)
        for c in range(4):
            for j in range(CK):
                self.ts(diag, diag[:, c, j, :], self.ident, self.ident[:, :], wdw[:, c, j:j + 1], ALU.mult, reads=[wdw])
        self.p3c = (wout, wr, brt, cpar, diag)

    def pass3(self, u, OT, tile_base):
        P = self.P
        S, n_own, smp = self.units[u]
        w = self.w
        GQ = min(512, n_own)
        ZW = n_own + 2 * CP + 2
        with ExitStack() as st:
            ds = self.gp
            G1 = self.mod_tile(st, "G1", u, 2, ds)
            A2 = self.mod_tile(st, "A2", u, 4, ds, "g_norm2")
            B2 = self.mod_tile(st, "B2", u, 3, ds)
            wout, wr, brt, cpar, diag = self.p3c
            zt = P.sbuf(st, "zt", [128, 4, ZW], BF16)
            if smp:
                self.memset(zt, zt[:, :, 0:1], 0.0)
                self.memset(zt, zt[:, :, ZW - 1:ZW], 0.0)
                self.dma("sp", zt, zt[:, :, 1:ZW - 1], self.ZT[u], self.ZT[u][:, :, 1:ZW - 1].rearrange("c p n -> p c n"), ds)
            else:
                self.memset(zt, zt[:, :, 0:CP + 1], 0.0)
                self.memset(zt, zt[:, :, CP + 1 + n_own:ZW], 0.0)
                self.dma("sp", zt, zt[:, :, CP + 1:CP + 1 + n_own], self.ZT[u], self.ZT[u][:, :, CP + 1:CP + 1 + n_own].rearrange("c p n -> p c n"), ds)
            pc = [P.psum(st, f"pc{i}", [128, GQ], F32) for i in range(2)]
            pst = [P.psum(st, f"pst{i}", [128, GQ], F32) for i in range(2)]
            po = [P.psum(st, f"po{i}", [128, 512], F32) for i in range(2)]
            pT = P.psum(st, "pT3", [128, 8, 128], BF16)
            pr = P.psum(st, "pr", [128, 512], F32)
            zc = [P.sbuf(st, f"zc{c}", [128, GQ], F32) for c in range(4)]
            zcb = P.sbuf(st, "zcb", [128, GQ], BF16)
            zsq = P.sbuf(st, "zsq", [128, GQ], BF16)
            mean = P.sbuf(st, "mean", [128, GQ], F32)
            var = P.sbuf(st, "var", [128, GQ], F32)
            rstd = P.sbuf(st, "rstd", [128, GQ], F32)
            tmpf = P.sbuf(st, "tmpf", [128, GQ], F32)
            tmpe = P.sbuf(st, "tmpe", [128, GQ], F32)
            cvT = P.sbuf(st, "cvT", [128, 4, GQ], BF16)
            xt = [P.sbuf(st, f"x3{i}", [128, D], F32) for i in range(3)]
            dsx = [self.cds("xt", i) for i in range(3)]
            n_own_t = n_own // 128

            def loadx3(ti):
                if ti < n_own_t:
                    self.dma("sp", xt[ti % 3], xt[ti % 3][:, :], self.xu[u], self.xu[u][ti * 128:(ti + 1) * 128, :], dsx[ti % 3])
            loadx3(0)
            loadx3(1)
            x1 = [P.sbuf(st, f"x1{i}", [128, D], F32) for i in range(2)]
            dx1 = [self.cds("x1", i) for i in range(2)]
            h2f = P.sbuf(st, "h2f", [128, D], F32)
            h2 = [P.sbuf(st, f"h2{i}", [128, D], BF16) for i in range(2)]
            dh2 = [self.cds("h2", i) for i in range(2)]
            h2T = P.sbuf(st, "h2T", [128, 8, 128], BF16)
            junk = P.sbuf(st, "junk3", [128, D], BF16)
            sm = P.sbuf(st, "sm3", [128, 8], F32)
            R = {k: P.sbuf(st, "r_" + k, shp, F32) for k, shp in [("lg", [128, 36]), ("gm", [128, 4]), ("goh", [128, 4]), ("ex", [128, 4]),
                                                                   ("leg4", [128, 4, 8]), ("leg", [128, 8]), ("oh1", [128, 8]), ("oh2", [128, 8]),
                                                                   ("msk", [128, 8]), ("sc", [128, 8]), ("E1", [128, 4, 8]), ("E2", [128, 4, 8]),
                                                                   ("pos", [128, NE]), ("tmp32", [128, NE])]}
            mskb = P.sbuf(st, "mskb", [128, NE], BF16)
            ntl = GQ // 128
            R4 = {k: P.sbuf(st, "r4_" + k, shp, F32) for k, shp in [
                ("lg", [128, ntl, 36]), ("gm", [128, ntl]), ("goh", [128, ntl, 4]), ("dd", [128, ntl, 4]), ("ex", [128, ntl, 4]),
                ("se", [128, ntl]), ("pg", [128, ntl]), ("leg4", [128, ntl, 4, 8]), ("leg", [128, ntl, 8]), ("m1", [128, ntl]),
                ("m2", [128, ntl]), ("oh1", [128, ntl, 8]), ("oh2", [128, ntl, 8]), ("msk", [128, ntl, 8]), ("d21", [128, ntl]),
                ("ed", [128, ntl]), ("p1", [128, ntl]), ("p2", [128, ntl]), ("E1", [128, ntl, 4, 8]), ("E2", [128, ntl, 4, 8]),
                ("pos", [128, ntl, NE]), ("tmp", [128, ntl, NE])]}
            mskb4 = P.sbuf(st, "mskb4", [128, ntl, NE], BF16)

            for g0 in range(0, n_own, GQ):
                for c in range(4):
                    b = pc[c % 2]
                    for j in range(CK):
                        self.mm(b, b[:, :], diag, diag[:, c, j, :], zt, zt[:, c, g0 + j + 1:g0 + j + 1 + GQ], j == 0, j == CK - 1)
                    self.ts(zc[c], zc[c][:, :], b, b[:, :], cpar[:, c, 0:1], ALU.add, reads=[cpar])
                    self.cp(zcb, zcb[:, :], zc[c], zc[c][:, :])
                    self.tt(zsq, zsq[:, :], zc[c], zc[c][:, :], zc[c], zc[c][:, :], ALU.mult)
                    self.mm(pst[0], pst[0][:, :], self.ones, self.ones[:, :], zcb, zcb[:, :], c == 0, c == 3)
                    self.mm(pst[1], pst[1][:, :], self.ones, self.ones[:, :], zsq, zsq[:, :], c == 0, c == 3)
                self.ts(mean, mean[:, :], pst[0], pst[0][:, :], 1.0 / 512, ALU.mult)
                self.tt(tmpf, tmpf[:, :], mean, mean[:, :], mean, mean[:, :], ALU.mult)
                self.stt(var, var[:, :], pst[1], pst[1][:, :], 1.0 / 512, tmpf, tmpf[:, :], ALU.mult, ALU.subtract)
                self.ts(var, var[:, :], var, var[:, :], 0.0, ALU.max, EPS, ALU.add)
                self.rsqrt(rstd, rstd[:, :], var, var[:, :], tmpe, tmpe[:, :])
                for c in range(4):
                    self.tt(tmpf, tmpf[:, :], zc[c], zc[c][:, :], mean, mean[:, :], ALU.subtract)
                    self.tt(tmpf, tmpf[:, :], tmpf, tmpf[:, :], rstd, rstd[:, :], ALU.mult)
                    self.ts(tmpf, tmpf[:, :], tmpf, tmpf[:, :], cpar[:, c, 1:2], ALU.mult, cpar[:, c, 2:3], ALU.add, reads=[cpar])
                    self.act(cvT, cvT[:, c, :], tmpf, tmpf[:, :], AF.Silu)
                def stA(tl):
                    tk = g0 + tl * 128
                    loadx3(tk // 128 + 2)
                    for cb in range(2):
                        for kc in range(8):
                            if kc < 4:
                                lb, lap = OT, OT[:, kc, tk:tk + 128]
                            else:
                                lb, lap = cvT, cvT[:, kc - 4, tl * 128:(tl + 1) * 128]
                            self.mm(po[cb], po[cb][:, :], lb, lap, wout, wout[:, kc, cb * 512:(cb + 1) * 512], kc == 0, kc == 7)

                def stB(tl):
                    tk = g0 + tl * 128
                    T = tile_base + tk // 128
                    s = (tk // 128) % 2
                    for cb in range(2):
                        self.tt(x1[s], x1[s][:, cb * 512:(cb + 1) * 512], po[cb], po[cb][:, :], G1, G1[:, cb * 512:(cb + 1) * 512], ALU.mult)
                    x_ = xt[(tk // 128) % 3]
                    self.tt(x1[s], x1[s][:, :], x1[s], x1[s][:, :], x_, x_[:, :], ALU.add)
                    self.dma("sp", self.X1, self.X1[T * 128:(T + 1) * 128, :], x1[s], x1[s][:, :], dx1[s])
                    self.rms_rstd(x1[s], x1[s][:, :], D, junk, junk[:, :], sm, sm[:, 0:1], sm, sm[:, 1:2], sm, sm[:, 2:3])
                    self.stt(h2f, h2f[:, :], x1[s], x1[s][:, :], sm[:, 2:3], A2, A2[:, :], ALU.mult, ALU.mult, reads=[sm])
                    self.tt(h2[s], h2[s][:, :], h2f, h2f[:, :], B2, B2[:, :], ALU.add)
                    self.dma("sp", self.H2, self.H2[T * 128:(T + 1) * 128, :], h2[s], h2[s][:, :], dh2[s])
                    for kc in range(8):
                        self.tp(pT, pT[:, kc, :], h2[s], h2[s][:, kc * 128:(kc + 1) * 128])
                    self.cp(h2T, h2T[:, :, :], pT, pT[:, :, :], eng="act")
                    for kc in range(8):
                        self.mm(pr, pr[:, tl * 64:tl * 64 + 36], h2T, h2T[:, kc, :], wr, wr[:, kc, :], kc == 0, kc == 7)

                stA(0)
                for tl in range(ntl):
                    stB(tl)
                    if tl + 1 < ntl:
                        stA(tl + 1)
                self.route4(R4, pr, brt, mskb4, tile_base + g0 // 128, ntl)
            P.barrier()

    def route4(self, R, pr, brt, mskb, T0, n):
        g = lambda k: R[k]
        lg, gm, goh, dd, ex, se, pg, leg4, leg, m1, m2, oh1, oh2, msk, d21, ed, p1, p2, E1, E2, pos, tmp = [g(k) for k in
            ("lg", "gm", "goh", "dd", "ex", "se", "pg", "leg4", "leg", "m1", "m2", "oh1", "oh2", "msk", "d21", "ed", "p1", "p2", "E1", "E2", "pos", "tmp")]
        ROUT = self.ROUT
        prv = pr[:, 0:n * 64].rearrange("p (t c) -> p t c", c=64)
        self.tt(lg, lg[:, :, :], pr, prv[:, :, 0:36], brt, brt[:, :].unsqueeze(1).to_broadcast([128, n, 36]), ALU.add)
        self.red(gm, gm[:, :], lg, lg[:, :, 0:4], ALU.max)
        gmb = gm[:, :].unsqueeze(2).to_broadcast([128, n, 4])
        self.tt(goh, goh[:, :, :], lg, lg[:, :, 0:4], gm, gmb, ALU.is_ge)
        self.tt(dd, dd[:, :, :], lg, lg[:, :, 0:4], gm, gmb, ALU.subtract)
        self.act(ex, ex[:, :, :], dd, dd[:, :, :], AF.Exp)
        self.red(se, se[:, :], ex, ex[:, :, :])
        self.recip(pg, pg[:, :], se, se[:, :])
        self.tt(leg4, leg4[:, :, :, :], lg, lg[:, :, 4:36].rearrange("p t (g e) -> p t g e", g=4), goh, goh[:, :, :].unsqueeze(3).to_broadcast([128, n, 4, 8]), ALU.mult)
        self.red(leg, leg[:, :, :], leg4, leg4[:, :, :, :].rearrange("p t g e -> p t e g"))
        self.red(m1, m1[:, :], leg, leg[:, :, :], ALU.max)
        self.tt(oh1, oh1[:, :, :], leg, leg[:, :, :], m1, m1[:, :].unsqueeze(2).to_broadcast([128, n, 8]), ALU.is_ge)
        self.stt(msk, msk[:, :, :], oh1, oh1[:, :, :], -1.0e30, leg, leg[:, :, :], ALU.mult, ALU.add)
        self.red(m2, m2[:, :], msk, msk[:, :, :], ALU.max)
        self.tt(oh2, oh2[:, :, :], msk, msk[:, :, :], m2, m2[:, :].unsqueeze(2).to_broadcast([128, n, 8]), ALU.is_ge)
        self.tt(d21, d21[:, :], m2, m2[:, :], m1, m1[:, :], ALU.subtract)
        self.act(ed, ed[:, :], d21, d21[:, :], AF.Exp)
        self.ts(p1, p1[:, :], ed, ed[:, :], 1.0, ALU.add)
        self.recip(p1, p1[:, :], p1, p1[:, :])
        self.tt(p2, p2[:, :], ed, ed[:, :], p1, p1[:, :], ALU.mult)
        self.tt(ROUT, ROUT[:, T0:T0 + n, 4], p1, p1[:, :], pg, pg[:, :], ALU.mult)
        self.tt(ROUT, ROUT[:, T0:T0 + n, 5], p2, p2[:, :], pg, pg[:, :], ALU.mult)
        gb = goh[:, :, :].unsqueeze(3).to_broadcast([128, n, 4, 8])
        self.tt(E1, E1[:, :, :, :], oh1, oh1[:, :, :].unsqueeze(2).to_broadcast([128, n, 4, 8]), goh, gb, ALU.mult)
        self.tt(E2, E2[:, :, :, :], oh2, oh2[:, :, :].unsqueeze(2).to_broadcast([128, n, 4, 8]), goh, gb, ALU.mult)
        E1f = E1[:, :, :, :].rearrange("p t g e -> p t (g e)")
        E2f = E2[:, :, :, :].rearrange("p t g e -> p t (g e)")
        self.tt(mskb, mskb[:, :, :], E1, E1f, E2, E2f, ALU.add)
        mf = mskb[:, :, :].rearrange("p t e -> p (t e)")
        self.mm(pr, pr[:, 256:256 + n * NE], self.utri, self.utri[:, :], mskb, mf, True, True)
        self.mm(pr, pr[:, 384:384 + n * NE], self.ones, self.ones[:, :], mskb, mf, True, True)
        for t in range(n):
            self.tt(pos, pos[:, t, :], pr, pr[:, 256 + t * NE:256 + (t + 1) * NE], self.Rcnt, self.Rcnt[:, :], ALU.add)
            self.tt(self.Rcnt, self.Rcnt[:, :], self.Rcnt, self.Rcnt[:, :], pr, pr[:, 384 + t * NE:384 + (t + 1) * NE], ALU.add)
        iob = self.iota32[:, :].unsqueeze(1).to_broadcast([128, n, NE])
        for k, (Eb, Ef) in enumerate(((E1, E1f), (E2, E2f))):
            self.tt(tmp, tmp[:, :, :], Eb, Ef, self.iota32, iob, ALU.mult)
            self.red(ROUT, ROUT[:, T0:T0 + n, k], tmp, tmp[:, :, :])
            self.tt(tmp, tmp[:, :, :], Eb, Ef, pos, pos[:, :, :], ALU.mult)
            self.red(ROUT, ROUT[:, T0:T0 + n, 2 + k], tmp, tmp[:, :, :])

    def route(self, R, pr, brt, mskb, T):
        lg, gm, goh, ex, leg4, leg, oh1, oh2, msk, sc, E1, E2, pos, tmp32 = [R[k] for k in
            ("lg", "gm", "goh", "ex", "leg4", "leg", "oh1", "oh2", "msk", "sc", "E1", "E2", "pos", "tmp32")]
        ROUT = self.ROUT
        self.tt(lg, lg[:, :], pr, pr[:, 0:36], brt, brt[:, :], ALU.add)
        self.red(gm, gm[:, 0:1], lg, lg[:, 0:4], ALU.max)
        self.ts(goh, goh[:, :], lg, lg[:, 0:4], gm[:, 0:1], ALU.is_ge, reads=[gm])
        self.ts(gm, gm[:, 1:2], gm, gm[:, 0:1], -1.0, ALU.mult)
        self.P.op("act", lambda e: e.activation(out=ex[:, :], in_=lg[:, 0:4], func=AF.Exp, bias=gm[:, 1:2], accum_out=gm[:, 2:3]), reads=[lg, gm], writes=[ex, gm])
        self.recip(gm, gm[:, 3:4], gm, gm[:, 2:3])
        self.tt(leg4, leg4[:, :, :], lg, lg[:, 4:36].rearrange("p (g e) -> p g e", g=4), goh, goh[:, :].unsqueeze(2).to_broadcast([128, 4, 8]), ALU.mult)
        self.red(leg, leg[:, :], leg4, leg4[:, :, :].rearrange("p g e -> p e g"))
        self.red(sc, sc[:, 0:1], leg, leg[:, :], ALU.max)
        self.ts(oh1, oh1[:, :], leg, leg[:, :], sc[:, 0:1], ALU.is_ge, reads=[sc])
        self.stt(msk, msk[:, :], oh1, oh1[:, :], -1.0e30, leg, leg[:, :], ALU.mult, ALU.add)
        self.red(sc, sc[:, 1:2], msk, msk[:, :], ALU.max)
        self.ts(oh2, oh2[:, :], msk, msk[:, :], sc[:, 1:2], ALU.is_ge, reads=[sc])
        self.tt(sc, sc[:, 2:3], sc, sc[:, 1:2], sc, sc[:, 0:1], ALU.subtract)
        self.act(sc, sc[:, 3:4], sc, sc[:, 2:3], AF.Exp)
        self.ts(sc, sc[:, 4:5], sc, sc[:, 3:4], 1.0, ALU.add)
        self.recip(sc, sc[:, 5:6], sc, sc[:, 4:5])
        self.tt(sc, sc[:, 6:7], sc, sc[:, 3:4], sc, sc[:, 5:6], ALU.mult)
        self.tt(ROUT, ROUT[:, T, 4:5], sc, sc[:, 5:6], gm, gm[:, 3:4], ALU.mult)
        self.tt(ROUT, ROUT[:, T, 5:6], sc, sc[:, 6:7], gm, gm[:, 3:4], ALU.mult)
        gb = goh[:, :].unsqueeze(2).to_broadcast([128, 4, 8])
        self.tt(E1, E1[:, :, :], oh1, oh1[:, :].unsqueeze(1).to_broadcast([128, 4, 8]), goh, gb, ALU.mult)
        self.tt(E2, E2[:, :, :], oh2, oh2[:, :].unsqueeze(1).to_broadcast([128, 4, 8]), goh, gb, ALU.mult)
        E1f = E1[:, :, :].rearrange("p g e -> p (g e)")
        E2f = E2[:, :, :].rearrange("p g e -> p (g e)")
        self.tt(mskb, mskb[:, :], E1, E1f, E2, E2f, ALU.add)
        self.mm(pr, pr[:, 0:NE], self.utri, self.utri[:, :], mskb, mskb[:, :], True, True)
        self.tt(pos, pos[:, :], pr, pr[:, 0:NE], self.Rcnt, self.Rcnt[:, :], ALU.add)
        self.mm(pr, pr[:, 0:NE], self.ones, self.ones[:, :], mskb, mskb[:, :], True, True)
        self.tt(self.Rcnt, self.Rcnt[:, :], self.Rcnt, self.Rcnt[:, :], pr, pr[:, 0:NE], ALU.add)
        for k, (Eb, Ef) in enumerate(((E1, E1f), (E2, E2f))):
            self.tt(tmp32, tmp32[:, :], Eb, Ef, self.iota32, self.iota32[:, :], ALU.mult)
            self.red(ROUT, ROUT[:, T, k:k + 1], tmp32, tmp32[:, :])
            self.tt(tmp32, tmp32[:, :], Eb, Ef, pos, pos[:, :], ALU.mult)
            self.red(ROUT, ROUT[:, T, 2 + k:3 + k], tmp32, tmp32[:, :])

    def moe(self):
        P = self.P
        NT, NB = self.n_tiles, self.n_blk
        ROUT = self.ROUT
        P.barrier(include_bg=True)
        with ExitStack() as st:
            ds = self.gp
            MB = 256
            assert MB * 128 >= 2 * self.n_tok
            wid = P.sbuf(st, "wid", [128, NB], I32)
            dst = P.sbuf(st, "dst", [128, NT, 2], I32)
            sA = ExitStack()
            st_outer, st = st, sA
            thr = P.sbuf(st, "thr", [128, MB], F32)
            thi = P.sbuf(st, "thi", [128, MB], I32)
            P.op("pool", lambda e: e.iota(thi[:, :], pattern=[[128, MB]], base=0, channel_multiplier=0), writes=[thi])
            self.cp(thr, thr[:, :], thi, thi[:, :])
            big = P.sbuf(st, "bigc", [128, NE, MB], F32)
            nblk = P.sbuf(st, "nblk", [128, NE], F32)
            self.tt(big, big[:, :, :], self.Rcnt, self.Rcnt[:, :].unsqueeze(2).to_broadcast([128, NE, MB]),
                    thr, thr[:, :].unsqueeze(1).to_broadcast([128, NE, MB]), ALU.is_gt)
            self.red(nblk, nblk[:, :], big, big[:, :, :])
            L = P.sbuf(st, "Ltri", [128, NE, NE], F32)
            self.tt(L, L[:, :, :], self.iota32, self.iota32[:, :].unsqueeze(1).to_broadcast([128, NE, NE]),
                    self.iota32, self.iota32[:, :].unsqueeze(2).to_broadcast([128, NE, NE]), ALU.is_le)
            self.tt(L, L[:, :, :], L, L[:, :, :], nblk, nblk[:, :].unsqueeze(1).to_broadcast([128, NE, NE]), ALU.mult)
            pend = P.sbuf(st, "pend", [128, NE], F32)
            pstart = P.sbuf(st, "pstart", [128, NE], F32)
            self.red(pend, pend[:, :], L, L[:, :, :])
            self.tt(pstart, pstart[:, :], pend, pend[:, :], nblk, nblk[:, :], ALU.subtract)
            self.ts(pend, pend[:, :], pend, pend[:, :], 128.0, ALU.mult)
            self.ts(pstart, pstart[:, :], pstart, pstart[:, :], 128.0, ALU.mult)
            cmpb = P.sbuf(st, "cmpb", [128, NB, NE], F32)
            eid = P.sbuf(st, "eid", [128, NB + 2], F32)
            self.tt(cmpb, cmpb[:, :, :], pend, pend[:, :].unsqueeze(1).to_broadcast([128, NB, NE]),
                    thr, thr[:, 0:NB].unsqueeze(2).to_broadcast([128, NB, NE]), ALU.is_le)
            self.memset(eid, eid[:, 0:2], -1.0)
            self.red(eid, eid[:, 2:NB + 2], cmpb, cmpb[:, :, :])
            self.ts(eid, eid[:, 2:NB + 2], eid, eid[:, 2:NB + 2], float(NE - 1), ALU.min)
            same = P.sbuf(st, "same", [128, NB], F32)
            widf = P.sbuf(st, "widf", [128, NB], F32)
            pidx = P.sbuf(st, "pidx", [128, 1], F32)
            pidi = P.sbuf(st, "pidi", [128, 1], I32)
            P.op("pool", lambda e: e.iota(pidi[:, :], pattern=[[0, 1]], base=0, channel_multiplier=1), writes=[pidi])
            self.cp(pidx, pidx[:, :], pidi, pidi[:, :])
            self.tt(same, same[:, :], eid, eid[:, 2:NB + 2], eid, eid[:, 0:NB], ALU.is_equal)
            self.ts(widf, widf[:, :], eid, eid[:, 2:NB + 2], 128.0, ALU.mult, pidx[:, 0:1], ALU.add, reads=[pidx])
            self.stt(widf, widf[:, :], same, same[:, :], BIG, widf, widf[:, :], ALU.mult, ALU.add)
            self.cp(wid, wid[:, :], widf, widf[:, :])
            oh = P.sbuf(st, "ohd", [128, NT * 2, NE], F32)
            ev = ROUT[:, :, 0:2]
            self.tt(oh, oh[:, :, :].rearrange("p (t k) e -> p t k e", k=2), ROUT, ev.unsqueeze(3).to_broadcast([128, NT, 2, NE]),
                    self.iota32, self.iota32[:, :].unsqueeze(1).unsqueeze(1).to_broadcast([128, NT, 2, NE]), ALU.is_equal)
            self.tt(oh, oh[:, :, :], oh, oh[:, :, :], pstart, pstart[:, :].unsqueeze(1).to_broadcast([128, NT * 2, NE]), ALU.mult)
            dstf = P.sbuf(st, "dstf", [128, NT, 2], F32)
            self.red(dstf, dstf[:, :, :].rearrange("p t k -> p (t k)"), oh, oh[:, :, :])
            self.tt(dstf, dstf[:, :, :], dstf, dstf[:, :, :], ROUT, ROUT[:, :, 2:4], ALU.add)
            self.cp(dst, dst[:, :, :], dstf, dstf[:, :, :])
            if self.dbg:
                self.dma("sp", self.DROUT, self.DROUT[:, :], ROUT, ROUT[:, :, :].rearrange("p t k -> p (t k)"), ds)
                self.dma("sp", self.DMISC, self.DMISC[:, 0:NE], self.Rcnt, self.Rcnt[:, :], ds)
                self.dma("sp", self.DMISC, self.DMISC[:, NE:NE + NB + 1], eid, eid[:, 1:NB + 2], ds)
                self.dma("sp", self.DMISC, self.DMISC[:, NE + NB + 1:], dstf, dstf[:, :, :].rearrange("p t k -> p (t k)"), ds)
            NSL = 6
            hl = [P.sbuf(st, f"hl{i}", [128, D], BF16) for i in range(NSL)]
            dhl = [P.dsem("hl") for i in range(NSL)]
            dsc = [[P.dsem("scat") for k in range(2)] for i in range(NSL)]
            XS = self.XS
            for T in range(NT):
                s = T % NSL
                self.dma("sp", hl[s], hl[s][:, :], self.H2, self.H2[T * 128:(T + 1) * 128, :], dhl[s])
                for k in range(2):
                    P.op("pool", (lambda hb, ia: lambda e: e.indirect_dma_start(out=XS[:, :], out_offset=bass.IndirectOffsetOnAxis(ap=ia, axis=0), in_=hb[:, :], in_offset=None))(hl[s], dst[:, T, k:k + 1]),
                         reads=[hl[s], dst], writes=[XS], dsem=dsc[s][k])
            P.barrier()
            sA.close()
            st = st_outer
            wgu = [P.sbuf(st, f"wgu{i}", [128, 8, 2 * FF], BF16) for i in range(2)]
            wdn = [P.sbuf(st, f"wdn{i}", [128, 4, D], BF16) for i in range(2)]
            dwg = [P.dsem("wgu") for i in range(2)]
            dwd = [P.dsem("wdn") for i in range(2)]
            xb = [P.sbuf(st, f"xb{i}", [128, D], BF16) for i in range(3)]
            dxb = [P.dsem("xb") for i in range(3)]
            xT = [P.sbuf(st, f"xT{i}", [128, 8, 128], BF16) for i in range(2)]
            pT = [P.psum(st, f"pT6{i}", [128, 8, 128], BF16) for i in range(2)]
            py = [P.psum(st, f"py{i}", [128, 512], F32) for i in range(2)]
            pg = [[P.psum(st, f"pg{i}{c}", [128, 512], F32) for c in range(2)] for i in range(2)]
            eg = [P.sbuf(st, f"eg6{i}", [128, FF], F32) for i in range(2)]
            ab = [P.sbuf(st, f"ab6{i}", [128, FF], BF16) for i in range(2)]
            aT = [P.sbuf(st, f"aT6{i}", [128, 4, 128], BF16) for i in range(2)]
            yb = [P.sbuf(st, f"yb{i}", [128, D], F32) for i in range(2)]
            dyb = [P.dsem("yb") for i in range(2)]
            WGU, WDN = self.WGU, self.WDN
            bound = NE * 128 - 1
            regbox = {}

            def breg(e):
                if "r" not in regbox:
                    r = e.alloc_register("wbound")
                    e.reg_mov(r, bound)
                    regbox["r"] = r
                return regbox["r"]

            def gather_w(j, which):
                if j >= NB:
                    return
                ia = wid[:, j:j + 1]
                wg_, wd_ = wgu[j % 2], wdn[j % 2]
                if which == 0:
                    P.op("pool", lambda e: e.indirect_dma_start(out=wg_[:, :, :].rearrange("p k n -> p (k n)"), out_offset=None, in_=WGU[:, :], in_offset=bass.IndirectOffsetOnAxis(ap=ia, axis=0), bounds_check=breg(e), oob_is_err=False),
                         reads=[WGU, wid], writes=[wg_], dsem=dwg[j % 2])
                else:
                    P.op("pool", lambda e: e.indirect_dma_start(out=wd_[:, :, :].rearrange("p k n -> p (k n)"), out_offset=None, in_=WDN[:, :], in_offset=bass.IndirectOffsetOnAxis(ap=ia, axis=0), bounds_check=breg(e), oob_is_err=False),
                         reads=[WDN, wid], writes=[wd_], dsem=dwd[j % 2])

            def loadx(j):
                if j < NB:
                    self.dma("sp", xb[j % 3], xb[j % 3][:, :], self.XS, self.XS[j * 128:(j + 1) * 128, :], dxb[j % 3])

            def front(j):
                s = j % 2
                loadx(j + 2)
                for kc in range(8):
                    self.tp(pT[s], pT[s][:, kc, :], xb[j % 3], xb[j % 3][:, kc * 128:(kc + 1) * 128])
                self.cp(xT[s], xT[s][:, :, :], pT[s], pT[s][:, :, :], eng="act")
                for cb in range(2):
                    for kc in range(8):
                        self.mm(pg[s][cb], pg[s][cb][:, :], xT[s], xT[s][:, kc, :], wgu[s], wgu[s][:, kc, cb * 512:(cb + 1) * 512], kc == 0, kc == 7)
                gather_w(j + 2, 0)

            def mid(j):
                s = j % 2
                self.act(eg[s], eg[s][:, :], pg[s][0], pg[s][0][:, :], AF.Silu)
                self.tt(ab[s], ab[s][:, :], eg[s], eg[s][:, :], pg[s][1], pg[s][1][:, :], ALU.mult)

            def back1(j):
                s = j % 2
                for c in range(4):
                    self.tp(pT[s], pT[s][:, c, :], ab[s], ab[s][:, c * 128:(c + 1) * 128])
                self.cp(aT[s], aT[s][:, :, :], pT[s], pT[s][:, 0:4, :])
                for cb in range(2):
                    for kc in range(4):
                        self.mm(py[cb], py[cb][:, :], aT[s], aT[s][:, kc, :], wdn[s], wdn[s][:, kc, cb * 512:(cb + 1) * 512], kc == 0, kc == 3)
                gather_w(j + 2, 1)

            def back2(j):
                s = j % 2
                for cb in range(2):
                    self.cp(yb[s], yb[s][:, cb * 512:(cb + 1) * 512], py[cb], py[cb][:, :], eng="act" if cb else "dve")
                self.dma("sp", self.YB, self.YB[j * 128:(j + 1) * 128, :], yb[s], yb[s][:, :], dyb[s])

            gather_w(0, 0)
            gather_w(0, 1)
            gather_w(1, 0)
            gather_w(1, 1)
            loadx(0)
            loadx(1)
            front(0)
            mid(0)
            for j in range(NB):
                if j + 1 < NB:
                    front(j + 1)
                back1(j)
                if j + 1 < NB:
                    mid(j + 1)
                back2(j)
            P.barrier()
            yg = [[P.sbuf(st, f"yg{i}{k}", [128, D], F32) for k in range(2)] for i in range(2)]
            dyg = [[P.dsem("yg") for k in range(2)] for i in range(2)]
            x1 = [P.sbuf(st, f"x7{i}", [128, D], F32) for i in range(2)]
            dx7 = [P.dsem("x7") for i in range(2)]
            ot = [P.sbuf(st, f"ot{i}", [128, D], F32) for i in range(2)]
            dot = [P.dsem("ot") for i in range(2)]
            YB = self.YB
            tiles = []
            for u, (S, n_own, smp) in enumerate(self.units):
                for tl in range(n_own // 128):
                    tiles.append((u, tl))

            def cload(T):
                if T >= len(tiles):
                    return
                s = T % 2
                self.dma("sp", x1[s], x1[s][:, :], self.X1, self.X1[T * 128:(T + 1) * 128, :], dx7[s])
                for k in range(2):
                    P.op("pool", (lambda ob, ia: lambda e: e.indirect_dma_start(out=ob[:, :], out_offset=None, in_=YB[:, :], in_offset=bass.IndirectOffsetOnAxis(ap=ia, axis=0)))(yg[s][k], dst[:, T, k:k + 1]),
                         reads=[YB, dst], writes=[yg[s][k]], dsem=dyg[s][k])

            G2s = [self.mod_tile(st, "G2", u, 5, ds) for u in range(len(self.units))]
            cload(0)
            for T, (u, tl) in enumerate(tiles):
                s = T % 2
                G2 = G2s[u]
                self.ts(yg[s][0], yg[s][0][:, :], yg[s][0], yg[s][0][:, :], ROUT[:, T, 4:5], ALU.mult, reads=[ROUT])
                self.stt(yg[s][0], yg[s][0][:, :], yg[s][1], yg[s][1][:, :], ROUT[:, T, 5:6], yg[s][0], yg[s][0][:, :], ALU.mult, ALU.add, reads=[ROUT])
                self.tt(yg[s][0], yg[s][0][:, :], yg[s][0], yg[s][0][:, :], G2, G2[:, :], ALU.mult)
                cload(T + 1)
                self.tt(ot[s], ot[s][:, :], yg[s][0], yg[s][0][:, :], x1[s], x1[s][:, :], ALU.add)
                self.dma("sp", self.yu[u], self.yu[u][tl * 128:(tl + 1) * 128, :], ot[s], ot[s][:, :], dot[s])
            P.barrier()

    def build(self):
        self.declare()
        self.setup()
        import os
        stage = int(os.environ.get("KSTAGE", "9"))
        for u in range(len(self.units)):
            self.pass1(u)
        self.wst.close()
        tb = 0
        if stage >= 2:
            with ExitStack() as sp3:
                self.p3_consts(sp3)
                for u, (S, n_own, smp) in enumerate(self.units):
                    with ExitStack() as so:
                        OT = self.pass2(u, so)
                        self.pass3(u, OT, tb)
                    tb += n_own // 128
                self.P.barrier()
        if stage >= 3:
            self.moe()
        self.P.barrier()
        stats = self.P.emit()
        self.st.close()
        return self.nc, stats


_CACHE = {}


def _run(x_prompt, x_sample, c_prompt, c_sample, W, n_cores, nq, dbg=False):
    Bp, Sp, _ = x_prompt.shape
    Bs, Ss, _ = x_sample.shape
    assert Bs * nq == n_cores and Bp % n_cores == 0
    ppc = Bp // n_cores
    own = Ss // nq
    units = [(Sp, Sp, False)] * ppc + [(Ss, own, True)]
    key = (tuple(units), dbg)
    if key not in _CACHE:
        k = K(units, dbg=dbg)
        nc, stats = k.build()
        _CACHE[key] = (nc, stats)
    nc, stats = _CACHE[key]
    base = dict(W)
    in_maps = []
    for c in range(n_cores):
        m = dict(base)
        cs = []
        for i in range(ppc):
            b = c * ppc + i
            m[f"x{i}"] = np.ascontiguousarray(x_prompt[b])
            m[f"pos{i}"] = np.ascontiguousarray(np.arange(Sp, dtype=np.float32).reshape(Sp // 128, 128).T)
            cs.append(c_prompt[b])
        sq, qq = c // nq, c % nq
        order = (qq * own + np.arange(Ss)) % Ss
        m[f"x{ppc}"] = np.ascontiguousarray(x_sample[sq][order])
        m[f"pos{ppc}"] = np.ascontiguousarray(order.astype(np.float32).reshape(Ss // 128, 128).T)
        cs.append(c_sample[sq])
        m["flags"] = np.array([[1.0 if qq > 0 else 0.0, 1.0 if qq < nq - 1 else 0.0]], np.float32)
        cu = np.stack(cs, 0)
        m["cT"] = np.ascontiguousarray(cu.reshape(len(cs), 8, 128).transpose(2, 1, 0))
        in_maps.append(m)
    res = run_bass_kernel_spmd(nc, in_maps, core_ids=list(range(n_cores)))
    if dbg:
        _CACHE["last"] = res.results
    yp = np.empty((Bp, Sp, D), np.float32)
    ys = np.empty((Bs, Ss, D), np.float32)
    for c in range(n_cores):
        r = res.results[c]
        for i in range(ppc):
            yp[c * ppc + i] = r[f"y{i}"]
        sq, qq = c // nq, c % nq
        ys[sq, qq * own:(qq + 1) * own] = r[f"y{ppc}"]
    return yp, ys


def _weights(w_ada, b_ada, g_norm1, w_in, g_q, g_k, lambda_q1, lambda_k1, lambda_q2, lambda_k2, g_subln, w_dw, b_dw,
             g_conv_ln, b_conv_ln, w_out, g_norm2, w_router_group, b_router_group, w_router_expert, b_router_expert,
             w_gate_up, w_down):
    f = lambda a: np.ascontiguousarray(np.asarray(a, dtype=np.float32))
    return dict(
        w_ada=f(w_ada[0]), b_ada=f(b_ada[0]).reshape(1, -1), g_norm1=f(g_norm1[0]).reshape(1, -1), w_in=f(w_in[0]),
        g_q=f(g_q[0]).reshape(1, -1), g_k=f(g_k[0]).reshape(1, -1),
        lambda_q1=f(lambda_q1[0]).reshape(1, -1), lambda_k1=f(lambda_k1[0]).reshape(1, -1),
        lambda_q2=f(lambda_q2[0]).reshape(1, -1), lambda_k2=f(lambda_k2[0]).reshape(1, -1),
        g_subln=f(g_subln[0]).reshape(-1, 1), w_dw=f(w_dw[0]), b_dw=f(b_dw[0]).reshape(-1, 1),
        g_conv_ln=f(g_conv_ln[0]).reshape(-1, 1), b_conv_ln=f(b_conv_ln[0]).reshape(-1, 1), w_out=f(w_out[0]),
        g_norm2=f(g_norm2[0]).reshape(1, -1), w_rg=f(w_router_group[0]), b_rg=f(b_router_group[0]).reshape(1, -1),
        w_re=f(w_router_expert[0]), b_re=f(b_router_expert[0]).reshape(1, -1), w_gu=f(w_gate_up[0]), w_dn=f(w_down[0]))


def kernel(x_prompt, x_sample, c_prompt, c_sample, **weights):
    W = _weights(**weights)
    f = lambda a: np.asarray(a, dtype=np.float32)
    yp, ys = _run(f(x_prompt), f(x_sample), f(c_prompt), f(c_sample), W, n_cores=8, nq=4)
    return yp, ys
```
